# Optimizing a Trainium2 kernel written in Bass

```python
import jax
import jax.numpy as jnp
from jax import lax
import numpy as np

D_MODEL = 1024
BATCH = 2
SEQ = 8192
DEPTH = 2
DEC_BATCH = 32
DEC_SEQ = 8
PAST_LEN = 16384
PAGE_SIZE = 128

N_EVEN = (DEPTH + 1) // 2
N_ODD = DEPTH // 2
HEAD_DIM = 64
NORM_EPS = 1e-6
SC_DIM = D_MODEL // 2
SC_WIDTH = 3
RW_HEADS = (D_MODEL // 2) // HEAD_DIM
RW_DIM = RW_HEADS * HEAD_DIM
DECAY_LORA = 64
AAA_LORA = 64
GATE_LORA = 128
RW_SHIFT_W = 3 * RW_DIM + DECAY_LORA + AAA_LORA + GATE_LORA
RW_GN_EPS = 64e-5
IN0 = 3 * SC_DIM + RW_SHIFT_W
FOX_HEADS = (D_MODEL // 2) // HEAD_DIM
FOX_DIM = FOX_HEADS * HEAD_DIM
FOX_IN = 3 * FOX_DIM + FOX_HEADS
Q_BLOCK = 128
SSM_DIM = D_MODEL // 2
SSM_HEAD_DIM = 64
SSM_HEADS = SSM_DIM // SSM_HEAD_DIM
SSM_GROUPS = 2
SSM_STATE = 128
SSM_CONV = 4
SSM_CONV_DIM = SSM_DIM + 2 * SSM_GROUPS * SSM_STATE
SSD_CHUNK = 128
IN1 = FOX_IN + SSM_DIM + SSM_CONV_DIM + SSM_HEADS
FFN_HIDDEN = -(-8 * D_MODEL // (3 * 256)) * 256

kernel_name = 'hybrid_conv_rwkv7_fox_mamba2_step'

F32 = jnp.float32


def rmsnorm(x, g):
    xf = x.astype(F32)
    y = xf * lax.rsqrt(jnp.mean(xf * xf, axis=-1, keepdims=True) + NORM_EPS)
    return (y * g.astype(F32)).astype(x.dtype)


def swiglu(h, w_gate, w_up, w_down):
    return (jax.nn.silu(h @ w_gate) * (h @ w_up)) @ w_down


def causal_dwconv(u, buf, w):
    K = w.shape[0]
    L = u.shape[1]
    up = jnp.concatenate([buf.astype(u.dtype), u], axis=1)
    y = sum(w[k] * up[:, k:k + L] for k in range(K))
    return y, up[:, L:]


def short_conv_mixer(z, buf, conv_w):
    gb, gc, h = jnp.split(z, 3, axis=-1)
    y, new_buf = causal_dwconv(gc * h, buf, conv_w)
    return gb * y, new_buf


def rwkv7_mixer(z, shift_prev, wkv0, p):
    Bsz, L, _ = z.shape
    zs = jnp.concatenate([shift_prev[:, None].astype(z.dtype), z[:, :-1]], axis=1)
    zx = z + p['mu'] * (zs - z)
    idx = [RW_DIM, 2 * RW_DIM, 3 * RW_DIM, 3 * RW_DIM + DECAY_LORA, 3 * RW_DIM + DECAY_LORA + AAA_LORA]
    r, k, v, wd, ad, gd = jnp.split(zx, idx, axis=-1)
    w_log = -jax.nn.softplus(-(p['w0'] + jnp.tanh(wd) @ p['w2']).astype(F32)) - 0.5
    decay = jnp.exp(-jnp.exp(w_log))
    a = jax.nn.sigmoid((p['a0'] + ad @ p['a2']).astype(F32))
    g = jax.nn.sigmoid(gd) @ p['g2']

    def heads(t):
        return t.reshape(Bsz, L, RW_HEADS, HEAD_DIM).astype(F32)

    r, k, v, decay, a = heads(r), heads(k), heads(v), heads(decay), heads(a)
    kk = k * p['k_k'].reshape(RW_HEADS, HEAD_DIM).astype(F32)
    kk = kk / jnp.maximum(jnp.sqrt(jnp.sum(kk * kk, axis=-1, keepdims=True)), 1e-12)
    k = k * (1.0 + (a - 1.0) * p['k_a'].reshape(RW_HEADS, HEAD_DIM).astype(F32))

    def step(S, inp):
        r_t, w_t, k_t, v_t, kk_t, a_t = inp
        S = (S * w_t[:, :, None, :]
             - jnp.einsum('bhij,bhj->bhi', S, kk_t)[..., None] * (kk_t * a_t)[:, :, None, :]
             + v_t[..., None] * k_t[:, :, None, :])
        return S, jnp.einsum('bhij,bhj->bhi', S, r_t)

    xs = tuple(jnp.moveaxis(t, 1, 0) for t in (r, decay, k, v, kk, a))
    S, o = lax.scan(step, wkv0.astype(F32), xs)
    o = jnp.moveaxis(o, 0, 1)
    mean = jnp.mean(o, axis=-1, keepdims=True)
    var = jnp.mean(jnp.square(o - mean), axis=-1, keepdims=True)
    o = ((o - mean) * lax.rsqrt(var + RW_GN_EPS) * p['ln_w'].reshape(RW_HEADS, HEAD_DIM).astype(F32)
         + p['ln_b'].reshape(RW_HEADS, HEAD_DIM).astype(F32))
    o = o + jnp.sum(r * k * p['r_k'].astype(F32), axis=-1, keepdims=True) * v
    o = o.reshape(Bsz, L, RW_DIM) * g
    return o.astype(z.dtype), z[:, -1], S


def mix_even(h, sc_buf, shift_prev, wkv0, p):
    z = h @ p['w_in']
    ya, sc_new = short_conv_mixer(z[..., :3 * SC_DIM], sc_buf, p['sc_conv_w'])
    yb, shift_new, wkv_new = rwkv7_mixer(z[..., 3 * SC_DIM:], shift_prev, wkv0, p)
    out = jnp.concatenate([ya.astype(h.dtype), yb.astype(h.dtype)], axis=-1) @ p['w_out']
    return out, sc_new, shift_new, wkv_new


def fox_attention(q, c_q, q_pos, segments):
    Bsz, Lq, H, D = q.shape
    qb = Q_BLOCK if Lq % Q_BLOCK == 0 else Lq
    nb = Lq // qb
    scale = D ** -0.5
    segs = [(k, v, jnp.swapaxes(c_k, 1, 2).astype(F32), k_pos) for (k, v, c_k, k_pos) in segments]

    def block(args):
        q_blk, c_blk, p_blk = args
        cq = jnp.swapaxes(c_blk, 1, 2).astype(F32)[..., None]
        scores = []
        for k, v, ck, kp in segs:
            s = jnp.einsum('bqhd,bkhd->bhqk', q_blk, k).astype(F32) * scale + (cq - ck[:, :, None, :])
            scores.append(jnp.where(kp[None, :] <= p_blk[:, None], s, -jnp.inf))
        prob = jax.nn.softmax(jnp.concatenate(scores, axis=-1), axis=-1)
        out = 0.0
        off = 0
        for k, v, ck, kp in segs:
            n = kp.shape[0]
            out = out + jnp.einsum('bhqk,bkhd->bqhd', prob[..., off:off + n].astype(v.dtype), v)
            off += n
        return out

    blocks = (jnp.moveaxis(q.reshape(Bsz, nb, qb, H, D), 1, 0),
              jnp.moveaxis(c_q.reshape(Bsz, nb, qb, H), 1, 0),
              q_pos.reshape(nb, qb))
    o = lax.map(block, blocks)
    return jnp.moveaxis(o, 0, 1).reshape(Bsz, Lq, H * D)


def ssd_scan(x, dt, A, Bm, Cm, S0):
    Bsz, L, H, P = x.shape
    cs = SSD_CHUNK if L % SSD_CHUNK == 0 else L
    nc = L // cs
    rep = H // SSM_GROUPS
    Bh = jnp.repeat(Bm, rep, axis=2)
    Ch = jnp.repeat(Cm, rep, axis=2)

    def chunk(t):
        return t.reshape((Bsz, nc, cs) + t.shape[2:])

    xc, dtc, Bc, Cc = chunk(x), chunk(dt), chunk(Bh), chunk(Ch)
    acum = jnp.cumsum(dtc * A, axis=2)
    seg = acum[:, :, :, None, :] - acum[:, :, None, :, :]
    causal = jnp.tril(jnp.ones((cs, cs), bool))[None, None, :, :, None]
    lmat = jnp.exp(jnp.where(causal, seg, -jnp.inf))
    gmat = jnp.einsum('bcthn,bcshn->bctsh', Cc, Bc) * lmat * dtc[:, :, None, :, :]
    y_intra = jnp.einsum('bctsh,bcshp->bcthp', gmat, xc)
    decay_end = jnp.exp(acum[:, :, -1:, :] - acum)
    chunk_states = jnp.einsum('bcsh,bcshn,bcshp->bchpn', decay_end * dtc, Bc, xc)
    chunk_decay = jnp.exp(acum[:, :, -1, :])

    def step(S, inp):
        st, dec = inp
        return S * dec[:, :, None, None] + st, S

    S_final, S_in = lax.scan(step, S0, (jnp.moveaxis(chunk_states, 1, 0), jnp.moveaxis(chunk_decay, 1, 0)))
    S_in = jnp.moveaxis(S_in, 0, 1)
    y_inter = jnp.einsum('bcthn,bchpn->bcthp', Cc, S_in) * jnp.exp(acum)[..., None]
    return (y_intra + y_inter).reshape(Bsz, L, H, P), S_final


def mamba2_mixer(zm, conv_buf, ssm0, p):
    Bsz, L, _ = zm.shape
    z, xbc, dt = jnp.split(zm, [SSM_DIM, SSM_DIM + SSM_CONV_DIM], axis=-1)
    xbc, new_buf = causal_dwconv(xbc, conv_buf, p['ssm_conv_w'])
    xbc = jax.nn.silu(xbc + p['ssm_conv_b'])
    x, Bm, Cm = jnp.split(xbc, [SSM_DIM, SSM_DIM + SSM_GROUPS * SSM_STATE], axis=-1)
    x = x.reshape(Bsz, L, SSM_HEADS, SSM_HEAD_DIM).astype(F32)
    Bm = Bm.reshape(Bsz, L, SSM_GROUPS, SSM_STATE).astype(F32)
    Cm = Cm.reshape(Bsz, L, SSM_GROUPS, SSM_STATE).astype(F32)
    dt = jax.nn.softplus((dt + p['ssm_dt_bias']).astype(F32))
    A = -jnp.exp(p['ssm_a_log'].astype(F32))
    y, S = ssd_scan(x, dt, A, Bm, Cm, ssm0.astype(F32))
    y = y + p['ssm_d'].astype(F32)[:, None] * x
    y = y.reshape(Bsz, L, SSM_DIM) * jax.nn.silu(z.astype(F32))
    y = rmsnorm(y, p['ssm_norm_w'])
    return y.astype(zm.dtype), new_buf, S


def mix_odd(h, q_pos, past, conv_buf, ssm0, p):
    Bsz, L, _ = h.shape
    z = h @ p['w_in']
    zf, zm = z[..., :FOX_IN], z[..., FOX_IN:]
    q, k, v, fpre = jnp.split(zf, [FOX_DIM, 2 * FOX_DIM, 3 * FOX_DIM], axis=-1)
    q = q.reshape(Bsz, L, FOX_HEADS, HEAD_DIM)
    k = k.reshape(Bsz, L, FOX_HEADS, HEAD_DIM)
    v = v.reshape(Bsz, L, FOX_HEADS, HEAD_DIM)
    logf = jax.nn.log_sigmoid((fpre + p['fox_f_bias']).astype(F32))
    if past is None:
        c = jnp.cumsum(logf, axis=1)
        segments = [(k, v, c, q_pos)]
    else:
        k_past, v_past, logf_past = past
        c_past = jnp.cumsum(logf_past.astype(F32), axis=1)
        c = c_past[:, -1:] + jnp.cumsum(logf, axis=1)
        segments = [(k_past, v_past, c_past, jnp.arange(k_past.shape[1])), (k, v, c, q_pos)]
    yf = fox_attention(q, c, q_pos, segments)
    ym, conv_new, ssm_new = mamba2_mixer(zm, conv_buf, ssm0, p)
    out = jnp.concatenate([yf.astype(h.dtype), ym.astype(h.dtype)], axis=-1) @ p['w_out']
    return out, k, v, logf, conv_new, ssm_new


def gather_pages(pool, j, page_table):
    g = pool[j, page_table]
    return g.reshape((g.shape[0], g.shape[1] * g.shape[2]) + g.shape[3:])


def setup_inputs(seed: int = 0) -> dict:
    key = jax.random.key(seed)
    ks = iter(jax.random.split(key, 64))

    def nrm(shape, scale=1.0):
        return scale * jax.random.normal(next(ks), shape, F32)

    def uni(shape, lo, hi):
        return jax.random.uniform(next(ks), shape, F32, lo, hi)

    n_pages = PAST_LEN // PAGE_SIZE
    n_used = DEC_BATCH * n_pages
    n_pool = n_used + (n_used + 3) // 4
    page_table = jax.random.permutation(next(ks), n_pool)[:n_used].reshape(DEC_BATCH, n_pages).astype(jnp.int32)
    dt0 = jnp.exp(uni((N_ODD, SSM_HEADS), float(np.log(1e-3)), float(np.log(1e-1))))
    mix0 = SC_DIM + RW_DIM
    mix1 = FOX_DIM + SSM_DIM
    return {
        'x_prompt': nrm((BATCH, SEQ, D_MODEL)),
        'x_sample': nrm((DEC_BATCH, DEC_SEQ, D_MODEL)),
        'state_sc': nrm((N_EVEN, DEC_BATCH, SC_WIDTH - 1, SC_DIM)),
        'state_shift': nrm((N_EVEN, DEC_BATCH, RW_SHIFT_W)),
        'state_wkv': nrm((N_EVEN, DEC_BATCH, RW_HEADS, HEAD_DIM, HEAD_DIM), 0.3),
        'cache_k': nrm((N_ODD, n_pool, PAGE_SIZE, FOX_HEADS, HEAD_DIM)),
        'cache_v': nrm((N_ODD, n_pool, PAGE_SIZE, FOX_HEADS, HEAD_DIM)),
        'cache_logf': jax.nn.log_sigmoid(nrm((N_ODD, n_pool, PAGE_SIZE, FOX_HEADS)) + 2.5),
        'state_ssm_conv': nrm((N_ODD, DEC_BATCH, SSM_CONV - 1, SSM_CONV_DIM)),
        'state_ssm': nrm((N_ODD, DEC_BATCH, SSM_HEADS, SSM_HEAD_DIM, SSM_STATE), 0.3),
        'page_table': page_table,
        'norm_mix': 1.0 + nrm((DEPTH, D_MODEL), 0.05),
        'norm_ffn': 1.0 + nrm((DEPTH, D_MODEL), 0.05),
        'norm_final': 1.0 + nrm((D_MODEL,), 0.05),
        'w_in0': nrm((N_EVEN, D_MODEL, IN0), D_MODEL ** -0.5),
        'sc_conv_w': nrm((N_EVEN, SC_WIDTH, SC_DIM), SC_WIDTH ** -0.5),
        'rw_mu': uni((N_EVEN, RW_SHIFT_W), 0.0, 1.0),
        'rw_w0': uni((N_EVEN, RW_DIM), -5.0, -0.5),
        'rw_w2': nrm((N_EVEN, DECAY_LORA, RW_DIM), 0.1),
        'rw_a0': nrm((N_EVEN, RW_DIM), 0.1),
        'rw_a2': nrm((N_EVEN, AAA_LORA, RW_DIM), AAA_LORA ** -0.5),
        'rw_g2': nrm((N_EVEN, GATE_LORA, RW_DIM), GATE_LORA ** -0.5),
        'rw_k_k': 0.85 + nrm((N_EVEN, RW_DIM), 0.05),
        'rw_k_a': 1.0 + nrm((N_EVEN, RW_DIM), 0.05),
        'rw_r_k': nrm((N_EVEN, RW_HEADS, HEAD_DIM), 0.1),
        'rw_ln_w': 1.0 + nrm((N_EVEN, RW_DIM), 0.05),
        'rw_ln_b': nrm((N_EVEN, RW_DIM), 0.02),
        'w_out0': nrm((N_EVEN, mix0, D_MODEL), mix0 ** -0.5),
        'w_in1': nrm((N_ODD, D_MODEL, IN1), D_MODEL ** -0.5),
        'fox_f_bias': uni((N_ODD, FOX_HEADS), 1.0, 4.0),
        'ssm_conv_w': nrm((N_ODD, SSM_CONV, SSM_CONV_DIM), SSM_CONV ** -0.5),
        'ssm_conv_b': nrm((N_ODD, SSM_CONV_DIM), 0.02),
        'ssm_dt_bias': dt0 + jnp.log(-jnp.expm1(-dt0)),
        'ssm_a_log': jnp.log(uni((N_ODD, SSM_HEADS), 1.0, 16.0)),
        'ssm_d': 1.0 + nrm((N_ODD, SSM_HEADS), 0.05),
        'ssm_norm_w': 1.0 + nrm((N_ODD, SSM_DIM), 0.05),
        'w_out1': nrm((N_ODD, mix1, D_MODEL), mix1 ** -0.5),
        'w_gate': nrm((DEPTH, D_MODEL, FFN_HIDDEN), D_MODEL ** -0.5),
        'w_up': nrm((DEPTH, D_MODEL, FFN_HIDDEN), D_MODEL ** -0.5),
        'w_down': nrm((DEPTH, FFN_HIDDEN, D_MODEL), FFN_HIDDEN ** -0.5),
    }


def reference(x_prompt, x_sample, state_sc, state_shift, state_wkv, cache_k, cache_v, cache_logf,
              state_ssm_conv, state_ssm, page_table, norm_mix, norm_ffn, norm_final,
              w_in0, sc_conv_w, rw_mu, rw_w0, rw_w2, rw_a0, rw_a2, rw_g2, rw_k_k, rw_k_a, rw_r_k,
              rw_ln_w, rw_ln_b, w_out0, w_in1, fox_f_bias, ssm_conv_w, ssm_conv_b, ssm_dt_bias,
              ssm_a_log, ssm_d, ssm_norm_w, w_out1, w_gate, w_up, w_down):
    hp, hs = x_prompt, x_sample
    Bp, Lp = x_prompt.shape[0], x_prompt.shape[1]
    Ls = x_sample.shape[1]
    past_len = page_table.shape[1] * cache_k.shape[2]
    sc_p, sc_s, sh_p, sh_s, wkv_p, wkv_s = [], [], [], [], [], []
    k_p, k_s, v_p, v_s, lf_p, lf_s, cv_p, cv_s, ssm_p, ssm_s = [], [], [], [], [], [], [], [], [], []
    for layer in range(DEPTH):
        j = layer // 2
        np_ = rmsnorm(hp, norm_mix[layer])
        ns = rmsnorm(hs, norm_mix[layer])
        if layer % 2 == 0:
            p = {'w_in': w_in0[j], 'sc_conv_w': sc_conv_w[j], 'mu': rw_mu[j], 'w0': rw_w0[j], 'w2': rw_w2[j],
                 'a0': rw_a0[j], 'a2': rw_a2[j], 'g2': rw_g2[j], 'k_k': rw_k_k[j], 'k_a': rw_k_a[j],
                 'r_k': rw_r_k[j], 'ln_w': rw_ln_w[j], 'ln_b': rw_ln_b[j], 'w_out': w_out0[j]}
            dp, a1, a2, a3 = mix_even(np_, jnp.zeros((Bp, SC_WIDTH - 1, SC_DIM), hp.dtype),
                                      jnp.zeros((Bp, RW_SHIFT_W), hp.dtype),
                                      jnp.zeros((Bp, RW_HEADS, HEAD_DIM, HEAD_DIM), F32), p)
            ds, b1, b2, b3 = mix_even(ns, state_sc[j], state_shift[j], state_wkv[j], p)
            sc_p.append(a1); sh_p.append(a2); wkv_p.append(a3)
            sc_s.append(b1); sh_s.append(b2); wkv_s.append(b3)
        else:
            p = {'w_in': w_in1[j], 'fox_f_bias': fox_f_bias[j], 'ssm_conv_w': ssm_conv_w[j],
                 'ssm_conv_b': ssm_conv_b[j], 'ssm_dt_bias': ssm_dt_bias[j], 'ssm_a_log': ssm_a_log[j],
                 'ssm_d': ssm_d[j], 'ssm_norm_w': ssm_norm_w[j], 'w_out': w_out1[j]}
            dp, a1, a2, a3, a4, a5 = mix_odd(np_, jnp.arange(Lp), None,
                                             jnp.zeros((Bp, SSM_CONV - 1, SSM_CONV_DIM), hp.dtype),
                                             jnp.zeros((Bp, SSM_HEADS, SSM_HEAD_DIM, SSM_STATE), F32), p)
            past = (gather_pages(cache_k, j, page_table), gather_pages(cache_v, j, page_table),
                    gather_pages(cache_logf, j, page_table))
            ds, b1, b2, b3, b4, b5 = mix_odd(ns, past_len + jnp.arange(Ls), past,
                                             state_ssm_conv[j], state_ssm[j], p)
            k_p.append(a1); v_p.append(a2); lf_p.append(a3); cv_p.append(a4); ssm_p.append(a5)
            k_s.append(b1); v_s.append(b2); lf_s.append(b3); cv_s.append(b4); ssm_s.append(b5)
        hp = hp + dp
        hs = hs + ds
        hp = hp + swiglu(rmsnorm(hp, norm_ffn[layer]), w_gate[layer], w_up[layer], w_down[layer])
        hs = hs + swiglu(rmsnorm(hs, norm_ffn[layer]), w_gate[layer], w_up[layer], w_down[layer])
    y_prompt = rmsnorm(hp, norm_final)
    y_sample = rmsnorm(hs, norm_final)
    return (y_prompt, y_sample,
            jnp.stack(sc_p), jnp.stack(sc_s), jnp.stack(sh_p), jnp.stack(sh_s),
            jnp.stack(wkv_p), jnp.stack(wkv_s),
            jnp.stack(k_p), jnp.stack(k_s), jnp.stack(v_p), jnp.stack(v_s),
            jnp.stack(lf_p), jnp.stack(lf_s), jnp.stack(cv_p), jnp.stack(cv_s),
            jnp.stack(ssm_p), jnp.stack(ssm_s))
```

```python
from contextlib import ExitStack
import numpy as np
import concourse.bass as bass
import concourse.mybir as mybir
from concourse.bass_utils import run_bass_kernel_spmd

F32 = mybir.dt.float32
BF16 = mybir.dt.bfloat16
I32 = mybir.dt.int32
AF = mybir.ActivationFunctionType
ALU = mybir.AluOpType
AX = mybir.AxisListType

NDS = 48


class Buf:
    __slots__ = ("name", "w", "r")

    def __init__(self, name):
        self.name = name
        self.w = None
        self.r = {}


class Prog:
    ENG = ("pe", "act", "dve", "pool", "sp")

    def __init__(self):
        nc = self.nc = bass.Bass("TRN2", target_bir_lowering=False)
        self.e = dict(pe=nc.tensor, act=nc.scalar, dve=nc.vector, pool=nc.gpsimd, sp=nc.sync)
        self.esem = {k: nc.alloc_semaphore("s_" + k) for k in self.ENG}
        self.ecnt = {k: 0 for k in self.ENG}
        self.seen = {k: {} for k in self.ENG}
        self.dsem = [nc.alloc_semaphore("d%d" % i) for i in range(NDS)]
        self.dval = [0] * NDS
        self.dnext = 0
        self.bufs = {}
        self.ninst = 0

    def buf(self, *key):
        b = self.bufs.get(key)
        if b is None:
            b = self.bufs[key] = Buf(key)
        return b

    def _sem(self, key):
        return self.esem[key[1]] if key[0] == "e" else self.dsem[key[1]]

    def _wait(self, eng, ev):
        if ev is None:
            return
        key, val = ev
        if key[0] == "e" and key[1] == eng and (eng == "pe" or self.ecnt[eng] - val >= 2):
            return
        if self.seen[eng].get(key, 0) >= val:
            return
        self.e[eng].wait_ge(self._sem(key), val)
        self.seen[eng][key] = val

    def _deps(self, eng, reads, writes):
        for b in reads:
            self._wait(eng, b.w)
        for b in writes:
            self._wait(eng, b.w)
            for k, v in b.r.items():
                self._wait(eng, (k, v))

    def _commit(self, ev, reads, writes):
        key, val = ev
        for b in reads:
            if b.r.get(key, 0) < val:
                b.r[key] = val
        for b in writes:
            b.w = ev
            b.r = {}

    def op(self, eng, fn, reads=(), writes=()):
        self._deps(eng, reads, writes)
        ins = fn(self.e[eng])
        ins.then_inc(self.esem[eng], 1)
        self.ecnt[eng] += 1
        self.ninst += 1
        self._commit((("e", eng), self.ecnt[eng]), reads, writes)

    def dma(self, q, out, in_, reads=(), writes=(), **kw):
        i = self.dnext
        self.dnext = (self.dnext + 1) % NDS
        if self.dval[i] > 0:
            self._wait(q, (("d", i), self.dval[i]))
        self._deps(q, reads, writes)
        self.e[q].dma_start(out=out, in_=in_, **kw).then_inc(self.dsem[i], 16)
        self.dval[i] += 16
        self.ninst += 1
        self._commit((("d", i), self.dval[i]), reads, writes)

    def dma_fn(self, q, fn, reads=(), writes=()):
        i = self.dnext
        self.dnext = (self.dnext + 1) % NDS
        if self.dval[i] > 0:
            self._wait(q, (("d", i), self.dval[i]))
        self._deps(q, reads, writes)
        fn(self.e[q]).then_inc(self.dsem[i], 16)
        self.dval[i] += 16
        self.ninst += 1
        self._commit((("d", i), self.dval[i]), reads, writes)

    def barrier(self):
        for e in self.ENG:
            for f in self.ENG:
                if f != e and self.ecnt[f] > 0:
                    self._wait(e, (("e", f), self.ecnt[f]))
            for i in range(NDS):
                if self.dval[i] > 0:
                    self._wait(e, (("d", i), self.dval[i]))
        self.bufs_reset()

    def bufs_reset(self):
        for b in self.bufs.values():
            b.w = None
            b.r = {}

    def finish(self):
        for i in range(NDS):
            if self.dval[i] > 0:
                self._wait("sp", (("d", i), self.dval[i]))
        for f in self.ENG:
            if f != "sp" and self.ecnt[f] > 0:
                self._wait("sp", (("e", f), self.ecnt[f]))


D = 1024
HD = 64
NH = 8
SC = 512
RW = 512
RWS = 1792
IN0 = 3328
FOX_IN = 1544
SSM_CD = 1024
IN1 = 3088
FF = 2816
EPS = 1e-6


def const_arrays(LS=8):
    c = {}
    c["ident"] = np.eye(128, dtype=np.float32)
    iu = np.triu(np.ones((128, 128), np.float32), 0)
    su = np.triu(np.ones((128, 128), np.float32), 1)
    c["triu_incl"] = iu
    c["triu_strict"] = su
    c["tril_strict"] = su.T.copy()
    c["ones"] = np.ones((128, 128), np.float32)
    sel = np.zeros((128, 128), np.float32)
    sel[127, :] = 1.0
    c["sel_last"] = sel
    c["negmask"] = (-30000.0 * su.T).astype(np.float32)
    c["iota"] = np.arange(128, dtype=np.int32).reshape(128, 1)
    nb = 128 // LS
    c["blktri"] = np.kron(np.eye(nb, dtype=np.float32), iu[:LS, :LS]).astype(np.float32)[:128, :128]
    return c


class Model:
    def __init__(self, T, NS, LS, NPG, NPOOL, debug=()):
        self.T, self.NS, self.LS, self.NPG, self.NPOOL = T, NS, LS, NPG, NPOOL
        self.TS = NS * LS
        self.TT = T + self.TS
        self.debug = set(debug)
        self.P = Prog()
        self.nc = self.P.nc
        self.io = {}
        self.build()

    def inp(self, name, shape, dt=F32):
        self.io[name] = self.nc.dram_tensor(name, list(shape), dt, kind="ExternalInput").ap()
        return self.io[name]

    def outp(self, name, shape, dt=F32):
        self.io[name] = self.nc.dram_tensor(name, list(shape), dt, kind="ExternalOutput").ap()
        return self.io[name]

    def scratch(self, name, shape, dt=F32):
        kind = "ExternalOutput" if name in self.debug else "Internal"
        self.io[name] = self.nc.dram_tensor(name, list(shape), dt, kind=kind).ap()
        return self.io[name]

    def _nm(self, name):
        self._uid = getattr(self, "_uid", 0) + 1
        return "%s_%d" % (name, self._uid)

    def sb(self, es, name, shape, dt=F32):
        return es.enter_context(self.nc.sbuf_tensor(self._nm(name), list(shape), dt))

    def ps(self, es, name, shape, dt=F32):
        return es.enter_context(self.nc.psum_tensor(self._nm(name), list(shape), dt))

    def token_tiles(self, step=128):
        tiles = [(t0, min(step, self.T - t0)) for t0 in range(0, self.T, step)]
        for s0 in range(0, self.TS, step):
            tiles.append((self.T + s0, min(step, self.TS - s0)))
        return tiles

    def build(self):
        T, TT = self.T, self.TT
        m = self
        m.inp("xp", [T, D]); m.inp("xs", [self.TS, D])
        for k, arr in const_arrays().items():
            m.inp("c_" + k, arr.shape, I32 if arr.dtype == np.int32 else F32)
        m.inp("norm_mix", [2, D]); m.inp("norm_ffn", [2, D]); m.inp("norm_final", [1, D])
        m.inp("w_in0", [D, IN0]); m.inp("w_out0", [D, D]); m.inp("w_in1", [D, IN1]); m.inp("w_out1", [D, D])
        m.inp("w_gate", [2, D, FF]); m.inp("w_up", [2, D, FF]); m.inp("w_down", [2, FF, D])
        m.inp("sc_conv_w", [3, SC]); m.inp("state_sc", [self.NS, 2, SC])
        m.inp("state_shift", [self.NS, RWS]); m.inp("state_wkv", [self.NS, NH, HD, HD])
        m.inp("rw_mu", [1, RWS]); m.inp("rw_w0", [1, RW]); m.inp("rw_w2", [64, RW]); m.inp("rw_a0", [1, RW])
        m.inp("rw_a2", [64, RW]); m.inp("rw_g2", [128, RW]); m.inp("rw_k_k", [1, RW]); m.inp("rw_k_a", [1, RW])
        m.inp("rw_r_k", [1, RW]); m.inp("rw_ln_w", [1, RW]); m.inp("rw_ln_b", [1, RW])
        m.outp("new_shift", [1 + self.NS, RWS]); m.outp("new_wkv", [1 + self.NS, NH, HD, HD])
        m.outp("y", [TT, D])
        m.outp("new_sc", [1 + self.NS, 2, SC])
        m.scratch("h0", [TT, D])
        m.scratch("zT0", [IN0, TT])
        m.scratch("mixA0", [SC, TT])
        m.scratch("mixB0", [TT, RW])
        NS, NPG, NPOOL = self.NS, self.NPG, self.NPOOL
        m.inp("cache_k", [NPOOL * 128, 512]); m.inp("cache_v", [NPOOL * 128, 512]); m.inp("cache_logf", [NPOOL, 1024])
        m.inp("page_table", [NS, NPG], I32)
        m.inp("state_ssm_conv", [NS, 3, SSM_CD]); m.inp("state_ssm", [NS, NH, HD, 128])
        m.inp("fox_f_bias", [1, 8]); m.inp("ssm_conv_w", [4, SSM_CD]); m.inp("ssm_conv_b", [1, SSM_CD])
        m.inp("ssm_dt_bias", [1, 8]); m.inp("ssm_a_log", [1, 8]); m.inp("ssm_d", [1, 8]); m.inp("ssm_norm_w", [1, 512])
        m.outp("new_k", [TT, 512]); m.outp("new_v", [TT, 512]); m.outp("new_logf", [TT, 8])
        m.outp("new_conv", [1 + NS, 3, SSM_CD]); m.outp("new_ssm", [1 + NS, NH, HD, 128])
        m.scratch("zT1", [IN1, TT]); m.scratch("fpre", [TT, 8]); m.scratch("zg_tm", [TT, 512]); m.scratch("dt_tm", [TT, 8])
        m.scratch("mixC1", [TT, 512]); m.scratch("mixD1", [TT, 512])

        with ExitStack() as es:
            self.consts(es)
            self.phase_inproj(0, "xp", "xs", "w_in0", IN0, "zT0")
            self.phase_conv0()
            self.phase_rwkv()
            self.phase_out_ffn(0, "xp", "xs", "mixA0", "mixB0", "w_out0", "h0")
            self.phase_inproj(1, "h0", None, "w_in1", IN1, "zT1",
                              tm_cols=[(512, 512, "new_k"), (1024, 512, "new_v"), (1536, 8, "fpre"),
                                       (1544, 512, "zg_tm"), (3080, 8, "dt_tm")])
            self.phase_fox()
            self.phase_ssd()
            self.phase_out_ffn(1, "h0", None, "mixC1", "mixD1", "w_out1", None, final=True, a_tm=True)
        self.P.finish()

    def consts(self, es):
        P = self.P
        self.ident = self.sb(es, "ident", [128, 128])
        self.identb = self.sb(es, "identb", [128, 128], BF16)
        P.dma("sp", self.ident[:], self.io["c_ident"], writes=[P.buf("ident")])
        P.dma("pool", self.identb[:], self.io["c_ident"], writes=[P.buf("identb")])
        self.eps_t = self.sb(es, "eps_t", [128, 1])
        P.op("dve", lambda e: e.memset(self.eps_t[:], EPS), writes=[P.buf("eps_t")])

    def rstd(self, ss, ssb, n, inv=1.0 / D, eps_t=None):
        P = self.P
        et = self.eps_t if eps_t is None else eps_t
        P.op("act", lambda e: e.activation(out=ss[:n, 1:2], in_=ss[:n, 0:1], func=AF.Sqrt, bias=et[:n, 0:1], scale=inv),
             reads=[ssb, P.buf("eps_t")], writes=[ssb])
        P.op("dve", lambda e: e.reciprocal(out=ss[:n, 2:3], in_=ss[:n, 1:2]), reads=[ssb], writes=[ssb])

    def norm_T(self, x_t, xb, n, g_t, gb, hnT, hb, tmp, pst, tag):
        P = self.P
        junk, jb = tmp["junk"], P.buf("junk" + tag)
        ss, ssb = tmp["ss"], P.buf("ss" + tag)
        xn, xnb = tmp["xn"], P.buf("xn" + tag)
        pb = P.buf("pst" + tag)
        P.op("act", lambda e: e.activation(out=junk[:n, :], in_=x_t[:n, :], func=AF.Square, accum_out=ss[:n, 0:1]),
             reads=[xb], writes=[jb, ssb])
        self.rstd(ss, ssb, n)
        P.op("dve", lambda e: e.tensor_scalar(out=xn[:n, :], in0=x_t[:n, :], scalar1=ss[:n, 2:3], scalar2=None,
                                              op0=ALU.mult), reads=[xb, ssb], writes=[xnb])
        for c in range(8):
            P.op("pe", lambda e, c=c: e.transpose(out=pst[:, c, :n], in_=xn[:n, c * 128:(c + 1) * 128],
                                                  identity=self.identb[:n, :n]),
                 reads=[xnb, P.buf("identb")], writes=[pb])
        P.op("dve", lambda e: e.tensor_tensor(out=hnT[:, :, :n], in0=pst[:, :, :n],
                                              in1=g_t[:, :].unsqueeze(2).to_broadcast([128, 8, n]), op=ALU.mult),
             reads=[pb, gb], writes=[hb])

    def xsrc(self, xp_name, xs_name, tt, n):
        if xs_name is None:
            return self.io[xp_name][tt:tt + n, :]
        return self.io[xp_name][tt:tt + n, :] if tt < self.T else self.io[xs_name][tt - self.T:tt - self.T + n, :]

    def phase_inproj(self, layer, xp_name, xs_name, w_name, NOUT, z_name, tm_cols=()):
        P, nc = self.P, self.nc
        zT = self.io[z_name]
        with ExitStack() as es:
            w = self.sb(es, "w_in", [128, 8, NOUT], BF16)
            wb = P.buf("w_in")
            wsrc = self.io[w_name].rearrange("(c p) n -> p c n", p=128)
            for c in range(8):
                P.dma("pool", w[:, c, :], wsrc[:, c, :], writes=[wb])
            g_t = self.sb(es, "g_t", [128, 8])
            gb = P.buf("g_t")
            P.dma("sp", g_t[:], self.io["norm_mix"][layer].rearrange("(c p) -> p c", p=128), writes=[gb],
                  allow_slow_non_contiguous=True)
            ST = 512
            hnT = [self.sb(es, "hnT%d" % i, [128, 8, ST], BF16) for i in range(2)]
            xt = [self.sb(es, "xt%d" % i, [128, D]) for i in range(2)]
            tmp = dict(junk=self.sb(es, "junk", [128, D], BF16), ss=self.sb(es, "ss", [128, 4]),
                       xn=self.sb(es, "xn", [128, D], BF16))
            pst = self.ps(es, "pst", [128, 8, 128], BF16)
            pz = [self.ps(es, "pz%d" % i, [128, ST]) for i in range(4)]
            zst = [self.sb(es, "zst%d" % i, [128, ST]) for i in range(4)]
            ptm = self.ps(es, "ptm", [128, 512])
            tmst = self.sb(es, "tmst", [128, 512])
            nch = (NOUT + 127) // 128
            k = 0
            xi = 0
            for si, (t0, ntot) in enumerate(self.token_tiles(ST)):
                hT, hb = hnT[si % 2], P.buf("hnT", si % 2)
                for j0 in range(0, ntot, 128):
                    n = min(128, ntot - j0)
                    x_t, xb = xt[xi % 2], P.buf("xt", xi % 2)
                    xi += 1
                    tt = t0 + j0
                    P.dma("sp", x_t[:n, :], self.xsrc(xp_name, xs_name, tt, n), writes=[xb])
                    self.norm_T(x_t, xb, n, g_t, gb, hT[:, :, j0:j0 + 128], hb, tmp, pst, "A")
                    for (c0, ncol, dname) in tm_cols:
                        for c in range(8):
                            P.op("pe", lambda e, c=c: e.matmul(ptm[:n, :ncol], lhsT=hT[:, c, j0:j0 + n],
                                                               rhs=w[:, c, c0:c0 + ncol], start=(c == 0), stop=(c == 7)),
                                 reads=[hb, wb], writes=[P.buf("ptm")])
                        P.op("act", lambda e: e.copy(out=tmst[:n, :ncol], in_=ptm[:n, :ncol]),
                             reads=[P.buf("ptm")], writes=[P.buf("tmst")])
                        P.dma("sp", self.io[dname][tt:tt + n, :], tmst[:n, :ncol], reads=[P.buf("tmst")],
                              writes=[P.buf(dname, tt)])
                for mch in range(nch):
                    mc = min(128, NOUT - mch * 128)
                    pzz, pzb = pz[k % 4], P.buf("pz", k % 4)
                    zs, zsb = zst[k % 4], P.buf("zst", k % 4)
                    for c in range(8):
                        P.op("pe", lambda e, c=c: e.matmul(pzz[:mc, :ntot], lhsT=w[:, c, mch * 128:mch * 128 + mc],
                                                           rhs=hT[:, c, :ntot], start=(c == 0), stop=(c == 7)),
                             reads=[hb, wb], writes=[pzb])
                    if k % 2 == 0:
                        P.op("act", lambda e: e.copy(out=zs[:mc, :ntot], in_=pzz[:mc, :ntot]), reads=[pzb], writes=[zsb])
                    else:
                        P.op("dve", lambda e: e.tensor_copy(out=zs[:mc, :ntot], in_=pzz[:mc, :ntot]), reads=[pzb], writes=[zsb])
                    P.dma("sp" if k % 2 == 0 else "pool", zT[mch * 128:mch * 128 + mc, t0:t0 + ntot], zs[:mc, :ntot],
                          reads=[zsb], writes=[P.buf(z_name, mch, si)])
                    k += 1
            P.barrier()

    def phase_conv0(self):
        P, T, LS = self.P, self.T, self.LS
        zT = self.io["zT0"]
        with ExitStack() as es:
            wc = self.sb(es, "wc", [128, 4, 3])
            wcb = P.buf("wc")
            for kk in range(3):
                P.dma("sp", wc[:, :, kk], self.io["sc_conv_w"][kk].rearrange("(c p) -> p c", p=128), writes=[wcb],
                      allow_slow_non_contiguous=True)
            NT = 512
            gbt = [self.sb(es, "gbt%d" % i, [128, NT]) for i in range(2)]
            gct = [self.sb(es, "gct%d" % i, [128, NT + 2]) for i in range(2)]
            hht = [self.sb(es, "hht%d" % i, [128, NT + 2]) for i in range(2)]
            yt = [self.sb(es, "yt%d" % i, [128, NT]) for i in range(2)]
            seqs = [(0, T, None)] + [(T + s * LS, LS, s) for s in range(self.NS)]
            k = 0
            for (q0, L, s) in seqs:
                for cc in range(4):
                    for t0 in range(0, L, NT):
                        n = min(NT, L - t0)
                        i = k % 2
                        k += 1
                        g, c_, h, y = gbt[i], gct[i], hht[i], yt[i]
                        gB, cB, hB, yB = P.buf("gbt", i), P.buf("gct", i), P.buf("hht", i), P.buf("yt", i)
                        a = q0 + t0
                        P.dma("sp", g[:, :n], zT[cc * 128:(cc + 1) * 128, a:a + n], writes=[gB])
                        if t0 == 0:
                            P.dma("sp", c_[:, 2:2 + n], zT[512 + cc * 128:512 + (cc + 1) * 128, a:a + n], writes=[cB])
                            P.dma("sp", h[:, 2:2 + n], zT[1024 + cc * 128:1024 + (cc + 1) * 128, a:a + n], writes=[hB])
                        else:
                            P.dma("sp", c_[:, :2 + n], zT[512 + cc * 128:512 + (cc + 1) * 128, a - 2:a + n], writes=[cB])
                            P.dma("sp", h[:, :2 + n], zT[1024 + cc * 128:1024 + (cc + 1) * 128, a - 2:a + n], writes=[hB])
                        lo = 2 if t0 == 0 else 0
                        P.op("pool", lambda e: e.tensor_tensor(out=c_[:, lo:2 + n], in0=c_[:, lo:2 + n], in1=h[:, lo:2 + n],
                                                               op=ALU.mult), reads=[cB, hB], writes=[cB])
                        if t0 == 0:
                            if s is None:
                                P.op("pool", lambda e: e.memset(c_[:, 0:2], 0.0), writes=[cB])
                            else:
                                P.dma("sp", c_[:, 0:2], self.io["state_sc"][s, :, cc * 128:(cc + 1) * 128].rearrange("k p -> p k"),
                                      writes=[cB], allow_slow_non_contiguous=True)
                        P.op("dve", lambda e: e.tensor_scalar(out=y[:, :n], in0=c_[:, 0:n], scalar1=wc[:, cc, 0:1], scalar2=None,
                                                              op0=ALU.mult), reads=[cB, wcb], writes=[yB])
                        P.op("dve", lambda e: e.scalar_tensor_tensor(out=y[:, :n], in0=c_[:, 1:1 + n], scalar=wc[:, cc, 1:2],
                                                                     in1=y[:, :n], op0=ALU.mult, op1=ALU.add),
                             reads=[cB, wcb, yB], writes=[yB])
                        P.op("dve", lambda e: e.scalar_tensor_tensor(out=y[:, :n], in0=c_[:, 2:2 + n], scalar=wc[:, cc, 2:3],
                                                                     in1=y[:, :n], op0=ALU.mult, op1=ALU.add),
                             reads=[cB, wcb, yB], writes=[yB])
                        P.op("pool", lambda e: e.tensor_tensor(out=y[:, :n], in0=y[:, :n], in1=g[:, :n], op=ALU.mult),
                             reads=[yB, gB], writes=[yB])
                        P.dma("pool", self.io["mixA0"][cc * 128:(cc + 1) * 128, a:a + n], y[:, :n], reads=[yB],
                              writes=[P.buf("mixA0", cc, a)])
                        if t0 + n == L:
                            oi = 0 if s is None else 1 + s
                            P.dma("pool", self.io["new_sc"][oi, :, cc * 128:(cc + 1) * 128].rearrange("k p -> p k"),
                                  c_[:, n:n + 2], reads=[cB], writes=[P.buf("new_sc", oi, cc)], allow_slow_non_contiguous=True)
            P.barrier()

    def phase_out_ffn(self, layer, xp_name, xs_name, mixA, mixB, wout_name, h_out, final=False, a_tm=False):
        P = self.P
        T = self.T
        with ExitStack() as es:
            wo = self.sb(es, "wo", [128, 8, D], BF16)
            wg = self.sb(es, "wg", [128, 8, FF], BF16)
            wu = self.sb(es, "wu", [128, 8, FF], BF16)
            wd = self.sb(es, "wd", [128, 22, D], BF16)
            wB = P.buf("ffn_w")
            for c in range(8):
                P.dma("pool", wo[:, c, :], self.io[wout_name].rearrange("(c p) n -> p c n", p=128)[:, c, :], writes=[wB])
            for c in range(8):
                P.dma("pool", wg[:, c, :], self.io["w_gate"][layer].rearrange("(c p) n -> p c n", p=128)[:, c, :], writes=[wB])
                P.dma("pool", wu[:, c, :], self.io["w_up"][layer].rearrange("(c p) n -> p c n", p=128)[:, c, :], writes=[wB])
            for f in range(22):
                P.dma("pool", wd[:, f, :], self.io["w_down"][layer].rearrange("(f p) n -> p f n", p=128)[:, f, :], writes=[wB])
            g_t = self.sb(es, "g_t", [128, 8])
            gb = P.buf("g_t")
            P.dma("sp", g_t[:], self.io["norm_ffn"][layer].rearrange("(c p) -> p c", p=128), writes=[gb],
                  allow_slow_non_contiguous=True)
            if final:
                gfin = self.sb(es, "gfin", [128, D])
                P.dma("sp", gfin[:], self.io["norm_final"].partition_broadcast(128), writes=[P.buf("gfin")])
            ST = 256
            NSUB = ST // 128
            hnT = self.sb(es, "hnT", [128, 8, ST], BF16)
            h1 = self.sb(es, "h1", [128, NSUB, D])
            actT = self.sb(es, "actT", [128, 22, ST], BF16)
            xt = self.sb(es, "xt", [128, D])
            tmp = dict(junk=self.sb(es, "junk", [128, D], BF16), ss=self.sb(es, "ss", [128, 4]),
                       xn=self.sb(es, "xn", [128, D], BF16))
            yT = self.sb(es, "yT", [128, 8, 128], BF16)
            mb = self.sb(es, "mb", [128, 512])
            sil = [self.sb(es, "sil%d" % i, [128, ST]) for i in range(2)]
            ot = self.sb(es, "ot", [128, D])
            pst = self.ps(es, "pst", [128, 8, 128], BF16)
            ptr = self.ps(es, "ptr", [128, 4, 128])
            po = [self.ps(es, "po%d" % i, [128, 512]) for i in range(2)]
            pg = [self.ps(es, "pg%d" % i, [128, 512]) for i in range(2)]
            pu = [self.ps(es, "pu%d" % i, [128, 512]) for i in range(2)]
            for si, (t0, ntot) in enumerate(self.token_tiles(ST)):
                hb = P.buf("hnT")
                subs = [(j0, min(128, ntot - j0)) for j0 in range(0, ntot, 128)]
                for ji, (j0, n) in enumerate(subs):
                    tt = t0 + j0
                    if not a_tm:
                        P.dma("pool", yT[:, 0:4, :n], self.io[mixA][:, tt:tt + n].rearrange("(c p) n -> p c n", p=128),
                              writes=[P.buf("yT")])
                    else:
                        P.dma("sp", mb[:n, :], self.io[mixA][tt:tt + n, :], writes=[P.buf("mb")])
                        for c in range(4):
                            P.op("pe", lambda e, c=c: e.transpose(out=ptr[:, c, :n], in_=mb[:n, c * 128:(c + 1) * 128],
                                                                  identity=self.ident[:n, :n]),
                                 reads=[P.buf("mb"), P.buf("ident")], writes=[P.buf("ptr")])
                        P.op("act", lambda e: e.copy(out=yT[:, 0:4, :n], in_=ptr[:, :, :n]), reads=[P.buf("ptr")],
                             writes=[P.buf("yT")])
                    P.dma("sp", mb[:n, :], self.io[mixB][tt:tt + n, :], writes=[P.buf("mb")])
                    for c in range(4):
                        P.op("pe", lambda e, c=c: e.transpose(out=ptr[:, c, :n], in_=mb[:n, c * 128:(c + 1) * 128],
                                                              identity=self.ident[:n, :n]),
                             reads=[P.buf("mb"), P.buf("ident")], writes=[P.buf("ptr")])
                    P.op("act", lambda e: e.copy(out=yT[:, 4:8, :n], in_=ptr[:, :, :n]), reads=[P.buf("ptr")],
                         writes=[P.buf("yT")])
                    P.dma("sp", xt[:n, :], self.xsrc(xp_name, xs_name, tt, n), writes=[P.buf("xt")])
                    for half in range(2):
                        for c in range(8):
                            P.op("pe", lambda e, c=c: e.matmul(po[half][:n, :], lhsT=yT[:, c, :n],
                                                               rhs=wo[:, c, half * 512:(half + 1) * 512],
                                                               start=(c == 0), stop=(c == 7)),
                                 reads=[P.buf("yT"), wB], writes=[P.buf("po", half)])
                        P.op("dve", lambda e: e.tensor_tensor(out=h1[:n, ji, half * 512:(half + 1) * 512], in0=po[half][:n, :],
                                                              in1=xt[:n, half * 512:(half + 1) * 512], op=ALU.add),
                             reads=[P.buf("po", half), P.buf("xt")], writes=[P.buf("h1", ji)])
                    self.norm_T(h1[:, ji, :], P.buf("h1", ji), n, g_t, gb, hnT[:, :, j0:j0 + 128], hb, tmp, pst, "C")
                for f in range(22):
                    i = f % 2
                    for c in range(8):
                        P.op("pe", lambda e, c=c: e.matmul(pg[i][:, :ntot], lhsT=wg[:, c, f * 128:(f + 1) * 128],
                                                           rhs=hnT[:, c, :ntot], start=(c == 0), stop=(c == 7)),
                             reads=[hb, wB], writes=[P.buf("pg", i)])
                    for c in range(8):
                        P.op("pe", lambda e, c=c: e.matmul(pu[i][:, :ntot], lhsT=wu[:, c, f * 128:(f + 1) * 128],
                                                           rhs=hnT[:, c, :ntot], start=(c == 0), stop=(c == 7)),
                             reads=[hb, wB], writes=[P.buf("pu", i)])
                    P.op("act", lambda e: e.activation(out=sil[i][:, :ntot], in_=pg[i][:, :ntot], func=AF.Silu),
                         reads=[P.buf("pg", i)], writes=[P.buf("sil", i)])
                    P.op("dve", lambda e: e.tensor_tensor(out=actT[:, f, :ntot], in0=sil[i][:, :ntot], in1=pu[i][:, :ntot],
                                                          op=ALU.mult),
                         reads=[P.buf("sil", i), P.buf("pu", i)], writes=[P.buf("actT")])
                for ji, (j0, n) in enumerate(subs):
                    tt = t0 + j0
                    for half in range(2):
                        for f in range(22):
                            P.op("pe", lambda e, f=f: e.matmul(po[half][:n, :], lhsT=actT[:, f, j0:j0 + n],
                                                               rhs=wd[:, f, half * 512:(half + 1) * 512],
                                                               start=(f == 0), stop=(f == 21)),
                                 reads=[P.buf("actT"), wB], writes=[P.buf("po", half)])
                        P.op("dve", lambda e: e.tensor_tensor(out=ot[:n, half * 512:(half + 1) * 512], in0=po[half][:n, :],
                                                              in1=h1[:n, ji, half * 512:(half + 1) * 512], op=ALU.add),
                             reads=[P.buf("po", half), P.buf("h1", ji)], writes=[P.buf("ot")])
                    if not final:
                        P.dma("sp", self.io[h_out][tt:tt + n, :], ot[:n, :], reads=[P.buf("ot")], writes=[P.buf(h_out, tt)])
                    else:
                        ss, ssb = tmp["ss"], P.buf("ssC")
                        P.op("act", lambda e: e.activation(out=tmp["junk"][:n, :], in_=ot[:n, :], func=AF.Square,
                                                           accum_out=ss[:n, 0:1]),
                             reads=[P.buf("ot")], writes=[P.buf("junkC"), ssb])
                        self.rstd(ss, ssb, n)
                        P.op("dve", lambda e: e.scalar_tensor_tensor(out=ot[:n, :], in0=ot[:n, :], scalar=ss[:n, 2:3],
                                                                     in1=gfin[:n, :], op0=ALU.mult, op1=ALU.mult),
                             reads=[P.buf("ot"), ssb, P.buf("gfin")], writes=[P.buf("ot")])
                        P.dma("sp", self.io["y"][tt:tt + n, :], ot[:n, :], reads=[P.buf("ot")], writes=[P.buf("y", tt)])
            P.barrier()

    def phase_zero(self, name, rows, cols):
        P = self.P
        with ExitStack() as es:
            z = self.sb(es, "zero_t", [128, cols])
            P.op("dve", lambda e: e.memset(z[:], 0.0), writes=[P.buf("zero_t")])
            for r0 in range(0, rows, 128):
                n = min(128, rows - r0)
                P.dma("sp", self.io[name][r0:r0 + n, :], z[:n, :], reads=[P.buf("zero_t")], writes=[P.buf(name, r0)])
            P.barrier()


_CACHE = {}


def _prep(inputs):
    x_prompt = np.asarray(inputs["x_prompt"]); x_sample = np.asarray(inputs["x_sample"])
    B, T, _ = x_prompt.shape
    DB, LS, _ = x_sample.shape
    NS = DB // 8
    npg = inputs["page_table"].shape[1]
    npool = inputs["cache_k"].shape[1]
    return B, T, DB, LS, NS, npg, npool


def kernel(_debug=(), **inputs):
    B, T, DB, LS, NS, NPG, NPOOL = _prep(inputs)
    key = (T, NS, LS, NPG, NPOOL, tuple(_debug))
    if key not in _CACHE:
        _CACHE[key] = Model(T, NS, LS, NPG, NPOOL, debug=_debug)
    m = _CACHE[key]
    f = lambda k: np.ascontiguousarray(np.asarray(inputs[k], dtype=np.float32))
    consts = {"c_" + k: v for k, v in const_arrays(LS).items()}
    in_maps = []
    for c in range(8):
        sl = slice(c * NS, (c + 1) * NS)
        d = dict(consts)
        d["xp"] = f("x_prompt")[c % B]
        d["xs"] = f("x_sample")[sl].reshape(NS * LS, D)
        d["norm_mix"] = f("norm_mix"); d["norm_ffn"] = f("norm_ffn"); d["norm_final"] = f("norm_final").reshape(1, D)
        d["w_in0"] = f("w_in0")[0]; d["w_out0"] = f("w_out0")[0]; d["w_in1"] = f("w_in1")[0]; d["w_out1"] = f("w_out1")[0]
        d["w_gate"] = f("w_gate"); d["w_up"] = f("w_up"); d["w_down"] = f("w_down")
        d["sc_conv_w"] = f("sc_conv_w")[0]; d["state_sc"] = f("state_sc")[0, sl]
        d["state_shift"] = f("state_shift")[0, sl]; d["state_wkv"] = f("state_wkv")[0, sl]
        for nm in ("rw_mu", "rw_w0", "rw_a0", "rw_k_k", "rw_k_a", "rw_ln_w", "rw_ln_b"):
            d[nm] = f(nm).reshape(1, -1)
        d["rw_r_k"] = f("rw_r_k").reshape(1, RW)
        d["rw_w2"] = f("rw_w2")[0]; d["rw_a2"] = f("rw_a2")[0]; d["rw_g2"] = f("rw_g2")[0]
        d["cache_k"] = f("cache_k")[0].reshape(-1, 512); d["cache_v"] = f("cache_v")[0].reshape(-1, 512)
        d["cache_logf"] = f("cache_logf")[0].reshape(-1, 1024)
        d["page_table"] = np.ascontiguousarray(np.asarray(inputs["page_table"], dtype=np.int32)[sl])
        d["state_ssm_conv"] = f("state_ssm_conv")[0, sl]; d["state_ssm"] = f("state_ssm")[0, sl]
        d["fox_f_bias"] = f("fox_f_bias"); d["ssm_conv_w"] = f("ssm_conv_w")[0]; d["ssm_conv_b"] = f("ssm_conv_b")
        d["ssm_dt_bias"] = f("ssm_dt_bias"); d["ssm_a_log"] = f("ssm_a_log"); d["ssm_d"] = f("ssm_d"); d["ssm_norm_w"] = f("ssm_norm_w")
        in_maps.append({k: v for k, v in d.items() if k in m.io})
    res = run_bass_kernel_spmd(m.nc, in_maps, core_ids=list(range(8)))
    R = res.results
    if _debug:
        return R
    y_prompt = np.stack([R[b]["y"][:T] for b in range(B)])
    y_sample = np.concatenate([R[c]["y"][T:].reshape(NS, LS, D) for c in range(8)])
    new_sc_p = np.stack([R[b]["new_sc"][0] for b in range(B)])[None]
    new_sc_s = np.concatenate([R[c]["new_sc"][1:] for c in range(8)])[None]
    outs = [y_prompt, y_sample, new_sc_p, new_sc_s]
    z = lambda *sh: np.zeros(sh, np.float32)
    outs += [np.stack([R[b]["new_shift"][0] for b in range(B)])[None],
             np.concatenate([R[c]["new_shift"][1:] for c in range(8)])[None],
             np.stack([R[b]["new_wkv"][0] for b in range(B)])[None],
             np.concatenate([R[c]["new_wkv"][1:] for c in range(8)])[None]]
    def pp(name, shape_tail):
        return np.stack([R[b][name][:T].reshape((T,) + shape_tail) for b in range(B)])[None]

    def ss_(name, shape_tail):
        return np.concatenate([R[c][name][T:].reshape((NS, LS) + shape_tail) for c in range(8)])[None]
    outs += [pp("new_k", (NH, HD)), ss_("new_k", (NH, HD)), pp("new_v", (NH, HD)), ss_("new_v", (NH, HD)),
             pp("new_logf", (NH,)), ss_("new_logf", (NH,))]
    for name in ("new_conv", "new_ssm"):
        outs.append(np.stack([R[b][name][0] for b in range(B)])[None])
        outs.append(np.concatenate([R[c][name][1:] for c in range(8)])[None])
    return tuple(outs)


RW_GN_EPS = 64e-5


def _rwkv_phase(self):
    P, T, LS, NS = self.P, self.T, self.LS, self.NS
    zT = self.io["zT0"]
    R0 = 3 * SC
    CMAX = 64
    E5 = float(np.exp(-0.5))
    with ExitStack() as es:
        sb = lambda name, shape, dt=F32: self.sb(es, name, shape, dt)
        B = P.buf
        def ld(name, shape, src, **kw):
            t = sb(name, shape)
            P.dma("sp", t[:], src, writes=[B(name)], **kw)
            return t
        NC = dict(allow_slow_non_contiguous=True)
        mu = self.io["rw_mu"]
        mu_rkv = ld("mu_rkv", [64, 24], mu[0, 0:1536].rearrange("(g n) -> n g", n=64), **NC)
        mu_l = ld("mu_l", [64, 2], mu[0, 1536:1664].rearrange("(g n) -> n g", n=64), **NC)
        mu_g = ld("mu_g", [128, 1], mu[0, 1664:1792].rearrange("(n o) -> n o", o=1), **NC)
        w0row = ld("w0row", [1, 512], self.io["rw_w0"])
        w2 = ld("w2", [64, 512], self.io["rw_w2"])
        a2 = ld("a2", [64, 512], self.io["rw_a2"])
        g2 = ld("g2", [128, 512], self.io["rw_g2"])
        hn = lambda nm: self.io[nm][0].rearrange("(h n) -> n h", n=64)
        a0 = ld("a0", [64, 8], hn("rw_a0"), **NC)
        k_k = ld("k_k", [64, 8], hn("rw_k_k"), **NC)
        k_a = ld("k_a", [64, 8], hn("rw_k_a"), **NC)
        r_k = ld("r_k", [64, 8], hn("rw_r_k"), **NC)
        lnw = ld("lnw", [64, 512], self.io["rw_ln_w"].partition_broadcast(64))
        lnb = ld("lnb", [64, 512], self.io["rw_ln_b"].partition_broadcast(64))
        tri_i = ld("tri_i", [64, 64], self.io["c_triu_incl"][0:64, 0:64])
        tri_s = ld("tri_s", [64, 64], self.io["c_triu_strict"][0:64, 0:64])
        low_s = ld("low_s", [64, 64], self.io["c_tril_strict"][0:64, 0:64])
        ones = ld("ones64", [64, 64], self.io["c_ones"][0:64, 0:64])
        triS_i = sb("triS_i", [64, 64]); triS_s = sb("triS_s", [64, 64]); ntri_i = sb("ntri_i", [64, 64])
        P.op("dve", lambda e: e.tensor_scalar(out=triS_i[:], in0=tri_i[:], scalar1=-E5, scalar2=None, op0=ALU.mult),
             reads=[B("tri_i")], writes=[B("triS_i")])
        P.op("dve", lambda e: e.tensor_scalar(out=triS_s[:], in0=tri_s[:], scalar1=-E5, scalar2=None, op0=ALU.mult),
             reads=[B("tri_s")], writes=[B("triS_s")])
        P.op("dve", lambda e: e.tensor_scalar(out=ntri_i[:], in0=tri_i[:], scalar1=-1.0, scalar2=None, op0=ALU.mult),
             reads=[B("tri_i")], writes=[B("ntri_i")])
        epsg = sb("epsg", [64, 1])
        P.op("dve", lambda e: e.memset(epsg[:], RW_GN_EPS), writes=[B("epsg")])
        ident = self.ident
        zr = sb("zr", [64, 24, CMAX + 1]); zl = sb("zl", [64, 2, CMAX + 1]); zg = sb("zg", [128, CMAX + 1])
        dd = sb("dd", [64, 24, CMAX]); zx = sb("zx", [64, 24, CMAX])
        dl = sb("dl", [64, 2, CMAX]); lx = sb("lx", [64, 2, CMAX]); th = sb("th", [64, CMAX])
        dg = sb("dg", [128, CMAX]); sg = sb("sg", [128, CMAX])
        sigw = sb("sigw", [64, 512]); gtm = sb("gtm", [64, 512])
        Einc = sb("Einc", [64, 8, CMAX]); Einv = sb("Einv", [64, 8, CMAX]); Eexc = sb("Eexc", [64, 8, CMAX])
        gamC = sb("gamC", [64, 8])
        av = sb("av", [64, 8, CMAX]); kk0 = sb("kk0", [64, 8, CMAX]); sq = sb("sq", [64, 8, CMAX])
        rn = sb("rn", [64, 8, CMAX]); kk = sb("kk", [64, 8, CMAX]); kp = sb("kp", [64, 8, CMAX])
        bb = sb("bb", [64, 8, CMAX]); tt_ = sb("tt_", [64, 8, CMAX])
        AR = sb("AR", [64, 8, 2, CMAX]); Kt = sb("Kt", [64, 8, CMAX]); Bt = sb("Bt", [64, 8, CMAX])
        rk = sb("rk", [64, 8, CMAX]); rks = sb("rks", [64, 8])
        Vtm = sb("Vtm", [64, 512]); Ktm = sb("Ktm", [64, 512]); nBtm = sb("nBtm", [64, 512])
        Nka = sb("Nka", [64, 8, CMAX]); Mkr = sb("Mkr", [64, 8, CMAX]); nMbr = sb("nMbr", [64, 8, CMAX])
        Am = [sb("Am%d" % i, [64, 8, CMAX]) for i in range(2)]
        At = [sb("At%d" % i, [64, 8, CMAX]) for i in range(2)]
        Pm = sb("Pm", [64, 8, CMAX]); Pt = sb("Pt", [64, 8, CMAX])
        WT = sb("WT", [64, 512]); UT = sb("UT", [64, 512])
        osb = sb("osb", [64, 8, 64]); oc = sb("oc", [64, 8, 64]); sq2 = sb("sq2", [64, 8, 64]); yv = sb("yv", [64, 8, 64])
        st8 = sb("st8", [64, 8, 4])
        ST = sb("ST", [64, 8, 64]); Snat = sb("Snat", [64, 8, 64])
        banks = [self.ps(es, "bk%d" % i, [128, 512]) for i in range(8)]
        bstate = [0]

        def bank():
            i = bstate[0] % 8
            bstate[0] += 1
            return banks[i], B("bank", i)

        def bc3(t2, C, n=64, g=8):
            return t2[:n, :g].unsqueeze(2).to_broadcast([n, g, C])

        seqs = [(0, T, None)] + [(T + s * LS, LS, s) for s in range(NS)]
        for (q0, L, s) in seqs:
            oi = 0 if s is None else 1 + s
            if s is None:
                P.op("dve", lambda e: e.memset(ST[:], 0.0), writes=[B("ST")])
            else:
                P.dma("sp", Snat[:], self.io["state_wkv"][s].rearrange("h v k -> v h k"), writes=[B("Snat")])
                pb, pbb = bank()
                for h in range(8):
                    P.op("pe", lambda e, h=h: e.transpose(out=pb[:64, h * 64:(h + 1) * 64], in_=Snat[:, h, :], identity=ident[:64, :64]),
                         reads=[B("Snat"), B("ident")], writes=[pbb])
                P.op("dve", lambda e: e.tensor_copy(out=ST[:].rearrange("n h v -> n (h v)"), in_=pb[:64, :]), reads=[pbb], writes=[B("ST")])
            for c0 in range(0, L, CMAX):
                C = min(CMAX, L - c0)
                a = q0 + c0
                NIT = max(int(np.ceil(np.log2(C))) - 1, 0)
                first = (c0 == 0)
                rsrc = zT[R0:R0 + 1536, :].rearrange("(g n) t -> n g t", n=64)
                lsrc = zT[R0 + 1536:R0 + 1664, :].rearrange("(g n) t -> n g t", n=64)
                gsrc = zT[R0 + 1664:R0 + 1792, :]
                if first:
                    P.dma("sp", zr[:, :, 1:C + 1], rsrc[:, :, a:a + C], writes=[B("zr")])
                    P.dma("sp", zl[:, :, 1:C + 1], lsrc[:, :, a:a + C], writes=[B("zl")])
                    P.dma("sp", zg[:, 1:C + 1], gsrc[:, a:a + C], writes=[B("zg")])
                    if s is None:
                        P.op("pool", lambda e: e.memset(zr[:, :, 0:1], 0.0), writes=[B("zr")])
                        P.op("pool", lambda e: e.memset(zl[:, :, 0:1], 0.0), writes=[B("zl")])
                        P.op("pool", lambda e: e.memset(zg[:, 0:1], 0.0), writes=[B("zg")])
                    else:
                        ss_ = self.io["state_shift"][s]
                        P.dma("sp", zr[:, :, 0], ss_[0:1536].rearrange("(g n) -> n g", n=64), writes=[B("zr")], **NC)
                        P.dma("sp", zl[:, :, 0], ss_[1536:1664].rearrange("(g n) -> n g", n=64), writes=[B("zl")], **NC)
                        P.dma("sp", zg[:, 0:1], ss_[1664:1792].rearrange("(n o) -> n o", o=1), writes=[B("zg")], **NC)
                else:
                    P.dma("sp", zr[:, :, 0:C + 1], rsrc[:, :, a - 1:a + C], writes=[B("zr")])
                    P.dma("sp", zl[:, :, 0:C + 1], lsrc[:, :, a - 1:a + C], writes=[B("zl")])
                    P.dma("sp", zg[:, 0:C + 1], gsrc[:, a - 1:a + C], writes=[B("zg")])
                P.op("dve", lambda e: e.tensor_tensor(out=dd[:, :, :C], in0=zr[:, :, 0:C], in1=zr[:, :, 1:C + 1], op=ALU.subtract),
                     reads=[B("zr")], writes=[B("dd")])
                P.op("pool", lambda e: e.tensor_tensor(out=dd[:, :, :C], in0=dd[:, :, :C], in1=bc3(mu_rkv, C, 64, 24), op=ALU.mult),
                     reads=[B("dd"), B("mu_rkv")], writes=[B("dd")])
                P.op("dve", lambda e: e.tensor_tensor(out=zx[:, :, :C], in0=dd[:, :, :C], in1=zr[:, :, 1:C + 1], op=ALU.add),
                     reads=[B("dd"), B("zr")], writes=[B("zx")])
                rx, kx, vx = zx[:, 0:8, :C], zx[:, 8:16, :C], zx[:, 16:24, :C]
                P.op("pool", lambda e: e.tensor_tensor(out=dl[:, :, :C], in0=zl[:, :, 0:C], in1=zl[:, :, 1:C + 1], op=ALU.subtract),
                     reads=[B("zl")], writes=[B("dl")])
                P.op("pool", lambda e: e.tensor_tensor(out=dl[:, :, :C], in0=dl[:, :, :C], in1=bc3(mu_l, C, 64, 2), op=ALU.mult),
                     reads=[B("dl"), B("mu_l")], writes=[B("dl")])
                P.op("pool", lambda e: e.tensor_tensor(out=lx[:, :, :C], in0=dl[:, :, :C], in1=zl[:, :, 1:C + 1], op=ALU.add),
                     reads=[B("dl"), B("zl")], writes=[B("lx")])
                P.op("pool", lambda e: e.tensor_tensor(out=dg[:, :C], in0=zg[:, 0:C], in1=zg[:, 1:C + 1], op=ALU.subtract),
                     reads=[B("zg")], writes=[B("dg")])
                P.op("dve", lambda e: e.scalar_tensor_tensor(out=dg[:, :C], in0=dg[:, :C], scalar=mu_g[:, 0:1], in1=zg[:, 1:C + 1],
                                                             op0=ALU.mult, op1=ALU.add),
                     reads=[B("dg"), B("mu_g"), B("zg")], writes=[B("dg")])
                P.op("act", lambda e: e.activation(out=th[:, :C], in_=lx[:, 0, :C], func=AF.Tanh), reads=[B("lx")], writes=[B("th")])
                P.op("act", lambda e: e.activation(out=sg[:, :C], in_=dg[:, :C], func=AF.Sigmoid), reads=[B("dg")], writes=[B("sg")])
                pw, pwb = bank()
                P.op("pe", lambda e: e.matmul(pw[:C, :], lhsT=th[:, :C], rhs=w2[:, :], start=True, stop=False),
                     reads=[B("th"), B("w2")], writes=[pwb])
                P.op("pe", lambda e: e.matmul(pw[:C, :], lhsT=ones[0:1, :C], rhs=w0row[0:1, :], start=False, stop=True),
                     reads=[B("ones64"), B("w0row")], writes=[pwb])
                P.op("act", lambda e: e.activation(out=sigw[:C, :], in_=pw[:C, :], func=AF.Sigmoid), reads=[pwb], writes=[B("sigw")])
                pcl, pclb = bank()
                pce, pceb = bank()
                for h in range(8):
                    P.op("pe", lambda e, h=h: e.matmul(pcl[:64, h * C:(h + 1) * C], lhsT=sigw[:C, h * 64:(h + 1) * 64], rhs=triS_i[:C, :C],
                                                       start=True, stop=True), reads=[B("sigw"), B("triS_i")], writes=[pclb])
                    P.op("pe", lambda e, h=h: e.matmul(pce[:64, h * C:(h + 1) * C], lhsT=sigw[:C, h * 64:(h + 1) * 64], rhs=triS_s[:C, :C],
                                                       start=True, stop=True), reads=[B("sigw"), B("triS_s")], writes=[pceb])
                v3 = lambda ap: ap.rearrange("n (h c) -> n h c", h=8)
                P.op("act", lambda e: e.activation(out=Einc[:, :, :C], in_=v3(pcl[:64, :8 * C]), func=AF.Exp), reads=[pclb], writes=[B("Einc")])
                P.op("act", lambda e: e.activation(out=Einv[:, :, :C], in_=v3(pcl[:64, :8 * C]), func=AF.Exp, scale=-1.0), reads=[pclb], writes=[B("Einv")])
                P.op("act", lambda e: e.activation(out=Eexc[:, :, :C], in_=v3(pce[:64, :8 * C]), func=AF.Exp), reads=[pceb], writes=[B("Eexc")])
                P.op("pool", lambda e: e.tensor_copy(out=gamC[:, :], in_=Einc[:, :, C - 1]), reads=[B("Einc")], writes=[B("gamC")])
                pa, pab = bank()
                for h in range(8):
                    P.op("pe", lambda e, h=h: e.matmul(pa[:64, h * C:(h + 1) * C], lhsT=a2[:, h * 64:(h + 1) * 64], rhs=lx[:, 1, :C],
                                                       start=True, stop=True), reads=[B("a2"), B("lx")], writes=[pab])
                P.op("dve", lambda e: e.tensor_tensor(out=av[:, :, :C], in0=v3(pa[:64, :8 * C]), in1=bc3(a0, C), op=ALU.add),
                     reads=[pab, B("a0")], writes=[B("av")])
                P.op("act", lambda e: e.activation(out=av[:, :, :C], in_=av[:, :, :C], func=AF.Sigmoid), reads=[B("av")], writes=[B("av")])
                pg_, pgb = bank()
                P.op("pe", lambda e: e.matmul(pg_[:C, :], lhsT=sg[:, :C], rhs=g2[:, :], start=True, stop=True),
                     reads=[B("sg"), B("g2")], writes=[pgb])
                P.op("act", lambda e: e.copy(out=gtm[:C, :], in_=pg_[:C, :]), reads=[pgb], writes=[B("gtm")])
                P.op("pool", lambda e: e.tensor_tensor(out=kk0[:, :, :C], in0=kx, in1=bc3(k_k, C), op=ALU.mult),
                     reads=[B("zx"), B("k_k")], writes=[B("kk0")])
                P.op("pool", lambda e: e.tensor_tensor(out=sq[:, :, :C], in0=kk0[:, :, :C], in1=kk0[:, :, :C], op=ALU.mult),
                     reads=[B("kk0")], writes=[B("sq")])
                pss, pssb = bank()
                for h in range(8):
                    P.op("pe", lambda e, h=h: e.matmul(pss[:64, h * C:(h + 1) * C], lhsT=ones[:, :], rhs=sq[:, h, :C], start=True, stop=True),
                         reads=[B("ones64"), B("sq")], writes=[pssb])
                P.op("act", lambda e: e.activation(out=rn[:, :, :C], in_=v3(pss[:64, :8 * C]), func=AF.Sqrt), reads=[pssb], writes=[B("rn")])
                P.op("dve", lambda e: e.tensor_scalar(out=rn[:, :, :C], in0=rn[:, :, :C], scalar1=1e-12, scalar2=None, op0=ALU.max),
                     reads=[B("rn")], writes=[B("rn")])
                P.op("dve", lambda e: e.reciprocal(out=rn[:, :, :C], in_=rn[:, :, :C]), reads=[B("rn")], writes=[B("rn")])
                P.op("dve", lambda e: e.tensor_tensor(out=kk[:, :, :C], in0=kk0[:, :, :C], in1=rn[:, :, :C], op=ALU.mult),
                     reads=[B("kk0"), B("rn")], writes=[B("kk")])
                P.op("dve", lambda e: e.scalar_tensor_tensor(out=tt_[:, :, :C], in0=av[:, :, :C], scalar=-1.0, in1=bc3(k_a, C),
                                                             op0=ALU.add, op1=ALU.mult), reads=[B("av"), B("k_a")], writes=[B("tt_")])
                P.op("dve", lambda e: e.scalar_tensor_tensor(out=kp[:, :, :C], in0=tt_[:, :, :C], scalar=1.0, in1=kx,
                                                             op0=ALU.add, op1=ALU.mult), reads=[B("tt_"), B("zx")], writes=[B("kp")])
                P.op("pool", lambda e: e.tensor_tensor(out=bb[:, :, :C], in0=kk[:, :, :C], in1=av[:, :, :C], op=ALU.mult),
                     reads=[B("kk"), B("av")], writes=[B("bb")])
                P.op("pool", lambda e: e.tensor_tensor(out=AR[:, :, 0, :C], in0=kk[:, :, :C], in1=Eexc[:, :, :C], op=ALU.mult),
                     reads=[B("kk"), B("Eexc")], writes=[B("AR")])
                P.op("dve", lambda e: e.tensor_tensor(out=AR[:, :, 1, :C], in0=rx, in1=Einc[:, :, :C], op=ALU.mult),
                     reads=[B("zx"), B("Einc")], writes=[B("AR")])
                P.op("dve", lambda e: e.tensor_tensor(out=Kt[:, :, :C], in0=kp[:, :, :C], in1=Einv[:, :, :C], op=ALU.mult),
                     reads=[B("kp"), B("Einv")], writes=[B("Kt")])
                P.op("pool", lambda e: e.tensor_tensor(out=Bt[:, :, :C], in0=bb[:, :, :C], in1=Einv[:, :, :C], op=ALU.mult),
                     reads=[B("bb"), B("Einv")], writes=[B("Bt")])
                P.op("pool", lambda e: e.tensor_tensor(out=rk[:, :, :C], in0=rx, in1=kp[:, :, :C], op=ALU.mult),
                     reads=[B("zx"), B("kp")], writes=[B("rk")])
                P.op("pool", lambda e: e.tensor_tensor(out=rk[:, :, :C], in0=rk[:, :, :C], in1=bc3(r_k, C), op=ALU.mult),
                     reads=[B("rk"), B("r_k")], writes=[B("rk")])
                prk, prkb = bank()
                for h in range(8):
                    P.op("pe", lambda e, h=h: e.matmul(prk[:C, h:h + 1], lhsT=rk[:, h, :C], rhs=ones[:, 0:1], start=True, stop=True),
                         reads=[B("rk"), B("ones64")], writes=[prkb])
                P.op("act", lambda e: e.copy(out=rks[:C, :], in_=prk[:C, 0:8]), reads=[prkb], writes=[B("rks")])
                for (src_t, srcb, dst, dstb, scl) in ((vx, B("zx"), Vtm, B("Vtm"), 1.0), (Kt[:, :, :C], B("Kt"), Ktm, B("Ktm"), 1.0),
                                                      (Bt[:, :, :C], B("Bt"), nBtm, B("nBtm"), -1.0)):
                    pt, ptb = bank()
                    for h in range(8):
                        P.op("pe", lambda e, h=h: e.transpose(out=pt[:C, h * 64:(h + 1) * 64], in_=src_t[:, h, :], identity=ident[:64, :64]),
                             reads=[srcb, B("ident")], writes=[ptb])
                    P.op("act", lambda e: e.mul(out=dst[:C, :], in_=pt[:C, :], mul=scl), reads=[ptb], writes=[dstb])
                for (lh, lhb, dN, dNb, mN, dM, dMb, mM) in ((Kt, B("Kt"), Nka, B("Nka"), tri_s, Mkr, B("Mkr"), tri_i),
                                                           (Bt, B("Bt"), Am[0], B("Am", 0), tri_s, nMbr, B("nMbr"), ntri_i)):
                    p0, p0b = bank()
                    p1, p1b = bank()
                    for h in range(8):
                        pp, ppb = (p0, p0b) if h < 4 else (p1, p1b)
                        hh = h % 4
                        P.op("pe", lambda e, h=h, pp=pp, hh=hh: e.matmul(pp[:C, hh * 2 * C:(hh + 1) * 2 * C].rearrange("i (s c) -> i s c", s=2),
                                                                         lhsT=lh[:, h, :C], rhs=AR[:, h, :, :C], start=True, stop=True),
                             reads=[lhb, B("AR")], writes=[ppb])
                    for half, (pp, ppb) in enumerate(((p0, p0b), (p1, p1b))):
                        v4 = pp[:C, :8 * C].rearrange("i (h s c) -> i h s c", h=4, s=2)
                        hs = slice(half * 4, half * 4 + 4)
                        P.op("dve", lambda e, v4=v4, hs=hs: e.tensor_tensor(out=dN[:C, hs, :C], in0=v4[:, :, 0, :],
                                                                             in1=mN[:C, :C].unsqueeze(1).to_broadcast([C, 4, C]), op=ALU.mult),
                             reads=[ppb, B("tri_s")], writes=[dNb])
                        P.op("dve", lambda e, v4=v4, hs=hs: e.tensor_tensor(out=dM[:C, hs, :C], in0=v4[:, :, 1, :],
                                                                             in1=mM[:C, :C].unsqueeze(1).to_broadcast([C, 4, C]), op=ALU.mult),
                             reads=[ppb, B("tri_i"), B("ntri_i")], writes=[dMb])
                pat, patb = bank()
                for h in range(8):
                    P.op("pe", lambda e, h=h: e.matmul(pat[:C, h * C:(h + 1) * C], lhsT=AR[:, h, 0, :C], rhs=Bt[:, h, :C], start=True, stop=True),
                         reads=[B("AR"), B("Bt")], writes=[patb])
                vh = lambda ap: ap.rearrange("i (h c) -> i h c", h=8)
                mb8 = lambda m_: m_[:C, :C].unsqueeze(1).to_broadcast([C, 8, C])
                P.op("dve", lambda e: e.tensor_tensor(out=At[0][:C, :, :C], in0=vh(pat[:C, :8 * C]), in1=mb8(low_s), op=ALU.mult),
                     reads=[patb, B("low_s")], writes=[B("At", 0)])
                P.op("pool", lambda e: e.tensor_tensor(out=Pm[:C, :, :C], in0=mb8(ident), in1=Am[0][:C, :, :C], op=ALU.subtract),
                     reads=[B("ident"), B("Am", 0)], writes=[B("Pm")])
                P.op("pool", lambda e: e.tensor_tensor(out=Pt[:C, :, :C], in0=mb8(ident), in1=At[0][:C, :, :C], op=ALU.subtract),
                     reads=[B("ident"), B("At", 0)], writes=[B("Pt")])
                cur = 0
                for it in range(NIT):
                    last = (it == NIT - 1)
                    nxt = 1 - cur
                    pA, pAb = bank()
                    for h in range(8):
                        P.op("pe", lambda e, h=h: e.matmul(pA[:C, h * C:(h + 1) * C], lhsT=At[cur][:C, h, :C], rhs=Am[cur][:C, h, :C],
                                                           start=True, stop=True), reads=[B("At", cur), B("Am", cur)], writes=[pAb])
                    P.op("act", lambda e: e.copy(out=Am[nxt][:C, :, :C], in_=vh(pA[:C, :8 * C])), reads=[pAb], writes=[B("Am", nxt)])
                    if not last:
                        pB, pBb = bank()
                        for h in range(8):
                            P.op("pe", lambda e, h=h: e.matmul(pB[:C, h * C:(h + 1) * C], lhsT=Am[cur][:C, h, :C], rhs=At[cur][:C, h, :C],
                                                               start=True, stop=True), reads=[B("At", cur), B("Am", cur)], writes=[pBb])
                        P.op("act", lambda e: e.copy(out=At[nxt][:C, :, :C], in_=vh(pB[:C, :8 * C])), reads=[pBb], writes=[B("At", nxt)])
                    pP, pPb = bank()
                    for h in range(8):
                        P.op("pe", lambda e, h=h: e.matmul(pP[:C, h * C:(h + 1) * C], lhsT=Pt[:C, h, :C], rhs=Am[nxt][:C, h, :C],
                                                           start=True, stop=True), reads=[B("Pt"), B("Am", nxt)], writes=[pPb])
                    if not last:
                        pQ, pQb = bank()
                        for h in range(8):
                            P.op("pe", lambda e, h=h: e.matmul(pQ[:C, h * C:(h + 1) * C], lhsT=Am[nxt][:C, h, :C], rhs=Pt[:C, h, :C],
                                                               start=True, stop=True), reads=[B("Pt"), B("Am", nxt)], writes=[pQb])
                    P.op("dve", lambda e: e.tensor_tensor(out=Pm[:C, :, :C], in0=Pm[:C, :, :C], in1=vh(pP[:C, :8 * C]), op=ALU.add),
                         reads=[B("Pm"), pPb], writes=[B("Pm")])
                    if not last:
                        P.op("dve", lambda e: e.tensor_tensor(out=Pt[:C, :, :C], in0=Pt[:C, :, :C], in1=vh(pQ[:C, :8 * C]), op=ALU.add),
                             reads=[B("Pt"), pQb], writes=[B("Pt")])
                    cur = nxt
                hv = lambda t, h: t[:C, h * 64:(h + 1) * 64]
                pW, pWb = bank()
                for h in range(8):
                    P.op("pe", lambda e, h=h: e.matmul(hv(pW, h), lhsT=AR[:, h, 0, :C], rhs=ST[:, h, :], start=True, stop=False),
                         reads=[B("AR"), B("ST")], writes=[pWb])
                    P.op("pe", lambda e, h=h: e.matmul(hv(pW, h), lhsT=Nka[:C, h, :C], rhs=hv(Vtm, h), start=False, stop=True),
                         reads=[B("Nka"), B("Vtm")], writes=[pWb])
                P.op("act", lambda e: e.copy(out=WT[:C, :], in_=pW[:C, :]), reads=[pWb], writes=[B("WT")])
                pU, pUb = bank()
                for h in range(8):
                    P.op("pe", lambda e, h=h: e.matmul(hv(pU, h), lhsT=Pm[:C, h, :C], rhs=hv(WT, h), start=True, stop=True),
                         reads=[B("Pm"), B("WT")], writes=[pUb])
                P.op("act", lambda e: e.copy(out=UT[:C, :], in_=pU[:C, :]), reads=[pUb], writes=[B("UT")])
                pO, pOb = bank()
                for h in range(8):
                    P.op("pe", lambda e, h=h: e.matmul(hv(pO, h), lhsT=AR[:, h, 1, :C], rhs=ST[:, h, :], start=True, stop=False),
                         reads=[B("AR"), B("ST")], writes=[pOb])
                    P.op("pe", lambda e, h=h: e.matmul(hv(pO, h), lhsT=Mkr[:C, h, :C], rhs=hv(Vtm, h), start=False, stop=False),
                         reads=[B("Mkr"), B("Vtm")], writes=[pOb])
                    P.op("pe", lambda e, h=h: e.matmul(hv(pO, h), lhsT=nMbr[:C, h, :C], rhs=hv(UT, h), start=False, stop=True),
                         reads=[B("nMbr"), B("UT")], writes=[pOb])
                P.op("act", lambda e: e.copy(out=osb[:C].rearrange("t h v -> t (h v)"), in_=pO[:C, :]), reads=[pOb], writes=[B("osb")])
                pS, pSb = bank()
                for h in range(8):
                    P.op("pe", lambda e, h=h: e.matmul(pS[:64, h * 64:(h + 1) * 64], lhsT=hv(Ktm, h), rhs=hv(Vtm, h), start=True, stop=False),
                         reads=[B("Ktm"), B("Vtm")], writes=[pSb])
                    P.op("pe", lambda e, h=h: e.matmul(pS[:64, h * 64:(h + 1) * 64], lhsT=hv(nBtm, h), rhs=hv(UT, h), start=False, stop=True),
                         reads=[B("nBtm"), B("UT")], writes=[pSb])
                P.op("dve", lambda e: e.tensor_tensor(out=ST[:], in0=ST[:], in1=pS[:64, :].rearrange("n (h v) -> n h v", h=8), op=ALU.add),
                     reads=[B("ST"), pSb], writes=[B("ST")])
                P.op("dve", lambda e: e.tensor_tensor(out=ST[:], in0=ST[:], in1=bc3(gamC, 64), op=ALU.mult),
                     reads=[B("ST"), B("gamC")], writes=[B("ST")])
                b8 = lambda col, w=64: st8[:C, :, col:col + 1].to_broadcast([C, 8, w])
                P.op("dve", lambda e: e.tensor_reduce(out=st8[:C, :, 0], in_=osb[:C], axis=AX.X, op=ALU.add), reads=[B("osb")], writes=[B("st8")])
                P.op("dve", lambda e: e.tensor_scalar(out=st8[:C, :, 0], in0=st8[:C, :, 0], scalar1=-1.0 / 64, scalar2=None, op0=ALU.mult),
                     reads=[B("st8")], writes=[B("st8")])
                P.op("dve", lambda e: e.tensor_tensor(out=oc[:C], in0=osb[:C], in1=b8(0), op=ALU.add), reads=[B("osb"), B("st8")], writes=[B("oc")])
                P.op("pool", lambda e: e.tensor_tensor(out=sq2[:C], in0=oc[:C], in1=oc[:C], op=ALU.mult), reads=[B("oc")], writes=[B("sq2")])
                P.op("dve", lambda e: e.tensor_reduce(out=st8[:C, :, 1], in_=sq2[:C], axis=AX.X, op=ALU.add), reads=[B("sq2")], writes=[B("st8")])
                P.op("act", lambda e: e.activation(out=st8[:C, :, 2], in_=st8[:C, :, 1], func=AF.Sqrt, bias=epsg[:C, 0:1], scale=1.0 / 64),
                     reads=[B("st8"), B("epsg")], writes=[B("st8")])
                P.op("dve", lambda e: e.reciprocal(out=st8[:C, :, 3], in_=st8[:C, :, 2]), reads=[B("st8")], writes=[B("st8")])
                P.op("dve", lambda e: e.tensor_tensor(out=yv[:C], in0=oc[:C], in1=b8(3), op=ALU.mult), reads=[B("oc"), B("st8")], writes=[B("yv")])
                f3 = lambda t: t[:C, :].rearrange("t (h v) -> t h v", h=8)
                P.op("pool", lambda e: e.tensor_tensor(out=yv[:C], in0=yv[:C], in1=f3(lnw), op=ALU.mult), reads=[B("yv"), B("lnw")], writes=[B("yv")])
                P.op("pool", lambda e: e.tensor_tensor(out=yv[:C], in0=yv[:C], in1=f3(lnb), op=ALU.add), reads=[B("yv"), B("lnb")], writes=[B("yv")])
                P.op("dve", lambda e: e.tensor_tensor(out=sq2[:C], in0=f3(Vtm), in1=rks[:C, :].unsqueeze(2).to_broadcast([C, 8, 64]), op=ALU.mult),
                     reads=[B("Vtm"), B("rks")], writes=[B("sq2")])
                P.op("dve", lambda e: e.tensor_tensor(out=yv[:C], in0=yv[:C], in1=sq2[:C], op=ALU.add), reads=[B("yv"), B("sq2")], writes=[B("yv")])
                P.op("dve", lambda e: e.tensor_tensor(out=yv[:C], in0=yv[:C], in1=f3(gtm), op=ALU.mult), reads=[B("yv"), B("gtm")], writes=[B("yv")])
                P.dma("pool", self.io["mixB0"][a:a + C, :], yv[:C].rearrange("t h v -> t (h v)"), reads=[B("yv")], writes=[B("mixB0", a)])
            pb, pbb = bank()
            for h in range(8):
                P.op("pe", lambda e, h=h: e.transpose(out=pb[:64, h * 64:(h + 1) * 64], in_=ST[:, h, :], identity=ident[:64, :64]),
                     reads=[B("ST"), B("ident")], writes=[pbb])
            P.op("act", lambda e: e.copy(out=Snat[:].rearrange("v h k -> v (h k)"), in_=pb[:64, :]), reads=[pbb], writes=[B("Snat")])
            P.dma("pool", self.io["new_wkv"][oi].rearrange("h v k -> v h k"), Snat[:], reads=[B("Snat")], writes=[B("new_wkv", oi)])
            tl = q0 + L - 1
            P.dma("pool", self.io["new_shift"][oi].rearrange("(c p) -> p c", p=128),
                  zT[R0:R0 + RWS, tl:tl + 1].rearrange("(c p) o -> p (c o)", p=128), writes=[B("new_shift", oi)], **NC)
        P.barrier()


Model.phase_rwkv = _rwkv_phase


def _fox_phase(self):
    P, T, LS, NS, NPG = self.P, self.T, self.LS, self.NS, self.NPG
    TS, TT = self.TS, self.TT
    zT = self.io["zT1"]
    NB = T // 128
    NC = dict(allow_slow_non_contiguous=True)
    B = P.buf
    with ExitStack() as es:
        sb = lambda name, shape, dt=F32: self.sb(es, name, shape, dt)

        def ld(name, shape, src, dt=F32, q="sp", **kw):
            t = sb(name, shape, dt)
            P.dma(q, t[:], src, writes=[B(name)], **kw)
            return t
        ident, identb = self.ident, self.identb
        tri_i = ld("f_tri_i", [128, 128], self.io["c_triu_incl"])
        tri_s = ld("f_tri_s", [128, 128], self.io["c_triu_strict"])
        ones = ld("f_ones", [128, 128], self.io["c_ones"])
        sel = ld("f_sel", [128, 128], self.io["c_sel_last"])
        blktri = ld("f_blktri", [128, 128], self.io["c_blktri"])
        negf = ld("f_negf", [128, 128], self.io["c_negmask"])
        negb = ld("f_negb", [128, 128], self.io["c_negmask"], BF16, q="pool")
        fb = ld("f_fb", [128, 8], self.io["fox_f_bias"].partition_broadcast(128))
        iota_i = ld("f_iota", [128, 1], self.io["c_iota"], I32)
        iota_f = sb("f_iotaf", [128, 1])
        P.op("dve", lambda e: e.tensor_copy(out=iota_f[:], in_=iota_i[:]), reads=[B("f_iota")], writes=[B("f_iotaf")])
        banks = [self.ps(es, "fbk%d" % i, [128, 512]) for i in range(5)]
        bankb = [self.ps(es, "fbkb%d" % i, [128, 1024], BF16) for i in range(1)]
        bstate = [0]

        def bank():
            i = bstate[0] % 5
            bstate[0] += 1
            return banks[i], B("fbank", i)

        LF = sb("LF", [128, max(NB, 1), 8])
        lft = sb("lft", [128, 8]); lfs = sb("lfs", [128, 8])
        for (tt, n) in self.token_tiles(128):
            P.dma("sp", lft[:n, :], self.io["fpre"][tt:tt + n, :], writes=[B("lft")])
            P.op("dve", lambda e: e.tensor_tensor(out=lft[:n, :], in0=lft[:n, :], in1=fb[:n, :], op=ALU.add), reads=[B("lft"), B("f_fb")], writes=[B("lft")])
            P.op("act", lambda e: e.activation(out=lft[:n, :], in_=lft[:n, :], func=AF.Sigmoid), reads=[B("lft")], writes=[B("lft")])
            dst = LF[:n, tt // 128, :] if tt < T else lfs[:n, :]
            dstb = B("LF") if tt < T else B("lfs")
            P.op("act", lambda e: e.activation(out=dst, in_=lft[:n, :], func=AF.Ln), reads=[B("lft")], writes=[dstb])
            P.dma("pool", self.io["new_logf"][tt:tt + n, :], dst, reads=[dstb], writes=[B("new_logf", tt)])

        tot = sb("tot", [128, 8]); excl = sb("excl", [128, 8]); rhs3 = sb("rhs3", [128, 128, 8])

        def cumsum2(src, srcb, dst, dstb, nb):
            cols = nb * 8
            pt_, ptb_ = bank()
            for h in range(8):
                P.op("pe", lambda e, h=h: e.matmul(pt_[:nb, h:h + 1], lhsT=src[:, :nb, h], rhs=ones[:, 0:1], start=True, stop=True),
                     reads=[srcb, B("f_ones")], writes=[ptb_])
            P.op("act", lambda e: e.copy(out=tot[:nb, :], in_=pt_[:nb, 0:8]), reads=[ptb_], writes=[B("tot")])
            pe_, peb_ = bank()
            P.op("pe", lambda e: e.matmul(pe_[:nb, 0:8], lhsT=tri_s[:nb, :nb], rhs=tot[:nb, :], start=True, stop=True),
                 reads=[B("tot"), B("f_tri_s")], writes=[peb_])
            P.op("act", lambda e: e.copy(out=excl[:nb, :], in_=pe_[:nb, 0:8]), reads=[peb_], writes=[B("excl")])
            P.op("dve", lambda e: e.tensor_tensor(out=rhs3[:nb, :nb, :], in0=ident[:nb, :nb].unsqueeze(2).to_broadcast([nb, nb, 8]),
                                                  in1=excl[:nb, :].unsqueeze(1).to_broadcast([nb, nb, 8]), op=ALU.mult),
                 reads=[B("ident"), B("excl")], writes=[B("rhs3")])
            for c0 in range(0, cols, 512):
                cn = min(512, cols - c0)
                b0_, b1_ = c0 // 8, (c0 + cn) // 8
                pw_, pwb_ = bank()
                P.op("pe", lambda e: e.matmul(pw_[:, :cn], lhsT=tri_i[:, :], rhs=src[:, b0_:b1_, :], start=True, stop=False),
                     reads=[srcb, B("f_tri_i")], writes=[pwb_])
                P.op("pe", lambda e: e.matmul(pw_[:, :cn], lhsT=ones[:nb, :], rhs=rhs3[:nb, b0_:b1_, :], start=False, stop=True),
                     reads=[B("rhs3"), B("f_ones")], writes=[pwb_])
                P.op("dve", lambda e: e.tensor_copy(out=dst[:, b0_:b1_, :], in_=pw_[:, :cn].rearrange("p (b h) -> p b h", h=8)),
                     reads=[pwb_], writes=[dstb])

        Ctm = sb("Ctm", [128, NB, 8])
        cumsum2(LF, B("LF"), Ctm, B("Ctm"), NB)
        groups = [(b0, min(b0 + 4, NB)) for b0 in range(0, NB, 4)]
        G = len(groups)
        crefb = sb("crefb", [128, G, 8])
        pc_, pcb_ = bank()
        for g, (b0, b1) in enumerate(groups):
            P.op("pe", lambda e, g=g, b1=b1: e.matmul(pc_[:, g * 8:(g + 1) * 8], lhsT=sel[:, :], rhs=Ctm[:, b1 - 1, :], start=True, stop=True),
                 reads=[B("Ctm"), B("f_sel")], writes=[pcb_])
        P.op("dve", lambda e: e.tensor_copy(out=crefb[:].rearrange("p g h -> p (g h)"), in_=pc_[:, :G * 8]), reads=[pcb_], writes=[B("crefb")])

        QTa = [sb("QTa%d" % i, [65, T], BF16) for i in range(2)]
        KTa = [sb("KTa%d" % i, [65, T], BF16) for i in range(2)]
        Vau = [sb("Vau%d" % i, [128, NB, 65], BF16) for i in range(2)]
        cst = [sb("cst%d" % i, [128, 65]) for i in range(4)]
        biasK = [sb("biasK%d" % i, [128, NB]) for i in range(2)]
        PT = [sb("PT%d" % i, [128, 512], BF16) for i in range(3)]
        rec = sb("rec", [128, 4, 1]); yf = [sb("yf%d" % i, [128, 4, 64]) for i in range(2)]
        for i in range(2):
            P.op("dve", lambda e, i=i: e.memset(KTa[i][64:65, :], 1.0), writes=[B("KTa", i)])
            P.op("pool", lambda e, i=i: e.memset(Vau[i][:, :, 64:65], 1.0), writes=[B("Vau", i)])
        for i in range(4):
            P.op("pool", lambda e, i=i: e.memset(cst[i][:], 0.0), writes=[B("cst", i)])
        pacc = [self.ps(es, "pacc%d" % i, [128, 4, 65]) for i in range(1)]
        ci = 0
        pti = 0
        for h in range(8):
            hi = h % 2
            qa, ka, va = QTa[hi], KTa[hi], Vau[hi]
            P.dma("pool", qa[0:64, :], zT[h * 64:(h + 1) * 64, 0:T], writes=[B("QTa", hi)])
            P.dma("pool", ka[0:64, :], zT[512 + h * 64:512 + (h + 1) * 64, 0:T], writes=[B("KTa", hi)])
            P.dma("pool", va[:, :, 0:64], self.io["new_v"][0:T, h * 64:(h + 1) * 64].rearrange("(b p) d -> p b d", p=128),
                  reads=[B("new_v", tt_) for (tt_, _) in self.token_tiles(128) if tt_ < T], writes=[B("Vau", hi)])
            for g, (b0, b1) in enumerate(groups):
                pq, pqb = bank()
                for b in range(b0, b1):
                    ct, ctb = cst[ci % 4], B("cst", ci % 4)
                    ci += 1
                    P.op("dve", lambda e, b=b, ct=ct: e.tensor_scalar(out=ct[:, 64:65], in0=Ctm[:, b, h:h + 1], scalar1=crefb[:, g, h:h + 1],
                                                                      scalar2=8.0, op0=ALU.subtract, op1=ALU.mult),
                         reads=[B("Ctm"), B("crefb")], writes=[ctb])
                    P.op("pe", lambda e, b=b, ct=ct: e.matmul(pq[0:65, (b - b0) * 128:(b - b0 + 1) * 128], lhsT=ct[:, :], rhs=ident[:, :],
                                                              start=True, stop=True), reads=[ctb, B("ident")], writes=[pqb])
                nq = (b1 - b0) * 128
                P.op("act", lambda e: e.copy(out=qa[64:65, b0 * 128:b1 * 128], in_=pq[64:65, :nq]), reads=[pqb], writes=[B("QTa", hi)])
            for g, (b0, b1) in enumerate(groups):
                nq = (b1 - b0) * 128
                nblk = b1 - b0
                bk, bkb = biasK[g % 2], B("biasK", g % 2)
                P.op("dve", lambda e: e.tensor_scalar(out=bk[:, :b1], in0=Ctm[:, :b1, h], scalar1=crefb[:, g, h:h + 1], scalar2=-1.0,
                                                      op0=ALU.subtract, op1=ALU.mult), reads=[B("Ctm"), B("crefb")], writes=[bkb])
                pa = pacc[0]
                pab = B("pacc", 0)
                first = True
                for kb in range(b1):
                    n0 = max(kb - b0, 0) * 128
                    ps_, psb_ = bank()
                    diag = kb >= b0
                    P.op("pe", lambda e: e.matmul(ps_[:, n0:nq], lhsT=ka[:, kb * 128:(kb + 1) * 128], rhs=qa[:, b0 * 128 + n0:b1 * 128],
                                                  start=True, stop=not diag), reads=[B("KTa", hi), B("QTa", hi)], writes=[psb_])
                    if diag:
                        P.op("pe", lambda e: e.matmul(ps_[:, n0:n0 + 128], lhsT=identb[:, :], rhs=negb[:, :], start=False, stop=True),
                             reads=[B("identb"), B("f_negb")], writes=[psb_])
                    pt, ptb = PT[pti % 3], B("PT", pti % 3)
                    pti += 1
                    P.op("act", lambda e: e.activation(out=pt[:, n0:nq], in_=ps_[:, n0:nq], func=AF.Exp, bias=bk[:, kb:kb + 1], scale=0.125),
                         reads=[psb_, bkb], writes=[ptb])
                    for j in range(n0 // 128, nblk):
                        P.op("pe", lambda e, j=j: e.matmul(pa[:, j, :], lhsT=pt[:, j * 128:(j + 1) * 128], rhs=va[:, kb, :],
                                                           start=first, stop=(kb == b0 + j)), reads=[ptb, B("Vau", hi)], writes=[pab])
                        first = False
                y_, yb_ = yf[g % 2], B("yf", g % 2)
                P.op("dve", lambda e: e.reciprocal(out=rec[:, :nblk, :], in_=pa[:, :nblk, 64:65]), reads=[pab], writes=[B("rec")])
                P.op("dve", lambda e: e.tensor_tensor(out=y_[:, :nblk, :], in0=pa[:, :nblk, 0:64], in1=rec[:, :nblk, :].to_broadcast([128, nblk, 64]),
                                                      op=ALU.mult), reads=[pab, B("rec")], writes=[yb_])
                P.dma("sp", self.io["mixC1"][b0 * 128:b1 * 128, h * 64:(h + 1) * 64].rearrange("(j p) d -> p j d", p=128), y_[:, :nblk, :],
                      reads=[yb_], writes=[B("mixC1", g, h)])

        PTc = sb("PTc", [128, 1], I32); PTb = sb("PTb", [128, NPG], I32); IDXf = sb("IDXf", [128, NPG]); IDX = sb("IDX", [128, NPG], I32)
        LFp = sb("LFp", [128, 128, 8]); LFs2 = sb("LFs2", [128, NPG, 8]); Cp = sb("Cp", [128, NPG, 8])
        cend = sb("cend", [128, 8]); biasP = sb("biasP", [128, NPG, 8])
        Qblk = sb("Qblk", [128, 4, 2 * LS], BF16); KTn = sb("KTn", [128, 4, LS], BF16); Vn = sb("Vn", [LS, 8, 65], BF16)
        lfn = sb("lfn", [LS, 8]); biasN = sb("biasN", [LS, 8])
        Kp = [sb("Kp%d" % i, [128, 512], BF16) for i in range(3)]
        Vp = [sb("Vp%d" % i, [128, 8, 65], BF16) for i in range(3)]
        Vg = [sb("Vg%d" % i, [128, 512], BF16) for i in range(3)]
        KT2 = [sb("KT2%d" % i, [128, 4, 128], BF16) for i in range(2)]
        Sf = [sb("Sf%d" % i, [128, 8, LS]) for i in range(2)]
        PTs = [sb("PTs%d" % i, [128, 8, LS], BF16) for i in range(2)]
        ys = sb("ys", [LS, 8, 64]); recs = sb("recs", [LS, 8, 1])
        for i in range(3):
            P.op("pool", lambda e, i=i: e.memset(Vp[i][:, :, 64:65], 1.0), writes=[B("Vp", i)])
        P.op("pool", lambda e: e.memset(Vn[:, :, 64:65], 1.0), writes=[B("Vn")])
        P.op("pool", lambda e: e.memset(Qblk[:], 0.0), writes=[B("Qblk")])
        paccs = [pacc[0], self.ps(es, "paccs1", [128, 4, 65])]
        ck_flat = self.io["cache_k"]
        cv_flat = self.io["cache_v"].rearrange("r (h d) -> r h d", h=8)
        for s in range(NS):
            a = T + s * LS
            pt_row = self.io["page_table"][s]
            P.dma("sp", PTc[:NPG, :], pt_row.rearrange("(j o) -> j o", o=1), writes=[B("PTc")])
            P.dma("sp", PTb[:, :], self.io["page_table"][s:s + 1, :].partition_broadcast(128), writes=[B("PTb")])
            P.op("dve", lambda e: e.tensor_copy(out=IDXf[:], in_=PTb[:]), reads=[B("PTb")], writes=[B("IDXf")])
            P.op("dve", lambda e: e.tensor_scalar(out=IDXf[:], in0=IDXf[:], scalar1=128.0, scalar2=iota_f[:, 0:1], op0=ALU.mult, op1=ALU.add),
                 reads=[B("IDXf"), B("f_iotaf")], writes=[B("IDXf")])
            P.op("dve", lambda e: e.tensor_copy(out=IDX[:], in_=IDXf[:]), reads=[B("IDXf")], writes=[B("IDX")])
            P.dma_fn("pool", lambda e: e.indirect_dma_start(out=LFp[:NPG].rearrange("j t h -> j (t h)"), out_offset=None, in_=self.io["cache_logf"],
                                                        in_offset=bass.IndirectOffsetOnAxis(ap=PTc[:NPG, 0:1], axis=0)),
                 reads=[B("PTc")], writes=[B("LFp")])
            P.dma("sp", lfn[:, :], self.io["new_logf"][a:a + LS, :], reads=[B("new_logf", tt_) for (tt_, _) in self.token_tiles(128) if tt_ >= T],
                  writes=[B("lfn")])
            for half in range(2):
                ptp, ptpb = bank()
                for hh in range(4):
                    P.op("pe", lambda e, hh=hh: e.transpose(out=ptp[:, hh * NPG:(hh + 1) * NPG], in_=LFp[:NPG, :, half * 4 + hh], identity=ident[:NPG, :NPG]),
                         reads=[B("LFp"), B("ident")], writes=[ptpb])
                P.op("dve", lambda e: e.tensor_copy(out=LFs2[:, :, half * 4:half * 4 + 4].rearrange("t j h -> t h j"),
                                                    in_=ptp[:, :4 * NPG].rearrange("t (h j) -> t h j", h=4)),
                     reads=[ptpb], writes=[B("LFs2")])
            cumsum2(LFs2, B("LFs2"), Cp, B("Cp"), NPG)
            pce, pceb = bank()
            P.op("pe", lambda e: e.matmul(pce[:, 0:8], lhsT=sel[:, :], rhs=Cp[:, NPG - 1, :], start=True, stop=True),
                 reads=[B("Cp"), B("f_sel")], writes=[pceb])
            P.op("act", lambda e: e.copy(out=cend[:], in_=pce[:, 0:8]), reads=[pceb], writes=[B("cend")])
            P.op("dve", lambda e: e.tensor_tensor(out=biasP[:], in0=cend[:, :].unsqueeze(1).to_broadcast([128, NPG, 8]), in1=Cp[:], op=ALU.subtract),
                 reads=[B("cend"), B("Cp")], writes=[B("biasP")])
            pcn, pcnb = bank()
            P.op("pe", lambda e: e.matmul(pcn[:LS, 0:8], lhsT=tri_i[:LS, :LS], rhs=lfn[:, :], start=True, stop=True),
                 reads=[B("lfn"), B("f_tri_i")], writes=[pcnb])
            P.op("act", lambda e: e.mul(out=biasN[:], in_=pcn[:LS, 0:8], mul=-1.0), reads=[pcnb], writes=[B("biasN")])
            qsrc = zT[0:512, a:a + LS].rearrange("(pr p) t -> p pr t", p=128)
            P.dma("pool", Qblk[0:64, :, 0:LS], qsrc[0:64], writes=[B("Qblk")])
            P.dma("pool", Qblk[64:128, :, LS:2 * LS], qsrc[64:128], writes=[B("Qblk")])
            P.dma("pool", KTn[:], zT[512:1024, a:a + LS].rearrange("(pr p) t -> p pr t", p=128), writes=[B("KTn")])
            P.dma("pool", Vn[:, :, 0:64], self.io["new_v"][a:a + LS, :].rearrange("t (h d) -> t h d", h=8),
                  reads=[B("new_v", tt_) for (tt_, _) in self.token_tiles(128) if tt_ >= T], writes=[B("Vn")])
            first = [True, True]
            for j in range(NPG):
                kp, kpb = Kp[j % 3], B("Kp", j % 3)
                vp, vpb = Vp[j % 3], B("Vp", j % 3)
                P.dma_fn("pool", lambda e: e.indirect_dma_start(out=kp[:, :], out_offset=None, in_=ck_flat,
                                                            in_offset=bass.IndirectOffsetOnAxis(ap=IDX[:, j:j + 1], axis=0)),
                     reads=[B("IDX")], writes=[kpb])
                vg, vgb = Vg[j % 3], B("Vg", j % 3)
                P.dma_fn("pool", lambda e: e.indirect_dma_start(out=vg[:, :], out_offset=None, in_=self.io["cache_v"],
                                                            in_offset=bass.IndirectOffsetOnAxis(ap=IDX[:, j:j + 1], axis=0)),
                     reads=[B("IDX")], writes=[vgb])
                P.op("dve", lambda e: e.tensor_copy(out=vp[:, :, 0:64], in_=vg[:, :].rearrange("k (h d) -> k h d", h=8)), reads=[vgb], writes=[vpb])
                k2, k2b = KT2[j % 2], B("KT2", j % 2)
                pkt, pktb = bankb[0], B("fbankb", 0)
                for pr in range(4):
                    P.op("pe", lambda e, pr=pr: e.transpose(out=pkt[:, pr * 128:(pr + 1) * 128], in_=kp[:, pr * 128:(pr + 1) * 128], identity=identb[:, :]),
                         reads=[kpb, B("identb")], writes=[pktb])
                P.op("dve", lambda e: e.tensor_copy(out=k2[:].rearrange("p a b -> p (a b)"), in_=pkt[:, 0:512]), reads=[pktb], writes=[k2b])
                pss, pssb = bank()
                for pr in range(4):
                    P.op("pe", lambda e, pr=pr: e.matmul(pss[:, pr * 2 * LS:(pr + 1) * 2 * LS], lhsT=k2[:, pr, :], rhs=Qblk[:, pr, :], start=True, stop=True),
                         reads=[k2b, B("Qblk")], writes=[pssb])
                sf, sfb = Sf[j % 2], B("Sf", j % 2)
                pts, ptsb = PTs[j % 2], B("PTs", j % 2)
                P.op("dve", lambda e: e.scalar_tensor_tensor(out=sf[:], in0=pss[:, :8 * LS].rearrange("k (h q) -> k h q", h=8), scalar=0.125,
                                                             in1=biasP[:, j, :].unsqueeze(2).to_broadcast([128, 8, LS]), op0=ALU.mult, op1=ALU.add),
                     reads=[pssb, B("biasP")], writes=[sfb])
                P.op("act", lambda e: e.activation(out=pts[:], in_=sf[:], func=AF.Exp), reads=[sfb], writes=[ptsb])
                for h in range(8):
                    bkk = h // 4
                    P.op("pe", lambda e, h=h, bkk=bkk: e.matmul(paccs[bkk][:LS, h % 4, :], lhsT=pts[:, h, :], rhs=vp[:, h, :], start=first[bkk], stop=False),
                         reads=[ptsb, vpb], writes=[(B("pacc", 0) if bkk == 0 else B("paccs", 1))])
                    first[bkk] = False
            psn, psnb = bank()
            for pr in range(4):
                P.op("pe", lambda e, pr=pr: e.matmul(psn[:LS, pr * 2 * LS:(pr + 1) * 2 * LS], lhsT=KTn[:, pr, :], rhs=Qblk[:, pr, :], start=True, stop=True),
                     reads=[B("KTn"), B("Qblk")], writes=[psnb])
            sf, sfb = Sf[0], B("Sf", 0)
            pts, ptsb = PTs[0], B("PTs", 0)
            P.op("dve", lambda e: e.scalar_tensor_tensor(out=sf[:LS], in0=psn[:LS, :8 * LS].rearrange("k (h q) -> k h q", h=8), scalar=0.125,
                                                         in1=biasN[:, :].unsqueeze(2).to_broadcast([LS, 8, LS]), op0=ALU.mult, op1=ALU.add),
                 reads=[psnb, B("biasN")], writes=[sfb])
            P.op("dve", lambda e: e.tensor_tensor(out=sf[:LS], in0=sf[:LS], in1=negf[:LS, :LS].unsqueeze(1).to_broadcast([LS, 8, LS]), op=ALU.add),
                 reads=[sfb, B("f_negf")], writes=[sfb])
            P.op("act", lambda e: e.activation(out=pts[:LS], in_=sf[:LS], func=AF.Exp), reads=[sfb], writes=[ptsb])
            for h in range(8):
                bkk = h // 4
                P.op("pe", lambda e, h=h, bkk=bkk: e.matmul(paccs[bkk][:LS, h % 4, :], lhsT=pts[:LS, h, :], rhs=Vn[:, h, :], start=False, stop=True),
                     reads=[ptsb, B("Vn")], writes=[(B("pacc", 0) if bkk == 0 else B("paccs", 1))])
            for bkk in range(2):
                hs = slice(bkk * 4, bkk * 4 + 4)
                P.op("dve", lambda e, bkk=bkk, hs=hs: e.reciprocal(out=recs[:, hs, :], in_=paccs[bkk][:LS, :, 64:65]), reads=[(B("pacc", 0) if bkk == 0 else B("paccs", 1))], writes=[B("recs")])
                P.op("dve", lambda e, bkk=bkk, hs=hs: e.tensor_tensor(out=ys[:, hs, :], in0=paccs[bkk][:LS, :, 0:64],
                                                                      in1=recs[:, hs, :].to_broadcast([LS, 4, 64]), op=ALU.mult),
                     reads=[(B("pacc", 0) if bkk == 0 else B("paccs", 1)), B("recs")], writes=[B("ys")])
            P.dma("sp", self.io["mixC1"][a:a + LS, :], ys[:].rearrange("t h d -> t (h d)"), reads=[B("ys")], writes=[B("mixC1s", s)])
        P.barrier()


Model.phase_fox = _fox_phase


def _ssd_phase(self):
    P, T, LS, NS = self.P, self.T, self.LS, self.NS
    zT = self.io["zT1"]
    X0 = FOX_IN + 512
    NC = dict(allow_slow_non_contiguous=True)
    B = P.buf
    CM = 128
    with ExitStack() as es:
        sb = lambda name, shape, dt=F32: self.sb(es, name, shape, dt)

        def ld(name, shape, src, **kw):
            t = sb(name, shape)
            P.dma("sp", t[:], src, writes=[B(name)], **kw)
            return t
        ident = self.ident
        tri_i = ld("s_tri_i", [128, 128], self.io["c_triu_incl"])
        ones = ld("s_ones", [128, 128], self.io["c_ones"])
        cw = sb("s_cw", [128, 8, 4])
        for k in range(4):
            P.dma("sp", cw[:, :, k], self.io["ssm_conv_w"][k].rearrange("(c p) -> p c", p=128), writes=[B("s_cw")], **NC)
        cb = ld("s_cb", [128, 8], self.io["ssm_conv_b"][0].rearrange("(c p) -> p c", p=128), **NC)
        dtb = ld("s_dtb", [128, 8], self.io["ssm_dt_bias"].partition_broadcast(128))
        Abc = ld("s_A", [128, 8], self.io["ssm_a_log"].partition_broadcast(128))
        Dbc = ld("s_D", [128, 8], self.io["ssm_d"].partition_broadcast(128))
        nw = ld("s_nw", [128, 512], self.io["ssm_norm_w"].partition_broadcast(128))
        P.op("act", lambda e: e.activation(out=Abc[:], in_=Abc[:], func=AF.Exp), reads=[B("s_A")], writes=[B("s_A")])
        P.op("dve", lambda e: e.tensor_scalar(out=Abc[:], in0=Abc[:], scalar1=-1.0, scalar2=None, op0=ALU.mult), reads=[B("s_A")], writes=[B("s_A")])
        U = sb("s_U", [128, 8, CM + 3]); acc = sb("s_acc", [128, 8, CM]); tmp = sb("s_tmp", [128, 8, CM]); xbc = sb("s_xbc", [128, 8, CM])
        xtm = sb("s_xtm", [128, 512]); Btm = sb("s_Btm", [128, 2, 128])
        dt = sb("s_dt", [128, 8]); av = sb("s_a", [128, 8]); abc = sb("s_abc", [128, 8, CM]); acs = sb("s_acs", [128, 8])
        Lm = sb("s_Lm", [128, 8, CM]); CBm = sb("s_CBm", [128, 2, CM]); Gm = sb("s_Gm", [128, 8, CM])
        eac = sb("s_eac", [128, 8]); wgt = sb("s_wgt", [128, 8]); eend = sb("s_eend", [128, 8])
        yv = sb("s_y", [128, 8, 64]); t2 = sb("s_t2", [128, 8, 64]); xw = sb("s_xw", [128, 8, 64])
        zg = sb("s_zg", [128, 512]); junk = sb("s_junk", [128, 512]); ss = sb("s_ss", [128, 4])
        ST = sb("s_ST", [128, 8, 64]); Sn = sb("s_Sn", [128, 4, 128])
        banks = [self.ps(es, "sbk%d" % i, [128, 512]) for i in range(8)]
        bstate = [0]

        def bank():
            i = bstate[0] % 8
            bstate[0] += 1
            return banks[i], B("sbank", i)

        seqs = [(0, T, None)] + [(T + s * LS, LS, s) for s in range(NS)]
        for (q0, L, s) in seqs:
            oi = 0 if s is None else 1 + s
            if s is None:
                P.op("dve", lambda e: e.memset(ST[:], 0.0), writes=[B("s_ST")])
            else:
                P.dma("sp", Sn[:], self.io["state_ssm"][s].rearrange("h p n -> (h p) n").rearrange("(c q) n -> q c n", q=128), writes=[B("s_Sn")])
                pb, pbb = bank()
                for c in range(4):
                    P.op("pe", lambda e, c=c: e.transpose(out=pb[:, c * 128:(c + 1) * 128], in_=Sn[:, c, :], identity=ident[:, :]),
                         reads=[B("s_Sn"), B("ident")], writes=[pbb])
                P.op("dve", lambda e: e.tensor_copy(out=ST[:].rearrange("n h p -> n (h p)"), in_=pb[:, :]), reads=[pbb], writes=[B("s_ST")])
            for c0 in range(0, L, CM):
                C = min(CM, L - c0)
                a = q0 + c0
                usrc = zT[X0:X0 + 1024, :].rearrange("(c p) t -> p c t", p=128)
                if c0 == 0:
                    P.dma("sp", U[:, :, 3:3 + C], usrc[:, :, a:a + C], writes=[B("s_U")])
                    if s is None:
                        P.op("pool", lambda e: e.memset(U[:, :, 0:3], 0.0), writes=[B("s_U")])
                    else:
                        for k in range(3):
                            P.dma("sp", U[:, :, k], self.io["state_ssm_conv"][s, k].rearrange("(c p) -> p c", p=128), writes=[B("s_U")], **NC)
                else:
                    P.dma("sp", U[:, :, 0:3 + C], usrc[:, :, a - 3:a + C], writes=[B("s_U")])
                wb_ = lambda k: cw[:, :, k:k + 1].to_broadcast([128, 8, C])
                P.op("dve", lambda e: e.tensor_tensor(out=acc[:, :, :C], in0=U[:, :, 0:C], in1=wb_(0), op=ALU.mult), reads=[B("s_U"), B("s_cw")], writes=[B("s_acc")])
                for k in range(1, 4):
                    P.op("pool", lambda e, k=k: e.tensor_tensor(out=tmp[:, :, :C], in0=U[:, :, k:k + C], in1=wb_(k), op=ALU.mult),
                         reads=[B("s_U"), B("s_cw")], writes=[B("s_tmp")])
                    P.op("dve", lambda e: e.tensor_tensor(out=acc[:, :, :C], in0=acc[:, :, :C], in1=tmp[:, :, :C], op=ALU.add),
                         reads=[B("s_acc"), B("s_tmp")], writes=[B("s_acc")])
                P.op("dve", lambda e: e.tensor_tensor(out=acc[:, :, :C], in0=acc[:, :, :C], in1=cb[:, :].unsqueeze(2).to_broadcast([128, 8, C]), op=ALU.add),
                     reads=[B("s_acc"), B("s_cb")], writes=[B("s_acc")])
                P.op("act", lambda e: e.activation(out=xbc[:, :, :C], in_=acc[:, :, :C], func=AF.Silu), reads=[B("s_acc")], writes=[B("s_xbc")])
                px, pxb = bank()
                for c in range(4):
                    P.op("pe", lambda e, c=c: e.transpose(out=px[:C, c * 128:(c + 1) * 128], in_=xbc[:, c, :C], identity=ident[:, :]),
                         reads=[B("s_xbc"), B("ident")], writes=[pxb])
                P.op("act", lambda e: e.copy(out=xtm[:C, :], in_=px[:C, :]), reads=[pxb], writes=[B("s_xtm")])
                pB, pBb = bank()
                for g in range(2):
                    P.op("pe", lambda e, g=g: e.transpose(out=pB[:C, g * 128:(g + 1) * 128], in_=xbc[:, 4 + g, :C], identity=ident[:, :]),
                         reads=[B("s_xbc"), B("ident")], writes=[pBb])
                P.op("act", lambda e: e.copy(out=Btm[:C].rearrange("s g n -> s (g n)"), in_=pB[:C, 0:256]), reads=[pBb], writes=[B("s_Btm")])
                P.dma("sp", dt[:C, :], self.io["dt_tm"][a:a + C, :], writes=[B("s_dt")])
                P.op("dve", lambda e: e.tensor_tensor(out=dt[:C, :], in0=dt[:C, :], in1=dtb[:C, :], op=ALU.add), reads=[B("s_dt"), B("s_dtb")], writes=[B("s_dt")])
                P.op("act", lambda e: e.activation(out=dt[:C, :], in_=dt[:C, :], func=AF.Exp), reads=[B("s_dt")], writes=[B("s_dt")])
                P.op("act", lambda e: e.activation(out=dt[:C, :], in_=dt[:C, :], func=AF.Ln, bias=ones[:C, 0:1], scale=1.0),
                     reads=[B("s_dt"), B("s_ones")], writes=[B("s_dt")])
                P.op("dve", lambda e: e.tensor_tensor(out=av[:C, :], in0=dt[:C, :], in1=Abc[:C, :], op=ALU.mult), reads=[B("s_dt"), B("s_A")], writes=[B("s_a")])
                P.op("pool", lambda e: e.tensor_copy(out=abc[:C, :, :C], in_=av[:C, :].unsqueeze(2).to_broadcast([C, 8, C])), reads=[B("s_a")], writes=[B("s_abc")])
                pac, pacb = bank()
                P.op("pe", lambda e: e.matmul(pac[:C, 0:8], lhsT=tri_i[:C, :C], rhs=av[:C, :], start=True, stop=True), reads=[B("s_a"), B("s_tri_i")], writes=[pacb])
                P.op("pe", lambda e: e.matmul(pac[:, 8:16], lhsT=ones[:C, :], rhs=av[:C, :], start=True, stop=True), reads=[B("s_a"), B("s_ones")], writes=[pacb])
                P.op("act", lambda e: e.copy(out=acs[:C, :], in_=pac[:C, 0:8]), reads=[pacb], writes=[B("s_acs")])
                P.op("act", lambda e: e.activation(out=eac[:C, :], in_=pac[:C, 0:8], func=AF.Exp), reads=[pacb], writes=[B("s_eac")])
                P.op("act", lambda e: e.activation(out=eend[:, :], in_=pac[:, 8:16], func=AF.Exp), reads=[pacb], writes=[B("s_eend")])
                P.op("dve", lambda e: e.tensor_tensor(out=wgt[:C, :], in0=pac[:C, 8:16], in1=acs[:C, :], op=ALU.subtract), reads=[pacb, B("s_acs")], writes=[B("s_wgt")])
                P.op("act", lambda e: e.activation(out=wgt[:C, :], in_=wgt[:C, :], func=AF.Exp), reads=[B("s_wgt")], writes=[B("s_wgt")])
                P.op("dve", lambda e: e.tensor_tensor(out=wgt[:C, :], in0=wgt[:C, :], in1=dt[:C, :], op=ALU.mult), reads=[B("s_wgt"), B("s_dt")], writes=[B("s_wgt")])
                for half in range(2):
                    pb_, pbb_ = bank()
                    for hh in range(4):
                        h = half * 4 + hh
                        P.op("pe", lambda e, h=h, hh=hh: e.matmul(pb_[:C, hh * C:(hh + 1) * C], lhsT=abc[:C, h, :C], rhs=tri_i[:C, :C], start=True, stop=True),
                             reads=[B("s_abc"), B("s_tri_i")], writes=[pbb_])
                    hs = slice(half * 4, half * 4 + 4)
                    P.op("dve", lambda e: e.tensor_tensor(out=Lm[:C, hs, :C], in0=pb_[:C, :4 * C].rearrange("s (h t) -> s h t", h=4),
                                                          in1=acs[:C, hs].unsqueeze(2).to_broadcast([C, 4, C]), op=ALU.subtract),
                         reads=[pbb_, B("s_acs")], writes=[B("s_Lm")])
                P.op("pool", lambda e: e.tensor_scalar(out=Lm[:C, :, :C], in0=Lm[:C, :, :C], scalar1=0.0, scalar2=None, op0=ALU.min), reads=[B("s_Lm")], writes=[B("s_Lm")])
                P.op("act", lambda e: e.activation(out=Lm[:C, :, :C], in_=Lm[:C, :, :C], func=AF.Exp), reads=[B("s_Lm")], writes=[B("s_Lm")])
                pcb, pcbb = bank()
                for g in range(2):
                    P.op("pe", lambda e, g=g: e.matmul(pcb[:C, g * C:(g + 1) * C], lhsT=xbc[:, 4 + g, :C], rhs=xbc[:, 6 + g, :C], start=True, stop=True),
                         reads=[B("s_xbc")], writes=[pcbb])
                P.op("dve", lambda e: e.tensor_tensor(out=CBm[:C, :, :C], in0=pcb[:C, :2 * C].rearrange("s (g t) -> s g t", g=2),
                                                      in1=tri_i[:C, :C].unsqueeze(1).to_broadcast([C, 2, C]), op=ALU.mult),
                     reads=[pcbb, B("s_tri_i")], writes=[B("s_CBm")])
                for g in range(2):
                    hs = slice(g * 4, g * 4 + 4)
                    P.op("dve" if g == 0 else "pool", lambda e, g=g, hs=hs: e.tensor_tensor(out=Gm[:C, hs, :C], in0=Lm[:C, hs, :C],
                                                                                         in1=CBm[:C, g:g + 1, :C].to_broadcast([C, 4, C]), op=ALU.mult),
                         reads=[B("s_Lm"), B("s_CBm")], writes=[B("s_Gm")])
                P.op("dve", lambda e: e.tensor_tensor(out=Gm[:C, :, :C], in0=Gm[:C, :, :C], in1=dt[:C, :].unsqueeze(2).to_broadcast([C, 8, C]), op=ALU.mult),
                     reads=[B("s_Gm"), B("s_dt")], writes=[B("s_Gm")])
                hv = lambda t, h: t[:C, h * 64:(h + 1) * 64]
                py, pyb = bank()
                pyi, pyib = bank()
                for h in range(8):
                    P.op("pe", lambda e, h=h: e.matmul(hv(py, h), lhsT=Gm[:C, h, :C], rhs=hv(xtm, h), start=True, stop=True),
                         reads=[B("s_Gm"), B("s_xtm")], writes=[pyb])
                    P.op("pe", lambda e, h=h: e.matmul(hv(pyi, h), lhsT=xbc[:, 6 + h // 4, :C], rhs=ST[:, h, :], start=True, stop=True),
                         reads=[B("s_xbc"), B("s_ST")], writes=[pyib])
                v3 = lambda ap: ap.rearrange("t (h p) -> t h p", h=8)
                b8 = lambda t: t[:C, :].unsqueeze(2).to_broadcast([C, 8, 64])
                P.op("dve", lambda e: e.tensor_tensor(out=yv[:C], in0=v3(pyi[:C, :]), in1=b8(eac), op=ALU.mult), reads=[pyib, B("s_eac")], writes=[B("s_y")])
                P.op("dve", lambda e: e.tensor_tensor(out=yv[:C], in0=yv[:C], in1=v3(py[:C, :]), op=ALU.add), reads=[B("s_y"), pyb], writes=[B("s_y")])
                P.op("pool", lambda e: e.tensor_tensor(out=t2[:C], in0=v3(xtm[:C, :]), in1=b8(Dbc), op=ALU.mult), reads=[B("s_xtm"), B("s_D")], writes=[B("s_t2")])
                P.op("dve", lambda e: e.tensor_tensor(out=yv[:C], in0=yv[:C], in1=t2[:C], op=ALU.add), reads=[B("s_y"), B("s_t2")], writes=[B("s_y")])
                P.op("pool", lambda e: e.tensor_tensor(out=xw[:C], in0=v3(xtm[:C, :]), in1=b8(wgt), op=ALU.mult), reads=[B("s_xtm"), B("s_wgt")], writes=[B("s_xw")])
                pS, pSb = bank()
                for h in range(8):
                    P.op("pe", lambda e, h=h: e.matmul(pS[:, h * 64:(h + 1) * 64], lhsT=Btm[:C, h // 4, :], rhs=xw[:C, h, :], start=True, stop=True),
                         reads=[B("s_Btm"), B("s_xw")], writes=[pSb])
                P.op("dve", lambda e: e.tensor_tensor(out=ST[:], in0=ST[:], in1=eend[:, :].unsqueeze(2).to_broadcast([128, 8, 64]), op=ALU.mult),
                     reads=[B("s_ST"), B("s_eend")], writes=[B("s_ST")])
                P.op("dve", lambda e: e.tensor_tensor(out=ST[:], in0=ST[:], in1=pS[:, :].rearrange("n (h p) -> n h p", h=8), op=ALU.add),
                     reads=[B("s_ST"), pSb], writes=[B("s_ST")])
                P.dma("sp", zg[:C, :], self.io["zg_tm"][a:a + C, :], writes=[B("s_zg")])
                P.op("act", lambda e: e.activation(out=zg[:C, :], in_=zg[:C, :], func=AF.Silu), reads=[B("s_zg")], writes=[B("s_zg")])
                yf_ = yv[:C].rearrange("t h p -> t (h p)")
                P.op("dve", lambda e: e.tensor_tensor(out=yf_, in0=yf_, in1=zg[:C, :], op=ALU.mult), reads=[B("s_y"), B("s_zg")], writes=[B("s_y")])
                P.op("act", lambda e: e.activation(out=junk[:C, :], in_=yf_, func=AF.Square, accum_out=ss[:C, 0:1]), reads=[B("s_y")], writes=[B("s_junk"), B("s_ss")])
                self.rstd(ss, B("s_ss"), C, inv=1.0 / 512)
                P.op("dve", lambda e: e.scalar_tensor_tensor(out=yf_, in0=yf_, scalar=ss[:C, 2:3], in1=nw[:C, :], op0=ALU.mult, op1=ALU.mult),
                     reads=[B("s_y"), B("s_ss"), B("s_nw")], writes=[B("s_y")])
                P.dma("pool", self.io["mixD1"][a:a + C, :], yf_, reads=[B("s_y")], writes=[B("mixD1", a)])
                if c0 + C == L:
                    for k in range(3):
                        P.dma("pool", self.io["new_conv"][oi, k].rearrange("(c p) -> p c", p=128), U[:, :, C + k], reads=[B("s_U")],
                              writes=[B("new_conv", oi, k)], **NC)
            pb, pbb = bank()
            STf = ST[:].rearrange("n h p -> n (h p)")
            for c in range(4):
                P.op("pe", lambda e, c=c: e.transpose(out=pb[:, c * 128:(c + 1) * 128], in_=STf[:, c * 128:(c + 1) * 128], identity=ident[:, :]),
                     reads=[B("s_ST"), B("ident")], writes=[pbb])
            P.op("act", lambda e: e.copy(out=Sn[:].rearrange("q c n -> q (c n)"), in_=pb[:, :]), reads=[pbb], writes=[B("s_Sn")])
            P.dma("pool", self.io["new_ssm"][oi].rearrange("h p n -> (h p) n").rearrange("(c q) n -> q c n", q=128), Sn[:], reads=[B("s_Sn")],
                  writes=[B("new_ssm", oi)])
        P.barrier()


Model.phase_ssd = _ssd_phase
```

```python
from contextlib import ExitStack
import numpy as np
import concourse.bass as bass
import concourse.mybir as mybir
from concourse.bass_utils import run_bass_kernel_spmd

F32 = mybir.dt.float32
BF16 = mybir.dt.bfloat16
I32 = mybir.dt.int32
AF = mybir.ActivationFunctionType
ALU = mybir.AluOpType
AX = mybir.AxisListType

NDS = 48


class Buf:
    __slots__ = ("name", "w", "r")

    def __init__(self, name):
        self.name = name
        self.w = None
        self.r = {}


class Prog:
    ENG = ("pe", "act", "dve", "pool", "sp")

    def __init__(self):
        nc = self.nc = bass.Bass("TRN2", target_bir_lowering=False)
        self.e = dict(pe=nc.tensor, act=nc.scalar, dve=nc.vector, pool=nc.gpsimd, sp=nc.sync)
        self.esem = {k: nc.alloc_semaphore("s_" + k) for k in self.ENG}
        self.ecnt = {k: 0 for k in self.ENG}
        self.seen = {k: {} for k in self.ENG}
        self.dsem = [nc.alloc_semaphore("d%d" % i) for i in range(NDS)]
        self.dval = [0] * NDS
        self.dnext = 0
        self.bufs = {}
        self.ninst = 0

    def buf(self, *key):
        b = self.bufs.get(key)
        if b is None:
            b = self.bufs[key] = Buf(key)
        return b

    def _sem(self, key):
        return self.esem[key[1]] if key[0] == "e" else self.dsem[key[1]]

    def _wait(self, eng, ev):
        if ev is None:
            return
        key, val = ev
        if key[0] == "e" and key[1] == eng and (eng == "pe" or self.ecnt[eng] - val >= 2):
            return
        if self.seen[eng].get(key, 0) >= val:
            return
        self.e[eng].wait_ge(self._sem(key), val)
        self.seen[eng][key] = val

    def _deps(self, eng, reads, writes):
        for b in reads:
            self._wait(eng, b.w)
        for b in writes:
            self._wait(eng, b.w)
            for k, v in b.r.items():
                self._wait(eng, (k, v))

    def _commit(self, ev, reads, writes):
        key, val = ev
        for b in reads:
            if b.r.get(key, 0) < val:
                b.r[key] = val
        for b in writes:
            b.w = ev
            b.r = {}

    def op(self, eng, fn, reads=(), writes=()):
        self._deps(eng, reads, writes)
        ins = fn(self.e[eng])
        ins.then_inc(self.esem[eng], 1)
        self.ecnt[eng] += 1
        self.ninst += 1
        self._commit((("e", eng), self.ecnt[eng]), reads, writes)

    def dma(self, q, out, in_, reads=(), writes=(), **kw):
        i = self.dnext
        self.dnext = (self.dnext + 1) % NDS
        if self.dval[i] > 0:
            self._wait(q, (("d", i), self.dval[i]))
        self._deps(q, reads, writes)
        self.e[q].dma_start(out=out, in_=in_, **kw).then_inc(self.dsem[i], 16)
        self.dval[i] += 16
        self.ninst += 1
        self._commit((("d", i), self.dval[i]), reads, writes)

    def dma_fn(self, q, fn, reads=(), writes=()):
        i = self.dnext
        self.dnext = (self.dnext + 1) % NDS
        if self.dval[i] > 0:
            self._wait(q, (("d", i), self.dval[i]))
        self._deps(q, reads, writes)
        fn(self.e[q]).then_inc(self.dsem[i], 16)
        self.dval[i] += 16
        self.ninst += 1
        self._commit((("d", i), self.dval[i]), reads, writes)

    def barrier(self):
        for e in self.ENG:
            for f in self.ENG:
                if f != e and self.ecnt[f] > 0:
                    self._wait(e, (("e", f), self.ecnt[f]))
            for i in range(NDS):
                if self.dval[i] > 0:
                    self._wait(e, (("d", i), self.dval[i]))
        self.bufs_reset()

    def bufs_reset(self):
        for b in self.bufs.values():
            b.w = None
            b.r = {}

    def finish(self):
        for i in range(NDS):
            if self.dval[i] > 0:
                self._wait("sp", (("d", i), self.dval[i]))
        for f in self.ENG:
            if f != "sp" and self.ecnt[f] > 0:
                self._wait("sp", (("e", f), self.ecnt[f]))


D = 1024
HD = 64
NH = 8
SC = 512
RW = 512
RWS = 1792
IN0 = 3328
FOX_IN = 1544
SSM_CD = 1024
IN1 = 3088
FF = 2816
EPS = 1e-6


def const_arrays(LS=8):
    c = {}
    c["ident"] = np.eye(128, dtype=np.float32)
    iu = np.triu(np.ones((128, 128), np.float32), 0)
    su = np.triu(np.ones((128, 128), np.float32), 1)
    c["triu_incl"] = iu
    c["triu_strict"] = su
    c["tril_strict"] = su.T.copy()
    c["ones"] = np.ones((128, 128), np.float32)
    sel = np.zeros((128, 128), np.float32)
    sel[127, :] = 1.0
    c["sel_last"] = sel
    c["negmask"] = (-30000.0 * su.T).astype(np.float32)
    c["iota"] = np.arange(128, dtype=np.int32).reshape(128, 1)
    nb = 128 // LS
    c["blktri"] = np.kron(np.eye(nb, dtype=np.float32), iu[:LS, :LS]).astype(np.float32)[:128, :128]
    return c


class Model:
    def __init__(self, T, NS, LS, NPG, NPOOL, debug=()):
        self.T, self.NS, self.LS, self.NPG, self.NPOOL = T, NS, LS, NPG, NPOOL
        self.TS = NS * LS
        self.TT = T + self.TS
        self.debug = set(debug)
        self.P = Prog()
        self.nc = self.P.nc
        self.io = {}
        self.build()

    def inp(self, name, shape, dt=F32):
        self.io[name] = self.nc.dram_tensor(name, list(shape), dt, kind="ExternalInput").ap()
        return self.io[name]

    def outp(self, name, shape, dt=F32):
        self.io[name] = self.nc.dram_tensor(name, list(shape), dt, kind="ExternalOutput").ap()
        return self.io[name]

    def scratch(self, name, shape, dt=F32):
        kind = "ExternalOutput" if name in self.debug else "Internal"
        self.io[name] = self.nc.dram_tensor(name, list(shape), dt, kind=kind).ap()
        return self.io[name]

    def _nm(self, name):
        self._uid = getattr(self, "_uid", 0) + 1
        return "%s_%d" % (name, self._uid)

    def sb(self, es, name, shape, dt=F32):
        return es.enter_context(self.nc.sbuf_tensor(self._nm(name), list(shape), dt))

    def ps(self, es, name, shape, dt=F32):
        return es.enter_context(self.nc.psum_tensor(self._nm(name), list(shape), dt))

    def token_tiles(self, step=128):
        tiles = [(t0, min(step, self.T - t0)) for t0 in range(0, self.T, step)]
        for s0 in range(0, self.TS, step):
            tiles.append((self.T + s0, min(step, self.TS - s0)))
        return tiles

    def build(self):
        T, TT = self.T, self.TT
        m = self
        m.inp("xp", [T, D]); m.inp("xs", [self.TS, D])
        for k, arr in const_arrays().items():
            m.inp("c_" + k, arr.shape, I32 if arr.dtype == np.int32 else F32)
        m.inp("norm_mix", [2, D]); m.inp("norm_ffn", [2, D]); m.inp("norm_final", [1, D])
        m.inp("w_in0", [D, IN0]); m.inp("w_out0", [D, D]); m.inp("w_in1", [D, IN1]); m.inp("w_out1", [D, D])
        m.inp("w_gate", [2, D, FF]); m.inp("w_up", [2, D, FF]); m.inp("w_down", [2, FF, D])
        m.inp("sc_conv_w", [3, SC]); m.inp("state_sc", [self.NS, 2, SC])
        m.inp("state_shift", [self.NS, RWS]); m.inp("state_wkv", [self.NS, NH, HD, HD])
        m.inp("rw_mu", [1, RWS]); m.inp("rw_w0", [1, RW]); m.inp("rw_w2", [64, RW]); m.inp("rw_a0", [1, RW])
        m.inp("rw_a2", [64, RW]); m.inp("rw_g2", [128, RW]); m.inp("rw_k_k", [1, RW]); m.inp("rw_k_a", [1, RW])
        m.inp("rw_r_k", [1, RW]); m.inp("rw_ln_w", [1, RW]); m.inp("rw_ln_b", [1, RW])
        m.outp("new_shift", [1 + self.NS, RWS]); m.outp("new_wkv", [1 + self.NS, NH, HD, HD])
        m.outp("y", [TT, D])
        m.outp("new_sc", [1 + self.NS, 2, SC])
        m.scratch("h0", [TT, D])
        m.scratch("zT0", [IN0, TT])
        m.scratch("mixA0", [SC, TT])
        m.scratch("mixB0", [TT, RW])
        NS, NPG, NPOOL = self.NS, self.NPG, self.NPOOL
        m.inp("cache_k", [NPOOL * 128, 512]); m.inp("cache_v", [NPOOL * 128, 512]); m.inp("cache_logf", [NPOOL, 1024])
        m.inp("page_table", [NS, NPG], I32)
        m.inp("state_ssm_conv", [NS, 3, SSM_CD]); m.inp("state_ssm", [NS, NH, HD, 128])
        m.inp("fox_f_bias", [1, 8]); m.inp("ssm_conv_w", [4, SSM_CD]); m.inp("ssm_conv_b", [1, SSM_CD])
        m.inp("ssm_dt_bias", [1, 8]); m.inp("ssm_a_log", [1, 8]); m.inp("ssm_d", [1, 8]); m.inp("ssm_norm_w", [1, 512])
        m.outp("new_k", [TT, 512]); m.outp("new_v", [TT, 512]); m.outp("new_logf", [TT, 8])
        m.outp("new_conv", [1 + NS, 3, SSM_CD]); m.outp("new_ssm", [1 + NS, NH, HD, 128])
        m.scratch("zT1", [IN1, TT]); m.scratch("fpre", [TT, 8]); m.scratch("zg_tm", [TT, 512]); m.scratch("dt_tm", [TT, 8])
        m.scratch("mixC1", [TT, 512]); m.scratch("mixD1", [TT, 512])

        with ExitStack() as es:
            self.consts(es)
            self.phase_inproj(0, "xp", "xs", "w_in0", IN0, "zT0")
            self.phase_conv0()
            self.phase_rwkv()
            self.phase_out_ffn(0, "xp", "xs", "mixA0", "mixB0", "w_out0", "h0")
            self.phase_inproj(1, "h0", None, "w_in1", IN1, "zT1",
                              tm_cols=[(512, 512, "new_k"), (1024, 512, "new_v"), (1536, 8, "fpre"),
                                       (1544, 512, "zg_tm"), (3080, 8, "dt_tm")])
            self.phase_fox()
            self.phase_ssd()
            self.phase_out_ffn(1, "h0", None, "mixC1", "mixD1", "w_out1", None, final=True, a_tm=True)
        self.P.finish()

    def consts(self, es):
        P = self.P
        self.ident = self.sb(es, "ident", [128, 128])
        self.identb = self.sb(es, "identb", [128, 128], BF16)
        P.dma("sp", self.ident[:], self.io["c_ident"], writes=[P.buf("ident")])
        P.dma("pool", self.identb[:], self.io["c_ident"], writes=[P.buf("identb")])
        self.eps_t = self.sb(es, "eps_t", [128, 1])
        P.op("dve", lambda e: e.memset(self.eps_t[:], EPS), writes=[P.buf("eps_t")])

    def rstd(self, ss, ssb, n, inv=1.0 / D, eps_t=None):
        P = self.P
        et = self.eps_t if eps_t is None else eps_t
        P.op("act", lambda e: e.activation(out=ss[:n, 1:2], in_=ss[:n, 0:1], func=AF.Sqrt, bias=et[:n, 0:1], scale=inv),
             reads=[ssb, P.buf("eps_t")], writes=[ssb])
        P.op("dve", lambda e: e.reciprocal(out=ss[:n, 2:3], in_=ss[:n, 1:2]), reads=[ssb], writes=[ssb])

    def norm_T(self, x_t, xb, n, g_t, gb, hnT, hb, tmp, pst, tag):
        P = self.P
        junk, jb = tmp["junk"], P.buf("junk" + tag)
        ss, ssb = tmp["ss"], P.buf("ss" + tag)
        xn, xnb = tmp["xn"], P.buf("xn" + tag)
        pb = P.buf("pst" + tag)
        P.op("act", lambda e: e.activation(out=junk[:n, :], in_=x_t[:n, :], func=AF.Square, accum_out=ss[:n, 0:1]),
             reads=[xb], writes=[jb, ssb])
        self.rstd(ss, ssb, n)
        P.op("dve", lambda e: e.tensor_scalar(out=xn[:n, :], in0=x_t[:n, :], scalar1=ss[:n, 2:3], scalar2=None,
                                              op0=ALU.mult), reads=[xb, ssb], writes=[xnb])
        for c in range(8):
            P.op("pe", lambda e, c=c: e.transpose(out=pst[:, c, :n], in_=xn[:n, c * 128:(c + 1) * 128],
                                                  identity=self.identb[:n, :n]),
                 reads=[xnb, P.buf("identb")], writes=[pb])
        P.op("dve", lambda e: e.tensor_tensor(out=hnT[:, :, :n], in0=pst[:, :, :n],
                                              in1=g_t[:, :].unsqueeze(2).to_broadcast([128, 8, n]), op=ALU.mult),
             reads=[pb, gb], writes=[hb])

    def xsrc(self, xp_name, xs_name, tt, n):
        if xs_name is None:
            return self.io[xp_name][tt:tt + n, :]
        return self.io[xp_name][tt:tt + n, :] if tt < self.T else self.io[xs_name][tt - self.T:tt - self.T + n, :]

    def phase_inproj(self, layer, xp_name, xs_name, w_name, NOUT, z_name, tm_cols=()):
        P, nc = self.P, self.nc
        zT = self.io[z_name]
        with ExitStack() as es:
            w = self.sb(es, "w_in", [128, 8, NOUT], BF16)
            wb = P.buf("w_in")
            wsrc = self.io[w_name].rearrange("(c p) n -> p c n", p=128)
            for c in range(8):
                P.dma("pool", w[:, c, :], wsrc[:, c, :], writes=[wb])
            g_t = self.sb(es, "g_t", [128, 8])
            gb = P.buf("g_t")
            P.dma("sp", g_t[:], self.io["norm_mix"][layer].rearrange("(c p) -> p c", p=128), writes=[gb],
                  allow_slow_non_contiguous=True)
            ST = 512
            hnT = [self.sb(es, "hnT%d" % i, [128, 8, ST], BF16) for i in range(2)]
            xt = [self.sb(es, "xt%d" % i, [128, D]) for i in range(2)]
            tmp = dict(junk=self.sb(es, "junk", [128, D], BF16), ss=self.sb(es, "ss", [128, 4]),
                       xn=self.sb(es, "xn", [128, D], BF16))
            pst = self.ps(es, "pst", [128, 8, 128], BF16)
            pz = [self.ps(es, "pz%d" % i, [128, ST]) for i in range(4)]
            zst = [self.sb(es, "zst%d" % i, [128, ST]) for i in range(4)]
            ptm = self.ps(es, "ptm", [128, 512])
            tmst = self.sb(es, "tmst", [128, 512])
            nch = (NOUT + 127) // 128
            k = 0
            xi = 0
            for si, (t0, ntot) in enumerate(self.token_tiles(ST)):
                hT, hb = hnT[si % 2], P.buf("hnT", si % 2)
                for j0 in range(0, ntot, 128):
                    n = min(128, ntot - j0)
                    x_t, xb = xt[xi % 2], P.buf("xt", xi % 2)
                    xi += 1
                    tt = t0 + j0
                    P.dma("sp", x_t[:n, :], self.xsrc(xp_name, xs_name, tt, n), writes=[xb])
                    self.norm_T(x_t, xb, n, g_t, gb, hT[:, :, j0:j0 + 128], hb, tmp, pst, "A")
                    for (c0, ncol, dname) in tm_cols:
                        for c in range(8):
                            P.op("pe", lambda e, c=c: e.matmul(ptm[:n, :ncol], lhsT=hT[:, c, j0:j0 + n],
                                                               rhs=w[:, c, c0:c0 + ncol], start=(c == 0), stop=(c == 7)),
                                 reads=[hb, wb], writes=[P.buf("ptm")])
                        P.op("act", lambda e: e.copy(out=tmst[:n, :ncol], in_=ptm[:n, :ncol]),
                             reads=[P.buf("ptm")], writes=[P.buf("tmst")])
                        P.dma("sp", self.io[dname][tt:tt + n, :], tmst[:n, :ncol], reads=[P.buf("tmst")],
                              writes=[P.buf(dname, tt)])
                for mch in range(nch):
                    mc = min(128, NOUT - mch * 128)
                    pzz, pzb = pz[k % 4], P.buf("pz", k % 4)
                    zs, zsb = zst[k % 4], P.buf("zst", k % 4)
                    for c in range(8):
                        P.op("pe", lambda e, c=c: e.matmul(pzz[:mc, :ntot], lhsT=w[:, c, mch * 128:mch * 128 + mc],
                                                           rhs=hT[:, c, :ntot], start=(c == 0), stop=(c == 7)),
                             reads=[hb, wb], writes=[pzb])
                    if k % 2 == 0:
                        P.op("act", lambda e: e.copy(out=zs[:mc, :ntot], in_=pzz[:mc, :ntot]), reads=[pzb], writes=[zsb])
                    else:
                        P.op("dve", lambda e: e.tensor_copy(out=zs[:mc, :ntot], in_=pzz[:mc, :ntot]), reads=[pzb], writes=[zsb])
                    P.dma("sp" if k % 2 == 0 else "pool", zT[mch * 128:mch * 128 + mc, t0:t0 + ntot], zs[:mc, :ntot],
                          reads=[zsb], writes=[P.buf(z_name, mch, si)])
                    k += 1
            P.barrier()

    def phase_conv0(self):
        P, T, LS = self.P, self.T, self.LS
        zT = self.io["zT0"]
        with ExitStack() as es:
            wc = self.sb(es, "wc", [128, 4, 3])
            wcb = P.buf("wc")
            for kk in range(3):
                P.dma("sp", wc[:, :, kk], self.io["sc_conv_w"][kk].rearrange("(c p) -> p c", p=128), writes=[wcb],
                      allow_slow_non_contiguous=True)
            NT = 512
            gbt = [self.sb(es, "gbt%d" % i, [128, NT]) for i in range(2)]
            gct = [self.sb(es, "gct%d" % i, [128, NT + 2]) for i in range(2)]
            hht = [self.sb(es, "hht%d" % i, [128, NT + 2]) for i in range(2)]
            yt = [self.sb(es, "yt%d" % i, [128, NT]) for i in range(2)]
            seqs = [(0, T, None)] + [(T + s * LS, LS, s) for s in range(self.NS)]
            k = 0
            for (q0, L, s) in seqs:
                for cc in range(4):
                    for t0 in range(0, L, NT):
                        n = min(NT, L - t0)
                        i = k % 2
                        k += 1
                        g, c_, h, y = gbt[i], gct[i], hht[i], yt[i]
                        gB, cB, hB, yB = P.buf("gbt", i), P.buf("gct", i), P.buf("hht", i), P.buf("yt", i)
                        a = q0 + t0
                        P.dma("sp", g[:, :n], zT[cc * 128:(cc + 1) * 128, a:a + n], writes=[gB])
                        if t0 == 0:
                            P.dma("sp", c_[:, 2:2 + n], zT[512 + cc * 128:512 + (cc + 1) * 128, a:a + n], writes=[cB])
                            P.dma("sp", h[:, 2:2 + n], zT[1024 + cc * 128:1024 + (cc + 1) * 128, a:a + n], writes=[hB])
                        else:
                            P.dma("sp", c_[:, :2 + n], zT[512 + cc * 128:512 + (cc + 1) * 128, a - 2:a + n], writes=[cB])
                            P.dma("sp", h[:, :2 + n], zT[1024 + cc * 128:1024 + (cc + 1) * 128, a - 2:a + n], writes=[hB])
                        lo = 2 if t0 == 0 else 0
                        P.op("pool", lambda e: e.tensor_tensor(out=c_[:, lo:2 + n], in0=c_[:, lo:2 + n], in1=h[:, lo:2 + n],
                                                               op=ALU.mult), reads=[cB, hB], writes=[cB])
                        if t0 == 0:
                            if s is None:
                                P.op("pool", lambda e: e.memset(c_[:, 0:2], 0.0), writes=[cB])
                            else:
                                P.dma("sp", c_[:, 0:2], self.io["state_sc"][s, :, cc * 128:(cc + 1) * 128].rearrange("k p -> p k"),
                                      writes=[cB], allow_slow_non_contiguous=True)
                        P.op("dve", lambda e: e.tensor_scalar(out=y[:, :n], in0=c_[:, 0:n], scalar1=wc[:, cc, 0:1], scalar2=None,
                                                              op0=ALU.mult), reads=[cB, wcb], writes=[yB])
                        P.op("dve", lambda e: e.scalar_tensor_tensor(out=y[:, :n], in0=c_[:, 1:1 + n], scalar=wc[:, cc, 1:2],
                                                                     in1=y[:, :n], op0=ALU.mult, op1=ALU.add),
                             reads=[cB, wcb, yB], writes=[yB])
                        P.op("dve", lambda e: e.scalar_tensor_tensor(out=y[:, :n], in0=c_[:, 2:2 + n], scalar=wc[:, cc, 2:3],
                                                                     in1=y[:, :n], op0=ALU.mult, op1=ALU.add),
                             reads=[cB, wcb, yB], writes=[yB])
                        P.op("pool", lambda e: e.tensor_tensor(out=y[:, :n], in0=y[:, :n], in1=g[:, :n], op=ALU.mult),
                             reads=[yB, gB], writes=[yB])
                        P.dma("pool", self.io["mixA0"][cc * 128:(cc + 1) * 128, a:a + n], y[:, :n], reads=[yB],
                              writes=[P.buf("mixA0", cc, a)])
                        if t0 + n == L:
                            oi = 0 if s is None else 1 + s
                            P.dma("pool", self.io["new_sc"][oi, :, cc * 128:(cc + 1) * 128].rearrange("k p -> p k"),
                                  c_[:, n:n + 2], reads=[cB], writes=[P.buf("new_sc", oi, cc)], allow_slow_non_contiguous=True)
            P.barrier()

    def phase_out_ffn(self, layer, xp_name, xs_name, mixA, mixB, wout_name, h_out, final=False, a_tm=False):
        P = self.P
        T = self.T
        with ExitStack() as es:
            wo = self.sb(es, "wo", [128, 8, D], BF16)
            wg = self.sb(es, "wg", [128, 8, FF], BF16)
            wu = self.sb(es, "wu", [128, 8, FF], BF16)
            wd = self.sb(es, "wd", [128, 22, D], BF16)
            wB = P.buf("ffn_w")
            for c in range(8):
                P.dma("pool", wo[:, c, :], self.io[wout_name].rearrange("(c p) n -> p c n", p=128)[:, c, :], writes=[wB])
            for c in range(8):
                P.dma("pool", wg[:, c, :], self.io["w_gate"][layer].rearrange("(c p) n -> p c n", p=128)[:, c, :], writes=[wB])
                P.dma("pool", wu[:, c, :], self.io["w_up"][layer].rearrange("(c p) n -> p c n", p=128)[:, c, :], writes=[wB])
            for f in range(22):
                P.dma("pool", wd[:, f, :], self.io["w_down"][layer].rearrange("(f p) n -> p f n", p=128)[:, f, :], writes=[wB])
            g_t = self.sb(es, "g_t", [128, 8])
            gb = P.buf("g_t")
            P.dma("sp", g_t[:], self.io["norm_ffn"][layer].rearrange("(c p) -> p c", p=128), writes=[gb],
                  allow_slow_non_contiguous=True)
            if final:
                gfin = self.sb(es, "gfin", [128, D])
                P.dma("sp", gfin[:], self.io["norm_final"].partition_broadcast(128), writes=[P.buf("gfin")])
            ST = 256
            NSUB = ST // 128
            hnT = self.sb(es, "hnT", [128, 8, ST], BF16)
            h1 = self.sb(es, "h1", [128, NSUB, D])
            actT = self.sb(es, "actT", [128, 22, ST], BF16)
            xt = self.sb(es, "xt", [128, D])
            tmp = dict(junk=self.sb(es, "junk", [128, D], BF16), ss=self.sb(es, "ss", [128, 4]),
                       xn=self.sb(es, "xn", [128, D], BF16))
            yT = self.sb(es, "yT", [128, 8, 128], BF16)
            mb = self.sb(es, "mb", [128, 512])
            sil = [self.sb(es, "sil%d" % i, [128, ST]) for i in range(2)]
            ot = self.sb(es, "ot", [128, D])
            pst = self.ps(es, "pst", [128, 8, 128], BF16)
            ptr = self.ps(es, "ptr", [128, 4, 128])
            po = [self.ps(es, "po%d" % i, [128, 512]) for i in range(2)]
            pg = [self.ps(es, "pg%d" % i, [128, 512]) for i in range(2)]
            pu = [self.ps(es, "pu%d" % i, [128, 512]) for i in range(2)]
            for si, (t0, ntot) in enumerate(self.token_tiles(ST)):
                hb = P.buf("hnT")
                subs = [(j0, min(128, ntot - j0)) for j0 in range(0, ntot, 128)]
                for ji, (j0, n) in enumerate(subs):
                    tt = t0 + j0
                    if not a_tm:
                        P.dma("pool", yT[:, 0:4, :n], self.io[mixA][:, tt:tt + n].rearrange("(c p) n -> p c n", p=128),
                              writes=[P.buf("yT")])
                    else:
                        P.dma("sp", mb[:n, :], self.io[mixA][tt:tt + n, :], writes=[P.buf("mb")])
                        for c in range(4):
                            P.op("pe", lambda e, c=c: e.transpose(out=ptr[:, c, :n], in_=mb[:n, c * 128:(c + 1) * 128],
                                                                  identity=self.ident[:n, :n]),
                                 reads=[P.buf("mb"), P.buf("ident")], writes=[P.buf("ptr")])
                        P.op("act", lambda e: e.copy(out=yT[:, 0:4, :n], in_=ptr[:, :, :n]), reads=[P.buf("ptr")],
                             writes=[P.buf("yT")])
                    P.dma("sp", mb[:n, :], self.io[mixB][tt:tt + n, :], writes=[P.buf("mb")])
                    for c in range(4):
                        P.op("pe", lambda e, c=c: e.transpose(out=ptr[:, c, :n], in_=mb[:n, c * 128:(c + 1) * 128],
                                                              identity=self.ident[:n, :n]),
                             reads=[P.buf("mb"), P.buf("ident")], writes=[P.buf("ptr")])
                    P.op("act", lambda e: e.copy(out=yT[:, 4:8, :n], in_=ptr[:, :, :n]), reads=[P.buf("ptr")],
                         writes=[P.buf("yT")])
                    P.dma("sp", xt[:n, :], self.xsrc(xp_name, xs_name, tt, n), writes=[P.buf("xt")])
                    for half in range(2):
                        for c in range(8):
                            P.op("pe", lambda e, c=c: e.matmul(po[half][:n, :], lhsT=yT[:, c, :n],
                                                               rhs=wo[:, c, half * 512:(half + 1) * 512],
                                                               start=(c == 0), stop=(c == 7)),
                                 reads=[P.buf("yT"), wB], writes=[P.buf("po", half)])
                        P.op("dve", lambda e: e.tensor_tensor(out=h1[:n, ji, half * 512:(half + 1) * 512], in0=po[half][:n, :],
                                                              in1=xt[:n, half * 512:(half + 1) * 512], op=ALU.add),
                             reads=[P.buf("po", half), P.buf("xt")], writes=[P.buf("h1", ji)])
                    self.norm_T(h1[:, ji, :], P.buf("h1", ji), n, g_t, gb, hnT[:, :, j0:j0 + 128], hb, tmp, pst, "C")
                for f in range(22):
                    i = f % 2
                    for c in range(8):
                        P.op("pe", lambda e, c=c: e.matmul(pg[i][:, :ntot], lhsT=wg[:, c, f * 128:(f + 1) * 128],
                                                           rhs=hnT[:, c, :ntot], start=(c == 0), stop=(c == 7)),
                             reads=[hb, wB], writes=[P.buf("pg", i)])
                    for c in range(8):
                        P.op("pe", lambda e, c=c: e.matmul(pu[i][:, :ntot], lhsT=wu[:, c, f * 128:(f + 1) * 128],
                                                           rhs=hnT[:, c, :ntot], start=(c == 0), stop=(c == 7)),
                             reads=[hb, wB], writes=[P.buf("pu", i)])
                    P.op("act", lambda e: e.activation(out=sil[i][:, :ntot], in_=pg[i][:, :ntot], func=AF.Silu),
                         reads=[P.buf("pg", i)], writes=[P.buf("sil", i)])
                    P.op("dve", lambda e: e.tensor_tensor(out=actT[:, f, :ntot], in0=sil[i][:, :ntot], in1=pu[i][:, :ntot],
                                                          op=ALU.mult),
                         reads=[P.buf("sil", i), P.buf("pu", i)], writes=[P.buf("actT")])
                for ji, (j0, n) in enumerate(subs):
                    tt = t0 + j0
                    for half in range(2):
                        for f in range(22):
                            P.op("pe", lambda e, f=f: e.matmul(po[half][:n, :], lhsT=actT[:, f, j0:j0 + n],
                                                               rhs=wd[:, f, half * 512:(half + 1) * 512],
                                                               start=(f == 0), stop=(f == 21)),
                                 reads=[P.buf("actT"), wB], writes=[P.buf("po", half)])
                        P.op("dve", lambda e: e.tensor_tensor(out=ot[:n, half * 512:(half + 1) * 512], in0=po[half][:n, :],
                                                              in1=h1[:n, ji, half * 512:(half + 1) * 512], op=ALU.add),
                             reads=[P.buf("po", half), P.buf("h1", ji)], writes=[P.buf("ot")])
                    if not final:
                        P.dma("sp", self.io[h_out][tt:tt + n, :], ot[:n, :], reads=[P.buf("ot")], writes=[P.buf(h_out, tt)])
                    else:
                        ss, ssb = tmp["ss"], P.buf("ssC")
                        P.op("act", lambda e: e.activation(out=tmp["junk"][:n, :], in_=ot[:n, :], func=AF.Square,
                                                           accum_out=ss[:n, 0:1]),
                             reads=[P.buf("ot")], writes=[P.buf("junkC"), ssb])
                        self.rstd(ss, ssb, n)
                        P.op("dve", lambda e: e.scalar_tensor_tensor(out=ot[:n, :], in0=ot[:n, :], scalar=ss[:n, 2:3],
                                                                     in1=gfin[:n, :], op0=ALU.mult, op1=ALU.mult),
                             reads=[P.buf("ot"), ssb, P.buf("gfin")], writes=[P.buf("ot")])
                        P.dma("sp", self.io["y"][tt:tt + n, :], ot[:n, :], reads=[P.buf("ot")], writes=[P.buf("y", tt)])
            P.barrier()

    def phase_zero(self, name, rows, cols):
        P = self.P
        with ExitStack() as es:
            z = self.sb(es, "zero_t", [128, cols])
            P.op("dve", lambda e: e.memset(z[:], 0.0), writes=[P.buf("zero_t")])
            for r0 in range(0, rows, 128):
                n = min(128, rows - r0)
                P.dma("sp", self.io[name][r0:r0 + n, :], z[:n, :], reads=[P.buf("zero_t")], writes=[P.buf(name, r0)])
            P.barrier()


_CACHE = {}


def _prep(inputs):
    x_prompt = np.asarray(inputs["x_prompt"]); x_sample = np.asarray(inputs["x_sample"])
    B, T, _ = x_prompt.shape
    DB, LS, _ = x_sample.shape
    NS = DB // 8
    npg = inputs["page_table"].shape[1]
    npool = inputs["cache_k"].shape[1]
    return B, T, DB, LS, NS, npg, npool


def kernel(_debug=(), **inputs):
    B, T, DB, LS, NS, NPG, NPOOL = _prep(inputs)
    key = (T, NS, LS, NPG, NPOOL, tuple(_debug))
    if key not in _CACHE:
        _CACHE[key] = Model(T, NS, LS, NPG, NPOOL, debug=_debug)
    m = _CACHE[key]
    f = lambda k: np.ascontiguousarray(np.asarray(inputs[k], dtype=np.float32))
    consts = {"c_" + k: v for k, v in const_arrays(LS).items()}
    in_maps = []
    for c in range(8):
        sl = slice(c * NS, (c + 1) * NS)
        d = dict(consts)
        d["xp"] = f("x_prompt")[c % B]
        d["xs"] = f("x_sample")[sl].reshape(NS * LS, D)
        d["norm_mix"] = f("norm_mix"); d["norm_ffn"] = f("norm_ffn"); d["norm_final"] = f("norm_final").reshape(1, D)
        d["w_in0"] = f("w_in0")[0]; d["w_out0"] = f("w_out0")[0]; d["w_in1"] = f("w_in1")[0]; d["w_out1"] = f("w_out1")[0]
        d["w_gate"] = f("w_gate"); d["w_up"] = f("w_up"); d["w_down"] = f("w_down")
        d["sc_conv_w"] = f("sc_conv_w")[0]; d["state_sc"] = f("state_sc")[0, sl]
        d["state_shift"] = f("state_shift")[0, sl]; d["state_wkv"] = f("state_wkv")[0, sl]
        for nm in ("rw_mu", "rw_w0", "rw_a0", "rw_k_k", "rw_k_a", "rw_ln_w", "rw_ln_b"):
            d[nm] = f(nm).reshape(1, -1)
        d["rw_r_k"] = f("rw_r_k").reshape(1, RW)
        d["rw_w2"] = f("rw_w2")[0]; d["rw_a2"] = f("rw_a2")[0]; d["rw_g2"] = f("rw_g2")[0]
        d["cache_k"] = f("cache_k")[0].reshape(-1, 512); d["cache_v"] = f("cache_v")[0].reshape(-1, 512)
        d["cache_logf"] = f("cache_logf")[0].reshape(-1, 1024)
        d["page_table"] = np.ascontiguousarray(np.asarray(inputs["page_table"], dtype=np.int32)[sl])
        d["state_ssm_conv"] = f("state_ssm_conv")[0, sl]; d["state_ssm"] = f("state_ssm")[0, sl]
        d["fox_f_bias"] = f("fox_f_bias"); d["ssm_conv_w"] = f("ssm_conv_w")[0]; d["ssm_conv_b"] = f("ssm_conv_b")
        d["ssm_dt_bias"] = f("ssm_dt_bias"); d["ssm_a_log"] = f("ssm_a_log"); d["ssm_d"] = f("ssm_d"); d["ssm_norm_w"] = f("ssm_norm_w")
        in_maps.append({k: v for k, v in d.items() if k in m.io})
    res = run_bass_kernel_spmd(m.nc, in_maps, core_ids=list(range(8)))
    R = res.results
    if _debug:
        return R
    y_prompt = np.stack([R[b]["y"][:T] for b in range(B)])
    y_sample = np.concatenate([R[c]["y"][T:].reshape(NS, LS, D) for c in range(8)])
    new_sc_p = np.stack([R[b]["new_sc"][0] for b in range(B)])[None]
    new_sc_s = np.concatenate([R[c]["new_sc"][1:] for c in range(8)])[None]
    outs = [y_prompt, y_sample, new_sc_p, new_sc_s]
    z = lambda *sh: np.zeros(sh, np.float32)
    outs += [np.stack([R[b]["new_shift"][0] for b in range(B)])[None],
             np.concatenate([R[c]["new_shift"][1:] for c in range(8)])[None],
             np.stack([R[b]["new_wkv"][0] for b in range(B)])[None],
             np.concatenate([R[c]["new_wkv"][1:] for c in range(8)])[None]]
    def pp(name, shape_tail):
        return np.stack([R[b][name][:T].reshape((T,) + shape_tail) for b in range(B)])[None]

    def ss_(name, shape_tail):
        return np.concatenate([R[c][name][T:].reshape((NS, LS) + shape_tail) for c in range(8)])[None]
    outs += [pp("new_k", (NH, HD)), ss_("new_k", (NH, HD)), pp("new_v", (NH, HD)), ss_("new_v", (NH, HD)),
             pp("new_logf", (NH,)), ss_("new_logf", (NH,))]
    for name in ("new_conv", "new_ssm"):
        outs.append(np.stack([R[b][name][0] for b in range(B)])[None])
        outs.append(np.concatenate([R[c][name][1:] for c in range(8)])[None])
    return tuple(outs)


RW_GN_EPS = 64e-5


def _rwkv_phase(self):
    P, T, LS, NS = self.P, self.T, self.LS, self.NS
    zT = self.io["zT0"]
    R0 = 3 * SC
    CMAX = 64
    E5 = float(np.exp(-0.5))
    with ExitStack() as es:
        sb = lambda name, shape, dt=F32: self.sb(es, name, shape, dt)
        B = P.buf
        def ld(name, shape, src, **kw):
            t = sb(name, shape)
            P.dma("sp", t[:], src, writes=[B(name)], **kw)
            return t
        NC = dict(allow_slow_non_contiguous=True)
        mu = self.io["rw_mu"]
        mu_rkv = ld("mu_rkv", [64, 24], mu[0, 0:1536].rearrange("(g n) -> n g", n=64), **NC)
        mu_l = ld("mu_l", [64, 2], mu[0, 1536:1664].rearrange("(g n) -> n g", n=64), **NC)
        mu_g = ld("mu_g", [128, 1], mu[0, 1664:1792].rearrange("(n o) -> n o", o=1), **NC)
        w0row = ld("w0row", [1, 512], self.io["rw_w0"])
        w2 = ld("w2", [64, 512], self.io["rw_w2"])
        a2 = ld("a2", [64, 512], self.io["rw_a2"])
        g2 = ld("g2", [128, 512], self.io["rw_g2"])
        hn = lambda nm: self.io[nm][0].rearrange("(h n) -> n h", n=64)
        a0 = ld("a0", [64, 8], hn("rw_a0"), **NC)
        k_k = ld("k_k", [64, 8], hn("rw_k_k"), **NC)
        k_a = ld("k_a", [64, 8], hn("rw_k_a"), **NC)
        r_k = ld("r_k", [64, 8], hn("rw_r_k"), **NC)
        lnw = ld("lnw", [64, 512], self.io["rw_ln_w"].partition_broadcast(64))
        lnb = ld("lnb", [64, 512], self.io["rw_ln_b"].partition_broadcast(64))
        tri_i = ld("tri_i", [64, 64], self.io["c_triu_incl"][0:64, 0:64])
        tri_s = ld("tri_s", [64, 64], self.io["c_triu_strict"][0:64, 0:64])
        low_s = ld("low_s", [64, 64], self.io["c_tril_strict"][0:64, 0:64])
        ones = ld("ones64", [64, 64], self.io["c_ones"][0:64, 0:64])
        triS_i = sb("triS_i", [64, 64]); triS_s = sb("triS_s", [64, 64]); ntri_i = sb("ntri_i", [64, 64])
        P.op("dve", lambda e: e.tensor_scalar(out=triS_i[:], in0=tri_i[:], scalar1=-E5, scalar2=None, op0=ALU.mult),
             reads=[B("tri_i")], writes=[B("triS_i")])
        P.op("dve", lambda e: e.tensor_scalar(out=triS_s[:], in0=tri_s[:], scalar1=-E5, scalar2=None, op0=ALU.mult),
             reads=[B("tri_s")], writes=[B("triS_s")])
        P.op("dve", lambda e: e.tensor_scalar(out=ntri_i[:], in0=tri_i[:], scalar1=-1.0, scalar2=None, op0=ALU.mult),
             reads=[B("tri_i")], writes=[B("ntri_i")])
        epsg = sb("epsg", [64, 1])
        P.op("dve", lambda e: e.memset(epsg[:], RW_GN_EPS), writes=[B("epsg")])
        ident = self.ident
        def mkset(si):
            S = {}
            sbs = lambda name, shape, dt=F32: sb(name + "_s%d" % si, shape, dt)
            S["zr"] = sbs("zr", [64, 24, CMAX + 1])
            S["zl"] = sbs("zl", [64, 2, CMAX + 1])
            S["zg"] = sbs("zg", [128, CMAX + 1])
            S["dd"] = sbs("dd", [64, 24, CMAX])
            S["dl"] = sbs("dl", [64, 2, CMAX])
            S["lx"] = sbs("lx", [64, 2, CMAX])
            S["th"] = sbs("th", [64, CMAX])
            S["dg"] = sbs("dg", [128, CMAX])
            S["sg"] = sbs("sg", [128, CMAX])
            S["sigw"] = sbs("sigw", [64, 512])
            S["gtm"] = sbs("gtm", [64, 512])
            S["Einc"] = sbs("Einc", [64, 8, CMAX])
            S["Einv"] = sbs("Einv", [64, 8, CMAX])
            S["Eexc"] = sbs("Eexc", [64, 8, CMAX])
            S["gamC"] = sbs("gamC", [64, 8])
            S["av"] = sbs("av", [64, 8, CMAX])
            S["kk0"] = sbs("kk0", [64, 8, CMAX])
            S["rn"] = sbs("rn", [64, 8, CMAX])
            S["kp"] = sbs("kp", [64, 8, CMAX])
            S["bb"] = sbs("bb", [64, 8, CMAX])
            S["tt_"] = sbs("tt_", [64, 8, CMAX])
            S["AR"] = sbs("AR", [64, 8, 2, CMAX])
            S["Kt"] = sbs("Kt", [64, 8, CMAX])
            S["Bt"] = sbs("Bt", [64, 8, CMAX])
            S["rk"] = sbs("rk", [64, 8, CMAX])
            S["rks"] = sbs("rks", [64, 8])
            S["Vtm"] = sbs("Vtm", [64, 512])
            S["Ktm"] = sbs("Ktm", [64, 512])
            S["nBtm"] = sbs("nBtm", [64, 512])
            S["Nka"] = sbs("Nka", [64, 8, CMAX])
            S["Mkr"] = sbs("Mkr", [64, 8, CMAX])
            S["nMbr"] = sbs("nMbr", [64, 8, CMAX])
            S["Am"] = [sbs("Am%d" % i, [64, 8, CMAX]) for i in range(2)]
            S["At"] = [sbs("At%d" % i, [64, 8, CMAX]) for i in range(2)]
            S["Pm"] = sbs("Pm", [64, 8, CMAX])
            S["Pt"] = sbs("Pt", [64, 8, CMAX])
            S["WT"] = sbs("WT", [64, 512])
            S["UT"] = sbs("UT", [64, 512])
            S["osb"] = sbs("osb", [64, 8, 64])
            S["sq2"] = sbs("sq2", [64, 8, 64])
            S["yv"] = sbs("yv", [64, 8, 64])
            S["st8"] = sbs("st8", [64, 8, 4])
            S["zx"] = S["dd"]
            S["oc"] = S["osb"]
            S["sq"] = S["tt_"]
            S["kk"] = S["kk0"]
            return S
        sets = [mkset(0), mkset(1)]
        PERSET = {'rks', 'th', 'dg', 'kk', 'nBtm', 'st8', 'kk0', 'zl', 'zx', 'AR', 'Kt', 'Einc', 'sq', 'Pm', 'Bt', 'tt_', 'Pt', 'nMbr', 'osb', 'dd', 'kp', 'Nka', 'Am', 'Eexc', 'At', 'UT', 'bb', 'dl', 'yv', 'rk', 'Einv', 'gamC', 'sg', 'WT', 'av', 'zr', 'oc', 'sq2', 'Ktm', 'gtm', 'Vtm', 'Mkr', 'sigw', 'lx', 'zg', 'rn'}
        ALIAS = {'zx': 'dd', 'oc': 'osb', 'sq': 'tt_', 'kk': 'kk0'}
        B0 = B

        def mkB(si):
            def Bs(n, *a):
                n = ALIAS.get(n, n)
                return B0(n + "@%d" % si, *a) if n in PERSET else B0(n, *a)
            return Bs
        ST = sb("ST", [64, 8, 64]); Snat = sb("Snat", [64, 8, 64])
        banks = [self.ps(es, "bk%d" % i, [128, 512]) for i in range(8)]
        bstate = [0]

        bcnt = [0, 0]

        def mkbank(si):
            def bank_():
                i = si * 4 + (bcnt[si] % 4)
                bcnt[si] += 1
                return banks[i], B0("bank", i)
            return bank_

        def bc3(t2, C, n=64, g=8):
            return t2[:n, :g].unsqueeze(2).to_broadcast([n, g, C])

        done = [0]

        def chunk(gi, q0, L, s, c0, si):
            S = sets[si]
            B = mkB(si)
            bank = mkbank(si)
            zr = S["zr"]
            zl = S["zl"]
            zg = S["zg"]
            dd = S["dd"]
            zx = S["zx"]
            dl = S["dl"]
            lx = S["lx"]
            th = S["th"]
            dg = S["dg"]
            sg = S["sg"]
            sigw = S["sigw"]
            gtm = S["gtm"]
            Einc = S["Einc"]
            Einv = S["Einv"]
            Eexc = S["Eexc"]
            gamC = S["gamC"]
            av = S["av"]
            kk0 = S["kk0"]
            sq = S["sq"]
            rn = S["rn"]
            kk = S["kk"]
            kp = S["kp"]
            bb = S["bb"]
            tt_ = S["tt_"]
            AR = S["AR"]
            Kt = S["Kt"]
            Bt = S["Bt"]
            rk = S["rk"]
            rks = S["rks"]
            Vtm = S["Vtm"]
            Ktm = S["Ktm"]
            nBtm = S["nBtm"]
            Nka = S["Nka"]
            Mkr = S["Mkr"]
            nMbr = S["nMbr"]
            Am = S["Am"]
            At = S["At"]
            Pm = S["Pm"]
            Pt = S["Pt"]
            WT = S["WT"]
            UT = S["UT"]
            osb = S["osb"]
            oc = S["oc"]
            sq2 = S["sq2"]
            yv = S["yv"]
            st8 = S["st8"]
            oi = 0 if s is None else 1 + s
            C = min(CMAX, L - c0)
            a = q0 + c0
            NIT = max(int(np.ceil(np.log2(C))) - 1, 0)
            first = (c0 == 0)
            yield
            rsrc = zT[R0:R0 + 1536, :].rearrange("(g n) t -> n g t", n=64)
            lsrc = zT[R0 + 1536:R0 + 1664, :].rearrange("(g n) t -> n g t", n=64)
            gsrc = zT[R0 + 1664:R0 + 1792, :]
            if first:
                P.dma("sp", zr[:, :, 1:C + 1], rsrc[:, :, a:a + C], writes=[B("zr")])
                P.dma("sp", zl[:, :, 1:C + 1], lsrc[:, :, a:a + C], writes=[B("zl")])
                P.dma("sp", zg[:, 1:C + 1], gsrc[:, a:a + C], writes=[B("zg")])
                if s is None:
                    P.op("pool", lambda e: e.memset(zr[:, :, 0:1], 0.0), writes=[B("zr")])
                    P.op("pool", lambda e: e.memset(zl[:, :, 0:1], 0.0), writes=[B("zl")])
                    P.op("pool", lambda e: e.memset(zg[:, 0:1], 0.0), writes=[B("zg")])
                else:
                    ss_ = self.io["state_shift"][s]
                    P.dma("sp", zr[:, :, 0], ss_[0:1536].rearrange("(g n) -> n g", n=64), writes=[B("zr")], **NC)
                    P.dma("sp", zl[:, :, 0], ss_[1536:1664].rearrange("(g n) -> n g", n=64), writes=[B("zl")], **NC)
                    P.dma("sp", zg[:, 0:1], ss_[1664:1792].rearrange("(n o) -> n o", o=1), writes=[B("zg")], **NC)
            else:
                P.dma("sp", zr[:, :, 0:C + 1], rsrc[:, :, a - 1:a + C], writes=[B("zr")])
                P.dma("sp", zl[:, :, 0:C + 1], lsrc[:, :, a - 1:a + C], writes=[B("zl")])
                P.dma("sp", zg[:, 0:C + 1], gsrc[:, a - 1:a + C], writes=[B("zg")])
            yield
            P.op("dve", lambda e: e.tensor_tensor(out=dd[:, :, :C], in0=zr[:, :, 0:C], in1=zr[:, :, 1:C + 1], op=ALU.subtract),
                 reads=[B("zr")], writes=[B("dd")])
            P.op("pool", lambda e: e.tensor_tensor(out=dd[:, :, :C], in0=dd[:, :, :C], in1=bc3(mu_rkv, C, 64, 24), op=ALU.mult),
                 reads=[B("dd"), B("mu_rkv")], writes=[B("dd")])
            P.op("dve", lambda e: e.tensor_tensor(out=zx[:, :, :C], in0=dd[:, :, :C], in1=zr[:, :, 1:C + 1], op=ALU.add),
                 reads=[B("dd"), B("zr")], writes=[B("zx")])
            rx, kx, vx = zx[:, 0:8, :C], zx[:, 8:16, :C], zx[:, 16:24, :C]
            P.op("pool", lambda e: e.tensor_tensor(out=dl[:, :, :C], in0=zl[:, :, 0:C], in1=zl[:, :, 1:C + 1], op=ALU.subtract),
                 reads=[B("zl")], writes=[B("dl")])
            P.op("pool", lambda e: e.tensor_tensor(out=dl[:, :, :C], in0=dl[:, :, :C], in1=bc3(mu_l, C, 64, 2), op=ALU.mult),
                 reads=[B("dl"), B("mu_l")], writes=[B("dl")])
            P.op("pool", lambda e: e.tensor_tensor(out=lx[:, :, :C], in0=dl[:, :, :C], in1=zl[:, :, 1:C + 1], op=ALU.add),
                 reads=[B("dl"), B("zl")], writes=[B("lx")])
            P.op("pool", lambda e: e.tensor_tensor(out=dg[:, :C], in0=zg[:, 0:C], in1=zg[:, 1:C + 1], op=ALU.subtract),
                 reads=[B("zg")], writes=[B("dg")])
            P.op("dve", lambda e: e.scalar_tensor_tensor(out=dg[:, :C], in0=dg[:, :C], scalar=mu_g[:, 0:1], in1=zg[:, 1:C + 1],
                                                         op0=ALU.mult, op1=ALU.add),
                 reads=[B("dg"), B("mu_g"), B("zg")], writes=[B("dg")])
            P.op("act", lambda e: e.activation(out=th[:, :C], in_=lx[:, 0, :C], func=AF.Tanh), reads=[B("lx")], writes=[B("th")])
            P.op("act", lambda e: e.activation(out=sg[:, :C], in_=dg[:, :C], func=AF.Sigmoid), reads=[B("dg")], writes=[B("sg")])
            yield
            pw, pwb = bank()
            P.op("pe", lambda e: e.matmul(pw[:C, :], lhsT=th[:, :C], rhs=w2[:, :], start=True, stop=False),
                 reads=[B("th"), B("w2")], writes=[pwb])
            P.op("pe", lambda e: e.matmul(pw[:C, :], lhsT=ones[0:1, :C], rhs=w0row[0:1, :], start=False, stop=True),
                 reads=[B("ones64"), B("w0row")], writes=[pwb])
            P.op("act", lambda e: e.activation(out=sigw[:C, :], in_=pw[:C, :], func=AF.Sigmoid), reads=[pwb], writes=[B("sigw")])
            pcl, pclb = bank()
            pce, pceb = bank()
            for h in range(8):
                P.op("pe", lambda e, h=h: e.matmul(pcl[:64, h * C:(h + 1) * C], lhsT=sigw[:C, h * 64:(h + 1) * 64], rhs=triS_i[:C, :C],
                                                   start=True, stop=True), reads=[B("sigw"), B("triS_i")], writes=[pclb])
                P.op("pe", lambda e, h=h: e.matmul(pce[:64, h * C:(h + 1) * C], lhsT=sigw[:C, h * 64:(h + 1) * 64], rhs=triS_s[:C, :C],
                                                   start=True, stop=True), reads=[B("sigw"), B("triS_s")], writes=[pceb])
            v3 = lambda ap: ap.rearrange("n (h c) -> n h c", h=8)
            P.op("act", lambda e: e.activation(out=Einc[:, :, :C], in_=v3(pcl[:64, :8 * C]), func=AF.Exp), reads=[pclb], writes=[B("Einc")])
            P.op("act", lambda e: e.activation(out=Einv[:, :, :C], in_=v3(pcl[:64, :8 * C]), func=AF.Exp, scale=-1.0), reads=[pclb], writes=[B("Einv")])
            P.op("act", lambda e: e.activation(out=Eexc[:, :, :C], in_=v3(pce[:64, :8 * C]), func=AF.Exp), reads=[pceb], writes=[B("Eexc")])
            P.op("pool", lambda e: e.tensor_copy(out=gamC[:, :], in_=Einc[:, :, C - 1]), reads=[B("Einc")], writes=[B("gamC")])
            yield
            pa, pab = bank()
            for h in range(8):
                P.op("pe", lambda e, h=h: e.matmul(pa[:64, h * C:(h + 1) * C], lhsT=a2[:, h * 64:(h + 1) * 64], rhs=lx[:, 1, :C],
                                                   start=True, stop=True), reads=[B("a2"), B("lx")], writes=[pab])
            P.op("dve", lambda e: e.tensor_tensor(out=av[:, :, :C], in0=v3(pa[:64, :8 * C]), in1=bc3(a0, C), op=ALU.add),
                 reads=[pab, B("a0")], writes=[B("av")])
            P.op("act", lambda e: e.activation(out=av[:, :, :C], in_=av[:, :, :C], func=AF.Sigmoid), reads=[B("av")], writes=[B("av")])
            pg_, pgb = bank()
            P.op("pe", lambda e: e.matmul(pg_[:C, :], lhsT=sg[:, :C], rhs=g2[:, :], start=True, stop=True),
                 reads=[B("sg"), B("g2")], writes=[pgb])
            P.op("act", lambda e: e.copy(out=gtm[:C, :], in_=pg_[:C, :]), reads=[pgb], writes=[B("gtm")])
            yield
            P.op("pool", lambda e: e.tensor_tensor(out=kk0[:, :, :C], in0=kx, in1=bc3(k_k, C), op=ALU.mult),
                 reads=[B("zx"), B("k_k")], writes=[B("kk0")])
            P.op("pool", lambda e: e.tensor_tensor(out=sq[:, :, :C], in0=kk0[:, :, :C], in1=kk0[:, :, :C], op=ALU.mult),
                 reads=[B("kk0")], writes=[B("sq")])
            pss, pssb = bank()
            for h in range(8):
                P.op("pe", lambda e, h=h: e.matmul(pss[:64, h * C:(h + 1) * C], lhsT=ones[:, :], rhs=sq[:, h, :C], start=True, stop=True),
                     reads=[B("ones64"), B("sq")], writes=[pssb])
            P.op("act", lambda e: e.activation(out=rn[:, :, :C], in_=v3(pss[:64, :8 * C]), func=AF.Sqrt), reads=[pssb], writes=[B("rn")])
            P.op("dve", lambda e: e.tensor_scalar(out=rn[:, :, :C], in0=rn[:, :, :C], scalar1=1e-12, scalar2=None, op0=ALU.max),
                 reads=[B("rn")], writes=[B("rn")])
            P.op("dve", lambda e: e.reciprocal(out=rn[:, :, :C], in_=rn[:, :, :C]), reads=[B("rn")], writes=[B("rn")])
            P.op("dve", lambda e: e.tensor_tensor(out=kk[:, :, :C], in0=kk0[:, :, :C], in1=rn[:, :, :C], op=ALU.mult),
                 reads=[B("kk0"), B("rn")], writes=[B("kk")])
            P.op("dve", lambda e: e.scalar_tensor_tensor(out=tt_[:, :, :C], in0=av[:, :, :C], scalar=-1.0, in1=bc3(k_a, C),
                                                         op0=ALU.add, op1=ALU.mult), reads=[B("av"), B("k_a")], writes=[B("tt_")])
            P.op("dve", lambda e: e.scalar_tensor_tensor(out=kp[:, :, :C], in0=tt_[:, :, :C], scalar=1.0, in1=kx,
                                                         op0=ALU.add, op1=ALU.mult), reads=[B("tt_"), B("zx")], writes=[B("kp")])
            P.op("pool", lambda e: e.tensor_tensor(out=bb[:, :, :C], in0=kk[:, :, :C], in1=av[:, :, :C], op=ALU.mult),
                 reads=[B("kk"), B("av")], writes=[B("bb")])
            P.op("pool", lambda e: e.tensor_tensor(out=AR[:, :, 0, :C], in0=kk[:, :, :C], in1=Eexc[:, :, :C], op=ALU.mult),
                 reads=[B("kk"), B("Eexc")], writes=[B("AR")])
            P.op("dve", lambda e: e.tensor_tensor(out=AR[:, :, 1, :C], in0=rx, in1=Einc[:, :, :C], op=ALU.mult),
                 reads=[B("zx"), B("Einc")], writes=[B("AR")])
            P.op("dve", lambda e: e.tensor_tensor(out=Kt[:, :, :C], in0=kp[:, :, :C], in1=Einv[:, :, :C], op=ALU.mult),
                 reads=[B("kp"), B("Einv")], writes=[B("Kt")])
            P.op("pool", lambda e: e.tensor_tensor(out=Bt[:, :, :C], in0=bb[:, :, :C], in1=Einv[:, :, :C], op=ALU.mult),
                 reads=[B("bb"), B("Einv")], writes=[B("Bt")])
            P.op("pool", lambda e: e.tensor_tensor(out=rk[:, :, :C], in0=rx, in1=kp[:, :, :C], op=ALU.mult),
                 reads=[B("zx"), B("kp")], writes=[B("rk")])
            P.op("pool", lambda e: e.tensor_tensor(out=rk[:, :, :C], in0=rk[:, :, :C], in1=bc3(r_k, C), op=ALU.mult),
                 reads=[B("rk"), B("r_k")], writes=[B("rk")])
            prk, prkb = bank()
            for h in range(8):
                P.op("pe", lambda e, h=h: e.matmul(prk[:C, h:h + 1], lhsT=rk[:, h, :C], rhs=ones[:, 0:1], start=True, stop=True),
                     reads=[B("rk"), B("ones64")], writes=[prkb])
            P.op("act", lambda e: e.copy(out=rks[:C, :], in_=prk[:C, 0:8]), reads=[prkb], writes=[B("rks")])
            yield
            for (src_t, srcb, dst, dstb, scl) in ((vx, B("zx"), Vtm, B("Vtm"), 1.0), (Kt[:, :, :C], B("Kt"), Ktm, B("Ktm"), 1.0),
                                                  (Bt[:, :, :C], B("Bt"), nBtm, B("nBtm"), -1.0)):
                pt, ptb = bank()
                for h in range(8):
                    P.op("pe", lambda e, h=h: e.transpose(out=pt[:C, h * 64:(h + 1) * 64], in_=src_t[:, h, :], identity=ident[:64, :64]),
                         reads=[srcb, B("ident")], writes=[ptb])
                P.op("act", lambda e: e.mul(out=dst[:C, :], in_=pt[:C, :], mul=scl), reads=[ptb], writes=[dstb])
            yield
            for (lh, lhb, dN, dNb, mN, dM, dMb, mM) in ((Kt, B("Kt"), Nka, B("Nka"), tri_s, Mkr, B("Mkr"), tri_i),
                                                       (Bt, B("Bt"), Am[0], B("Am", 0), tri_s, nMbr, B("nMbr"), ntri_i)):
                p0, p0b = bank()
                p1, p1b = bank()
                for h in range(8):
                    pp, ppb = (p0, p0b) if h < 4 else (p1, p1b)
                    hh = h % 4
                    P.op("pe", lambda e, h=h, pp=pp, hh=hh: e.matmul(pp[:C, hh * 2 * C:(hh + 1) * 2 * C].rearrange("i (s c) -> i s c", s=2),
                                                                     lhsT=lh[:, h, :C], rhs=AR[:, h, :, :C], start=True, stop=True),
                         reads=[lhb, B("AR")], writes=[ppb])
                for half, (pp, ppb) in enumerate(((p0, p0b), (p1, p1b))):
                    v4 = pp[:C, :8 * C].rearrange("i (h s c) -> i h s c", h=4, s=2)
                    hs = slice(half * 4, half * 4 + 4)
                    P.op("dve", lambda e, v4=v4, hs=hs: e.tensor_tensor(out=dN[:C, hs, :C], in0=v4[:, :, 0, :],
                                                                         in1=mN[:C, :C].unsqueeze(1).to_broadcast([C, 4, C]), op=ALU.mult),
                         reads=[ppb, B("tri_s")], writes=[dNb])
                    P.op("dve", lambda e, v4=v4, hs=hs: e.tensor_tensor(out=dM[:C, hs, :C], in0=v4[:, :, 1, :],
                                                                         in1=mM[:C, :C].unsqueeze(1).to_broadcast([C, 4, C]), op=ALU.mult),
                         reads=[ppb, B("tri_i"), B("ntri_i")], writes=[dMb])
            pat, patb = bank()
            for h in range(8):
                P.op("pe", lambda e, h=h: e.matmul(pat[:C, h * C:(h + 1) * C], lhsT=AR[:, h, 0, :C], rhs=Bt[:, h, :C], start=True, stop=True),
                     reads=[B("AR"), B("Bt")], writes=[patb])
            vh = lambda ap: ap.rearrange("i (h c) -> i h c", h=8)
            mb8 = lambda m_: m_[:C, :C].unsqueeze(1).to_broadcast([C, 8, C])
            P.op("dve", lambda e: e.tensor_tensor(out=At[0][:C, :, :C], in0=vh(pat[:C, :8 * C]), in1=mb8(low_s), op=ALU.mult),
                 reads=[patb, B("low_s")], writes=[B("At", 0)])
            yield
            P.op("pool", lambda e: e.tensor_tensor(out=Pm[:C, :, :C], in0=mb8(ident), in1=Am[0][:C, :, :C], op=ALU.subtract),
                 reads=[B("ident"), B("Am", 0)], writes=[B("Pm")])
            P.op("pool", lambda e: e.tensor_tensor(out=Pt[:C, :, :C], in0=mb8(ident), in1=At[0][:C, :, :C], op=ALU.subtract),
                 reads=[B("ident"), B("At", 0)], writes=[B("Pt")])
            cur = 0
            for it in range(NIT):
                last = (it == NIT - 1)
                nxt = 1 - cur
                pA, pAb = bank()
                for h in range(8):
                    P.op("pe", lambda e, h=h: e.matmul(pA[:C, h * C:(h + 1) * C], lhsT=At[cur][:C, h, :C], rhs=Am[cur][:C, h, :C],
                                                       start=True, stop=True), reads=[B("At", cur), B("Am", cur)], writes=[pAb])
                P.op("act", lambda e: e.copy(out=Am[nxt][:C, :, :C], in_=vh(pA[:C, :8 * C])), reads=[pAb], writes=[B("Am", nxt)])
                if not last:
                    pB, pBb = bank()
                    for h in range(8):
                        P.op("pe", lambda e, h=h: e.matmul(pB[:C, h * C:(h + 1) * C], lhsT=Am[cur][:C, h, :C], rhs=At[cur][:C, h, :C],
                                                           start=True, stop=True), reads=[B("At", cur), B("Am", cur)], writes=[pBb])
                    P.op("act", lambda e: e.copy(out=At[nxt][:C, :, :C], in_=vh(pB[:C, :8 * C])), reads=[pBb], writes=[B("At", nxt)])
                pP, pPb = bank()
                for h in range(8):
                    P.op("pe", lambda e, h=h: e.matmul(pP[:C, h * C:(h + 1) * C], lhsT=Pt[:C, h, :C], rhs=Am[nxt][:C, h, :C],
                                                       start=True, stop=True), reads=[B("Pt"), B("Am", nxt)], writes=[pPb])
                if not last:
                    pQ, pQb = bank()
                    for h in range(8):
                        P.op("pe", lambda e, h=h: e.matmul(pQ[:C, h * C:(h + 1) * C], lhsT=Am[nxt][:C, h, :C], rhs=Pt[:C, h, :C],
                                                           start=True, stop=True), reads=[B("Pt"), B("Am", nxt)], writes=[pQb])
                P.op("dve", lambda e: e.tensor_tensor(out=Pm[:C, :, :C], in0=Pm[:C, :, :C], in1=vh(pP[:C, :8 * C]), op=ALU.add),
                     reads=[B("Pm"), pPb], writes=[B("Pm")])
                if not last:
                    P.op("dve", lambda e: e.tensor_tensor(out=Pt[:C, :, :C], in0=Pt[:C, :, :C], in1=vh(pQ[:C, :8 * C]), op=ALU.add),
                         reads=[B("Pt"), pQb], writes=[B("Pt")])
                yield
                cur = nxt
            yield
            while done[0] < gi:
                yield
            if c0 == 0:
                if s is None:
                    P.op("dve", lambda e: e.memset(ST[:], 0.0), writes=[B("ST")])
                else:
                    P.dma("sp", Snat[:], self.io["state_wkv"][s].rearrange("h v k -> v h k"), writes=[B("Snat")])
                    pb, pbb = bank()
                    for h in range(8):
                        P.op("pe", lambda e, h=h: e.transpose(out=pb[:64, h * 64:(h + 1) * 64], in_=Snat[:, h, :], identity=ident[:64, :64]),
                             reads=[B("Snat"), B("ident")], writes=[pbb])
                    P.op("dve", lambda e: e.tensor_copy(out=ST[:].rearrange("n h v -> n (h v)"), in_=pb[:64, :]), reads=[pbb], writes=[B("ST")])
            yield
            hv = lambda t, h: t[:C, h * 64:(h + 1) * 64]
            pW, pWb = bank()
            for h in range(8):
                P.op("pe", lambda e, h=h: e.matmul(hv(pW, h), lhsT=AR[:, h, 0, :C], rhs=ST[:, h, :], start=True, stop=False),
                     reads=[B("AR"), B("ST")], writes=[pWb])
                P.op("pe", lambda e, h=h: e.matmul(hv(pW, h), lhsT=Nka[:C, h, :C], rhs=hv(Vtm, h), start=False, stop=True),
                     reads=[B("Nka"), B("Vtm")], writes=[pWb])
            P.op("act", lambda e: e.copy(out=WT[:C, :], in_=pW[:C, :]), reads=[pWb], writes=[B("WT")])
            yield
            pU, pUb = bank()
            for h in range(8):
                P.op("pe", lambda e, h=h: e.matmul(hv(pU, h), lhsT=Pm[:C, h, :C], rhs=hv(WT, h), start=True, stop=True),
                     reads=[B("Pm"), B("WT")], writes=[pUb])
            P.op("act", lambda e: e.copy(out=UT[:C, :], in_=pU[:C, :]), reads=[pUb], writes=[B("UT")])
            yield
            pO, pOb = bank()
            for h in range(8):
                P.op("pe", lambda e, h=h: e.matmul(hv(pO, h), lhsT=AR[:, h, 1, :C], rhs=ST[:, h, :], start=True, stop=False),
                     reads=[B("AR"), B("ST")], writes=[pOb])
                P.op("pe", lambda e, h=h: e.matmul(hv(pO, h), lhsT=Mkr[:C, h, :C], rhs=hv(Vtm, h), start=False, stop=False),
                     reads=[B("Mkr"), B("Vtm")], writes=[pOb])
                P.op("pe", lambda e, h=h: e.matmul(hv(pO, h), lhsT=nMbr[:C, h, :C], rhs=hv(UT, h), start=False, stop=True),
                     reads=[B("nMbr"), B("UT")], writes=[pOb])
            P.op("act", lambda e: e.copy(out=osb[:C].rearrange("t h v -> t (h v)"), in_=pO[:C, :]), reads=[pOb], writes=[B("osb")])
            yield
            pS, pSb = bank()
            for h in range(8):
                P.op("pe", lambda e, h=h: e.matmul(pS[:64, h * 64:(h + 1) * 64], lhsT=hv(Ktm, h), rhs=hv(Vtm, h), start=True, stop=False),
                     reads=[B("Ktm"), B("Vtm")], writes=[pSb])
                P.op("pe", lambda e, h=h: e.matmul(pS[:64, h * 64:(h + 1) * 64], lhsT=hv(nBtm, h), rhs=hv(UT, h), start=False, stop=True),
                     reads=[B("nBtm"), B("UT")], writes=[pSb])
            P.op("dve", lambda e: e.tensor_tensor(out=ST[:], in0=ST[:], in1=pS[:64, :].rearrange("n (h v) -> n h v", h=8), op=ALU.add),
                 reads=[B("ST"), pSb], writes=[B("ST")])
            P.op("dve", lambda e: e.tensor_tensor(out=ST[:], in0=ST[:], in1=bc3(gamC, 64), op=ALU.mult),
                 reads=[B("ST"), B("gamC")], writes=[B("ST")])
            if c0 + C == L:
                pb, pbb = bank()
                for h in range(8):
                    P.op("pe", lambda e, h=h: e.transpose(out=pb[:64, h * 64:(h + 1) * 64], in_=ST[:, h, :], identity=ident[:64, :64]),
                         reads=[B("ST"), B("ident")], writes=[pbb])
                P.op("act", lambda e: e.copy(out=Snat[:].rearrange("v h k -> v (h k)"), in_=pb[:64, :]), reads=[pbb], writes=[B("Snat")])
                P.dma("pool", self.io["new_wkv"][oi].rearrange("h v k -> v h k"), Snat[:], reads=[B("Snat")], writes=[B("new_wkv", oi)])
                tl = q0 + L - 1
                P.dma("pool", self.io["new_shift"][oi].rearrange("(c p) -> p c", p=128),
                      zT[R0:R0 + RWS, tl:tl + 1].rearrange("(c p) o -> p (c o)", p=128), writes=[B("new_shift", oi)], **NC)
            done[0] += 1
            yield
            b8 = lambda col, w=64: st8[:C, :, col:col + 1].to_broadcast([C, 8, w])
            P.op("dve", lambda e: e.tensor_reduce(out=st8[:C, :, 0], in_=osb[:C], axis=AX.X, op=ALU.add), reads=[B("osb")], writes=[B("st8")])
            P.op("dve", lambda e: e.tensor_scalar(out=st8[:C, :, 0], in0=st8[:C, :, 0], scalar1=-1.0 / 64, scalar2=None, op0=ALU.mult),
                 reads=[B("st8")], writes=[B("st8")])
            P.op("dve", lambda e: e.tensor_tensor(out=oc[:C], in0=osb[:C], in1=b8(0), op=ALU.add), reads=[B("osb"), B("st8")], writes=[B("oc")])
            P.op("pool", lambda e: e.tensor_tensor(out=sq2[:C], in0=oc[:C], in1=oc[:C], op=ALU.mult), reads=[B("oc")], writes=[B("sq2")])
            P.op("dve", lambda e: e.tensor_reduce(out=st8[:C, :, 1], in_=sq2[:C], axis=AX.X, op=ALU.add), reads=[B("sq2")], writes=[B("st8")])
            P.op("act", lambda e: e.activation(out=st8[:C, :, 2], in_=st8[:C, :, 1], func=AF.Sqrt, bias=epsg[:C, 0:1], scale=1.0 / 64),
                 reads=[B("st8"), B("epsg")], writes=[B("st8")])
            P.op("dve", lambda e: e.reciprocal(out=st8[:C, :, 3], in_=st8[:C, :, 2]), reads=[B("st8")], writes=[B("st8")])
            P.op("dve", lambda e: e.tensor_tensor(out=yv[:C], in0=oc[:C], in1=b8(3), op=ALU.mult), reads=[B("oc"), B("st8")], writes=[B("yv")])
            f3 = lambda t: t[:C, :].rearrange("t (h v) -> t h v", h=8)
            P.op("pool", lambda e: e.tensor_tensor(out=yv[:C], in0=yv[:C], in1=f3(lnw), op=ALU.mult), reads=[B("yv"), B("lnw")], writes=[B("yv")])
            P.op("pool", lambda e: e.tensor_tensor(out=yv[:C], in0=yv[:C], in1=f3(lnb), op=ALU.add), reads=[B("yv"), B("lnb")], writes=[B("yv")])
            P.op("dve", lambda e: e.tensor_tensor(out=sq2[:C], in0=f3(Vtm), in1=rks[:C, :].unsqueeze(2).to_broadcast([C, 8, 64]), op=ALU.mult),
                 reads=[B("Vtm"), B("rks")], writes=[B("sq2")])
            P.op("dve", lambda e: e.tensor_tensor(out=yv[:C], in0=yv[:C], in1=sq2[:C], op=ALU.add), reads=[B("yv"), B("sq2")], writes=[B("yv")])
            P.op("dve", lambda e: e.tensor_tensor(out=yv[:C], in0=yv[:C], in1=f3(gtm), op=ALU.mult), reads=[B("yv"), B("gtm")], writes=[B("yv")])
            P.dma("pool", self.io["mixB0"][a:a + C, :], yv[:C].rearrange("t h v -> t (h v)"), reads=[B("yv")], writes=[B("mixB0", a)])

        seqs = [(0, T, None)] + [(T + s * LS, LS, s) for s in range(NS)]
        work = [(q0, L_, s, c0) for (q0, L_, s) in seqs for c0 in range(0, L_, CMAX)]
        active = []
        nxt_i = 0
        KSTREAM = 2
        while active or nxt_i < len(work):
            while len(active) < KSTREAM and nxt_i < len(work):
                q0, L_, s, c0 = work[nxt_i]
                active.append(chunk(nxt_i, q0, L_, s, c0, nxt_i % 2))
                nxt_i += 1
            for g_ in list(active):
                try:
                    next(g_)
                except StopIteration:
                    active.remove(g_)
        P.barrier()


Model.phase_rwkv = _rwkv_phase


def _fox_phase(self):
    P, T, LS, NS, NPG = self.P, self.T, self.LS, self.NS, self.NPG
    TS, TT = self.TS, self.TT
    zT = self.io["zT1"]
    NB = T // 128
    NC = dict(allow_slow_non_contiguous=True)
    B = P.buf
    with ExitStack() as es:
        sb = lambda name, shape, dt=F32: self.sb(es, name, shape, dt)

        def ld(name, shape, src, dt=F32, q="sp", **kw):
            t = sb(name, shape, dt)
            P.dma(q, t[:], src, writes=[B(name)], **kw)
            return t
        ident, identb = self.ident, self.identb
        tri_i = ld("f_tri_i", [128, 128], self.io["c_triu_incl"])
        tri_s = ld("f_tri_s", [128, 128], self.io["c_triu_strict"])
        ones = ld("f_ones", [128, 128], self.io["c_ones"])
        sel = ld("f_sel", [128, 128], self.io["c_sel_last"])
        blktri = ld("f_blktri", [128, 128], self.io["c_blktri"])
        negf = ld("f_negf", [128, 128], self.io["c_negmask"])
        negb = ld("f_negb", [128, 128], self.io["c_negmask"], BF16, q="pool")
        fb = ld("f_fb", [128, 8], self.io["fox_f_bias"].partition_broadcast(128))
        iota_i = ld("f_iota", [128, 1], self.io["c_iota"], I32)
        iota_f = sb("f_iotaf", [128, 1])
        P.op("dve", lambda e: e.tensor_copy(out=iota_f[:], in_=iota_i[:]), reads=[B("f_iota")], writes=[B("f_iotaf")])
        banks = [self.ps(es, "fbk%d" % i, [128, 512]) for i in range(5)]
        bankb = [self.ps(es, "fbkb%d" % i, [128, 1024], BF16) for i in range(1)]
        bstate = [0]

        def bank():
            i = bstate[0] % 5
            bstate[0] += 1
            return banks[i], B("fbank", i)

        LF = sb("LF", [128, max(NB, 1), 8])
        lft = sb("lft", [128, 8]); lfs = sb("lfs", [128, 8])
        for (tt, n) in self.token_tiles(128):
            P.dma("sp", lft[:n, :], self.io["fpre"][tt:tt + n, :], writes=[B("lft")])
            P.op("dve", lambda e: e.tensor_tensor(out=lft[:n, :], in0=lft[:n, :], in1=fb[:n, :], op=ALU.add), reads=[B("lft"), B("f_fb")], writes=[B("lft")])
            P.op("act", lambda e: e.activation(out=lft[:n, :], in_=lft[:n, :], func=AF.Sigmoid), reads=[B("lft")], writes=[B("lft")])
            dst = LF[:n, tt // 128, :] if tt < T else lfs[:n, :]
            dstb = B("LF") if tt < T else B("lfs")
            P.op("act", lambda e: e.activation(out=dst, in_=lft[:n, :], func=AF.Ln), reads=[B("lft")], writes=[dstb])
            P.dma("pool", self.io["new_logf"][tt:tt + n, :], dst, reads=[dstb], writes=[B("new_logf", tt)])

        tot = sb("tot", [128, 8]); excl = sb("excl", [128, 8]); rhs3 = sb("rhs3", [128, 128, 8])

        def cumsum2(src, srcb, dst, dstb, nb):
            cols = nb * 8
            pt_, ptb_ = bank()
            for h in range(8):
                P.op("pe", lambda e, h=h: e.matmul(pt_[:nb, h:h + 1], lhsT=src[:, :nb, h], rhs=ones[:, 0:1], start=True, stop=True),
                     reads=[srcb, B("f_ones")], writes=[ptb_])
            P.op("act", lambda e: e.copy(out=tot[:nb, :], in_=pt_[:nb, 0:8]), reads=[ptb_], writes=[B("tot")])
            pe_, peb_ = bank()
            P.op("pe", lambda e: e.matmul(pe_[:nb, 0:8], lhsT=tri_s[:nb, :nb], rhs=tot[:nb, :], start=True, stop=True),
                 reads=[B("tot"), B("f_tri_s")], writes=[peb_])
            P.op("act", lambda e: e.copy(out=excl[:nb, :], in_=pe_[:nb, 0:8]), reads=[peb_], writes=[B("excl")])
            P.op("dve", lambda e: e.tensor_tensor(out=rhs3[:nb, :nb, :], in0=ident[:nb, :nb].unsqueeze(2).to_broadcast([nb, nb, 8]),
                                                  in1=excl[:nb, :].unsqueeze(1).to_broadcast([nb, nb, 8]), op=ALU.mult),
                 reads=[B("ident"), B("excl")], writes=[B("rhs3")])
            for c0 in range(0, cols, 512):
                cn = min(512, cols - c0)
                b0_, b1_ = c0 // 8, (c0 + cn) // 8
                pw_, pwb_ = bank()
                P.op("pe", lambda e: e.matmul(pw_[:, :cn], lhsT=tri_i[:, :], rhs=src[:, b0_:b1_, :], start=True, stop=False),
                     reads=[srcb, B("f_tri_i")], writes=[pwb_])
                P.op("pe", lambda e: e.matmul(pw_[:, :cn], lhsT=ones[:nb, :], rhs=rhs3[:nb, b0_:b1_, :], start=False, stop=True),
                     reads=[B("rhs3"), B("f_ones")], writes=[pwb_])
                P.op("dve", lambda e: e.tensor_copy(out=dst[:, b0_:b1_, :], in_=pw_[:, :cn].rearrange("p (b h) -> p b h", h=8)),
                     reads=[pwb_], writes=[dstb])

        Ctm = sb("Ctm", [128, NB, 8])
        cumsum2(LF, B("LF"), Ctm, B("Ctm"), NB)
        groups = [(b0, min(b0 + 4, NB)) for b0 in range(0, NB, 4)]
        G = len(groups)
        crefb = sb("crefb", [128, G, 8])
        pc_, pcb_ = bank()
        for g, (b0, b1) in enumerate(groups):
            P.op("pe", lambda e, g=g, b1=b1: e.matmul(pc_[:, g * 8:(g + 1) * 8], lhsT=sel[:, :], rhs=Ctm[:, b1 - 1, :], start=True, stop=True),
                 reads=[B("Ctm"), B("f_sel")], writes=[pcb_])
        P.op("dve", lambda e: e.tensor_copy(out=crefb[:].rearrange("p g h -> p (g h)"), in_=pc_[:, :G * 8]), reads=[pcb_], writes=[B("crefb")])

        QTa = [sb("QTa%d" % i, [65, T], BF16) for i in range(2)]
        KTa = [sb("KTa%d" % i, [65, T], BF16) for i in range(2)]
        Vau = [sb("Vau%d" % i, [128, NB, 65], BF16) for i in range(2)]
        cst = [sb("cst%d" % i, [128, 65]) for i in range(4)]
        biasK = [sb("biasK%d" % i, [128, NB]) for i in range(2)]
        PT = [sb("PT%d" % i, [128, 512], BF16) for i in range(3)]
        rec = sb("rec", [128, 4, 1]); yf = [sb("yf%d" % i, [128, 4, 64]) for i in range(2)]
        for i in range(2):
            P.op("dve", lambda e, i=i: e.memset(KTa[i][64:65, :], 1.0), writes=[B("KTa", i)])
            P.op("pool", lambda e, i=i: e.memset(Vau[i][:, :, 64:65], 1.0), writes=[B("Vau", i)])
        for i in range(4):
            P.op("pool", lambda e, i=i: e.memset(cst[i][:], 0.0), writes=[B("cst", i)])
        pacc = [self.ps(es, "pacc%d" % i, [128, 4, 65]) for i in range(1)]
        ci = 0
        pti = 0
        for h in range(8):
            hi = h % 2
            qa, ka, va = QTa[hi], KTa[hi], Vau[hi]
            P.dma("pool", qa[0:64, :], zT[h * 64:(h + 1) * 64, 0:T], writes=[B("QTa", hi)])
            P.dma("pool", ka[0:64, :], zT[512 + h * 64:512 + (h + 1) * 64, 0:T], writes=[B("KTa", hi)])
            P.dma("pool", va[:, :, 0:64], self.io["new_v"][0:T, h * 64:(h + 1) * 64].rearrange("(b p) d -> p b d", p=128),
                  reads=[B("new_v", tt_) for (tt_, _) in self.token_tiles(128) if tt_ < T], writes=[B("Vau", hi)])
            for g, (b0, b1) in enumerate(groups):
                pq, pqb = bank()
                for b in range(b0, b1):
                    ct, ctb = cst[ci % 4], B("cst", ci % 4)
                    ci += 1
                    P.op("dve", lambda e, b=b, ct=ct: e.tensor_scalar(out=ct[:, 64:65], in0=Ctm[:, b, h:h + 1], scalar1=crefb[:, g, h:h + 1],
                                                                      scalar2=8.0, op0=ALU.subtract, op1=ALU.mult),
                         reads=[B("Ctm"), B("crefb")], writes=[ctb])
                    P.op("pe", lambda e, b=b, ct=ct: e.matmul(pq[0:65, (b - b0) * 128:(b - b0 + 1) * 128], lhsT=ct[:, :], rhs=ident[:, :],
                                                              start=True, stop=True), reads=[ctb, B("ident")], writes=[pqb])
                nq = (b1 - b0) * 128
                P.op("act", lambda e: e.copy(out=qa[64:65, b0 * 128:b1 * 128], in_=pq[64:65, :nq]), reads=[pqb], writes=[B("QTa", hi)])
            for g, (b0, b1) in enumerate(groups):
                nq = (b1 - b0) * 128
                nblk = b1 - b0
                bk, bkb = biasK[g % 2], B("biasK", g % 2)
                P.op("dve", lambda e: e.tensor_scalar(out=bk[:, :b1], in0=Ctm[:, :b1, h], scalar1=crefb[:, g, h:h + 1], scalar2=-1.0,
                                                      op0=ALU.subtract, op1=ALU.mult), reads=[B("Ctm"), B("crefb")], writes=[bkb])
                pa = pacc[0]
                pab = B("pacc", 0)
                first = True

                def qk(kb):
                    n0 = max(kb - b0, 0) * 128
                    ps_, psb_ = bank()
                    diag = kb >= b0
                    P.op("pe", lambda e: e.matmul(ps_[:, n0:nq], lhsT=ka[:, kb * 128:(kb + 1) * 128], rhs=qa[:, b0 * 128 + n0:b1 * 128],
                                                  start=True, stop=not diag), reads=[B("KTa", hi), B("QTa", hi)], writes=[psb_])
                    if diag:
                        P.op("pe", lambda e: e.matmul(ps_[:, n0:n0 + 128], lhsT=identb[:, :], rhs=negb[:, :], start=False, stop=True),
                             reads=[B("identb"), B("f_negb")], writes=[psb_])
                    return (ps_, psb_, n0)
                pend = qk(0)
                for kb in range(b1):
                    nxt = qk(kb + 1) if kb + 1 < b1 else None
                    ps_, psb_, n0 = pend
                    pt, ptb = PT[pti % 3], B("PT", pti % 3)
                    pti += 1
                    P.op("act", lambda e: e.activation(out=pt[:, n0:nq], in_=ps_[:, n0:nq], func=AF.Exp, bias=bk[:, kb:kb + 1], scale=0.125),
                         reads=[psb_, bkb], writes=[ptb])
                    for j in range(n0 // 128, nblk):
                        P.op("pe", lambda e, j=j: e.matmul(pa[:, j, :], lhsT=pt[:, j * 128:(j + 1) * 128], rhs=va[:, kb, :],
                                                           start=first, stop=(kb == b0 + j)), reads=[ptb, B("Vau", hi)], writes=[pab])
                        first = False
                    pend = nxt
                y_, yb_ = yf[g % 2], B("yf", g % 2)
                P.op("dve", lambda e: e.reciprocal(out=rec[:, :nblk, :], in_=pa[:, :nblk, 64:65]), reads=[pab], writes=[B("rec")])
                P.op("dve", lambda e: e.tensor_tensor(out=y_[:, :nblk, :], in0=pa[:, :nblk, 0:64], in1=rec[:, :nblk, :].to_broadcast([128, nblk, 64]),
                                                      op=ALU.mult), reads=[pab, B("rec")], writes=[yb_])
                P.dma("sp", self.io["mixC1"][b0 * 128:b1 * 128, h * 64:(h + 1) * 64].rearrange("(j p) d -> p j d", p=128), y_[:, :nblk, :],
                      reads=[yb_], writes=[B("mixC1", g, h)])

        PTc = sb("PTc", [128, 1], I32); PTb = sb("PTb", [128, NPG], I32); IDXf = sb("IDXf", [128, NPG]); IDX = sb("IDX", [128, NPG], I32)
        LFp = sb("LFp", [128, 128, 8]); LFs2 = sb("LFs2", [128, NPG, 8]); Cp = sb("Cp", [128, NPG, 8])
        cend = sb("cend", [128, 8]); biasP = sb("biasP", [128, NPG, 8])
        Qblk = sb("Qblk", [128, 4, 2 * LS], BF16); KTn = sb("KTn", [128, 4, LS], BF16); Vn = sb("Vn", [LS, 8, 65], BF16)
        lfn = sb("lfn", [LS, 8]); biasN = sb("biasN", [LS, 8])
        Kp = [sb("Kp%d" % i, [128, 512], BF16) for i in range(4)]
        Vp = [sb("Vp%d" % i, [128, 8, 65], BF16) for i in range(4)]
        Vg = [sb("Vg%d" % i, [128, 512], BF16) for i in range(4)]
        KT2 = [sb("KT2%d" % i, [128, 4, 128], BF16) for i in range(3)]
        Sf = [sb("Sf%d" % i, [128, 8, LS]) for i in range(3)]
        PTs = [sb("PTs%d" % i, [128, 8, LS], BF16) for i in range(3)]
        ys = sb("ys", [LS, 8, 64]); recs = sb("recs", [LS, 8, 1])
        for i in range(4):
            P.op("pool", lambda e, i=i: e.memset(Vp[i][:, :, 64:65], 1.0), writes=[B("Vp", i)])
        P.op("pool", lambda e: e.memset(Vn[:, :, 64:65], 1.0), writes=[B("Vn")])
        P.op("pool", lambda e: e.memset(Qblk[:], 0.0), writes=[B("Qblk")])
        paccs = [pacc[0], self.ps(es, "paccs1", [128, 4, 65])]
        ck_flat = self.io["cache_k"]
        cv_flat = self.io["cache_v"].rearrange("r (h d) -> r h d", h=8)
        for s in range(NS):
            a = T + s * LS
            pt_row = self.io["page_table"][s]
            P.dma("sp", PTc[:NPG, :], pt_row.rearrange("(j o) -> j o", o=1), writes=[B("PTc")])
            P.dma("sp", PTb[:, :], self.io["page_table"][s:s + 1, :].partition_broadcast(128), writes=[B("PTb")])
            P.op("dve", lambda e: e.tensor_copy(out=IDXf[:], in_=PTb[:]), reads=[B("PTb")], writes=[B("IDXf")])
            P.op("dve", lambda e: e.tensor_scalar(out=IDXf[:], in0=IDXf[:], scalar1=128.0, scalar2=iota_f[:, 0:1], op0=ALU.mult, op1=ALU.add),
                 reads=[B("IDXf"), B("f_iotaf")], writes=[B("IDXf")])
            P.op("dve", lambda e: e.tensor_copy(out=IDX[:], in_=IDXf[:]), reads=[B("IDXf")], writes=[B("IDX")])
            P.dma_fn("pool", lambda e: e.indirect_dma_start(out=LFp[:NPG].rearrange("j t h -> j (t h)"), out_offset=None, in_=self.io["cache_logf"],
                                                        in_offset=bass.IndirectOffsetOnAxis(ap=PTc[:NPG, 0:1], axis=0)),
                 reads=[B("PTc")], writes=[B("LFp")])
            P.dma("sp", lfn[:, :], self.io["new_logf"][a:a + LS, :], reads=[B("new_logf", tt_) for (tt_, _) in self.token_tiles(128) if tt_ >= T],
                  writes=[B("lfn")])
            for half in range(2):
                ptp, ptpb = bank()
                for hh in range(4):
                    P.op("pe", lambda e, hh=hh: e.transpose(out=ptp[:, hh * NPG:(hh + 1) * NPG], in_=LFp[:NPG, :, half * 4 + hh], identity=ident[:NPG, :NPG]),
                         reads=[B("LFp"), B("ident")], writes=[ptpb])
                P.op("dve", lambda e: e.tensor_copy(out=LFs2[:, :, half * 4:half * 4 + 4].rearrange("t j h -> t h j"),
                                                    in_=ptp[:, :4 * NPG].rearrange("t (h j) -> t h j", h=4)),
                     reads=[ptpb], writes=[B("LFs2")])
            cumsum2(LFs2, B("LFs2"), Cp, B("Cp"), NPG)
            pce, pceb = bank()
            P.op("pe", lambda e: e.matmul(pce[:, 0:8], lhsT=sel[:, :], rhs=Cp[:, NPG - 1, :], start=True, stop=True),
                 reads=[B("Cp"), B("f_sel")], writes=[pceb])
            P.op("act", lambda e: e.copy(out=cend[:], in_=pce[:, 0:8]), reads=[pceb], writes=[B("cend")])
            P.op("dve", lambda e: e.tensor_tensor(out=biasP[:], in0=cend[:, :].unsqueeze(1).to_broadcast([128, NPG, 8]), in1=Cp[:], op=ALU.subtract),
                 reads=[B("cend"), B("Cp")], writes=[B("biasP")])
            pcn, pcnb = bank()
            P.op("pe", lambda e: e.matmul(pcn[:LS, 0:8], lhsT=tri_i[:LS, :LS], rhs=lfn[:, :], start=True, stop=True),
                 reads=[B("lfn"), B("f_tri_i")], writes=[pcnb])
            P.op("act", lambda e: e.mul(out=biasN[:], in_=pcn[:LS, 0:8], mul=-1.0), reads=[pcnb], writes=[B("biasN")])
            qsrc = zT[0:512, a:a + LS].rearrange("(pr p) t -> p pr t", p=128)
            P.dma("pool", Qblk[0:64, :, 0:LS], qsrc[0:64], writes=[B("Qblk")])
            P.dma("pool", Qblk[64:128, :, LS:2 * LS], qsrc[64:128], writes=[B("Qblk")])
            P.dma("pool", KTn[:], zT[512:1024, a:a + LS].rearrange("(pr p) t -> p pr t", p=128), writes=[B("KTn")])
            P.dma("pool", Vn[:, :, 0:64], self.io["new_v"][a:a + LS, :].rearrange("t (h d) -> t h d", h=8),
                  reads=[B("new_v", tt_) for (tt_, _) in self.token_tiles(128) if tt_ >= T], writes=[B("Vn")])
            first = [True, True]

            def stG(j):
                kp, kpb = Kp[j % 4], B("Kp", j % 4)
                vp, vpb = Vp[j % 4], B("Vp", j % 4)
                vg, vgb = Vg[j % 4], B("Vg", j % 4)
                P.dma_fn("pool", lambda e: e.indirect_dma_start(out=kp[:, :], out_offset=None, in_=ck_flat,
                                                                in_offset=bass.IndirectOffsetOnAxis(ap=IDX[:, j:j + 1], axis=0)),
                         reads=[B("IDX")], writes=[kpb])
                P.dma_fn("pool", lambda e: e.indirect_dma_start(out=vg[:, :], out_offset=None, in_=self.io["cache_v"],
                                                                in_offset=bass.IndirectOffsetOnAxis(ap=IDX[:, j:j + 1], axis=0)),
                         reads=[B("IDX")], writes=[vgb])
                P.op("dve", lambda e: e.tensor_copy(out=vp[:, :, 0:64], in_=vg[:, :].rearrange("k (h d) -> k h d", h=8)), reads=[vgb], writes=[vpb])

            def stT(j):
                kp, kpb = Kp[j % 4], B("Kp", j % 4)
                k2, k2b = KT2[j % 3], B("KT2", j % 3)
                pkt, pktb = bankb[0], B("fbankb", 0)
                for pr in range(4):
                    P.op("pe", lambda e, pr=pr: e.transpose(out=pkt[:, pr * 128:(pr + 1) * 128], in_=kp[:, pr * 128:(pr + 1) * 128], identity=identb[:, :]),
                         reads=[kpb, B("identb")], writes=[pktb])
                P.op("dve", lambda e: e.tensor_copy(out=k2[:].rearrange("p a b -> p (a b)"), in_=pkt[:, 0:512]), reads=[pktb], writes=[k2b])

            def stQ(j):
                k2, k2b = KT2[j % 3], B("KT2", j % 3)
                pss, pssb = bank()
                for pr in range(4):
                    P.op("pe", lambda e, pr=pr: e.matmul(pss[:, pr * 2 * LS:(pr + 1) * 2 * LS], lhsT=k2[:, pr, :], rhs=Qblk[:, pr, :], start=True, stop=True),
                         reads=[k2b, B("Qblk")], writes=[pssb])
                sf, sfb = Sf[j % 3], B("Sf", j % 3)
                pts, ptsb = PTs[j % 3], B("PTs", j % 3)
                P.op("dve", lambda e: e.scalar_tensor_tensor(out=sf[:], in0=pss[:, :8 * LS].rearrange("k (h q) -> k h q", h=8), scalar=0.125,
                                                             in1=biasP[:, j, :].unsqueeze(2).to_broadcast([128, 8, LS]), op0=ALU.mult, op1=ALU.add),
                     reads=[pssb, B("biasP")], writes=[sfb])
                P.op("act", lambda e: e.activation(out=pts[:], in_=sf[:], func=AF.Exp), reads=[sfb], writes=[ptsb])

            def stV(j):
                vp, vpb = Vp[j % 4], B("Vp", j % 4)
                pts, ptsb = PTs[j % 3], B("PTs", j % 3)
                for h in range(8):
                    bkk = h // 4
                    P.op("pe", lambda e, h=h, bkk=bkk: e.matmul(paccs[bkk][:LS, h % 4, :], lhsT=pts[:, h, :], rhs=vp[:, h, :], start=first[bkk], stop=False),
                         reads=[ptsb, vpb], writes=[(B("pacc", 0) if bkk == 0 else B("paccs", 1))])
                    first[bkk] = False
            for j in range(min(3, NPG)):
                stG(j)
            for j in range(min(2, NPG)):
                stT(j)
            stQ(0)
            for j in range(NPG):
                if j + 3 < NPG:
                    stG(j + 3)
                if j + 2 < NPG:
                    stT(j + 2)
                if j + 1 < NPG:
                    stQ(j + 1)
                stV(j)
            psn, psnb = bank()
            for pr in range(4):
                P.op("pe", lambda e, pr=pr: e.matmul(psn[:LS, pr * 2 * LS:(pr + 1) * 2 * LS], lhsT=KTn[:, pr, :], rhs=Qblk[:, pr, :], start=True, stop=True),
                     reads=[B("KTn"), B("Qblk")], writes=[psnb])
            sf, sfb = Sf[0], B("Sf", 0)
            pts, ptsb = PTs[0], B("PTs", 0)
            P.op("dve", lambda e: e.scalar_tensor_tensor(out=sf[:LS], in0=psn[:LS, :8 * LS].rearrange("k (h q) -> k h q", h=8), scalar=0.125,
                                                         in1=biasN[:, :].unsqueeze(2).to_broadcast([LS, 8, LS]), op0=ALU.mult, op1=ALU.add),
                 reads=[psnb, B("biasN")], writes=[sfb])
            P.op("dve", lambda e: e.tensor_tensor(out=sf[:LS], in0=sf[:LS], in1=negf[:LS, :LS].unsqueeze(1).to_broadcast([LS, 8, LS]), op=ALU.add),
                 reads=[sfb, B("f_negf")], writes=[sfb])
            P.op("act", lambda e: e.activation(out=pts[:LS], in_=sf[:LS], func=AF.Exp), reads=[sfb], writes=[ptsb])
            for h in range(8):
                bkk = h // 4
                P.op("pe", lambda e, h=h, bkk=bkk: e.matmul(paccs[bkk][:LS, h % 4, :], lhsT=pts[:LS, h, :], rhs=Vn[:, h, :], start=False, stop=True),
                     reads=[ptsb, B("Vn")], writes=[(B("pacc", 0) if bkk == 0 else B("paccs", 1))])
            for bkk in range(2):
                hs = slice(bkk * 4, bkk * 4 + 4)
                P.op("dve", lambda e, bkk=bkk, hs=hs: e.reciprocal(out=recs[:, hs, :], in_=paccs[bkk][:LS, :, 64:65]), reads=[(B("pacc", 0) if bkk == 0 else B("paccs", 1))], writes=[B("recs")])
                P.op("dve", lambda e, bkk=bkk, hs=hs: e.tensor_tensor(out=ys[:, hs, :], in0=paccs[bkk][:LS, :, 0:64],
                                                                      in1=recs[:, hs, :].to_broadcast([LS, 4, 64]), op=ALU.mult),
                     reads=[(B("pacc", 0) if bkk == 0 else B("paccs", 1)), B("recs")], writes=[B("ys")])
            P.dma("sp", self.io["mixC1"][a:a + LS, :], ys[:].rearrange("t h d -> t (h d)"), reads=[B("ys")], writes=[B("mixC1s", s)])
        P.barrier()


Model.phase_fox = _fox_phase


def _ssd_phase(self):
    P, T, LS, NS = self.P, self.T, self.LS, self.NS
    zT = self.io["zT1"]
    X0 = FOX_IN + 512
    NC = dict(allow_slow_non_contiguous=True)
    B = P.buf
    CM = 128
    with ExitStack() as es:
        sb = lambda name, shape, dt=F32: self.sb(es, name, shape, dt)

        def ld(name, shape, src, **kw):
            t = sb(name, shape)
            P.dma("sp", t[:], src, writes=[B(name)], **kw)
            return t
        ident = self.ident
        tri_i = ld("s_tri_i", [128, 128], self.io["c_triu_incl"])
        ones = ld("s_ones", [128, 128], self.io["c_ones"])
        cw = sb("s_cw", [128, 8, 4])
        for k in range(4):
            P.dma("sp", cw[:, :, k], self.io["ssm_conv_w"][k].rearrange("(c p) -> p c", p=128), writes=[B("s_cw")], **NC)
        cb = ld("s_cb", [128, 8], self.io["ssm_conv_b"][0].rearrange("(c p) -> p c", p=128), **NC)
        dtb = ld("s_dtb", [128, 8], self.io["ssm_dt_bias"].partition_broadcast(128))
        Abc = ld("s_A", [128, 8], self.io["ssm_a_log"].partition_broadcast(128))
        Dbc = ld("s_D", [128, 8], self.io["ssm_d"].partition_broadcast(128))
        nw = ld("s_nw", [128, 512], self.io["ssm_norm_w"].partition_broadcast(128))
        P.op("act", lambda e: e.activation(out=Abc[:], in_=Abc[:], func=AF.Exp), reads=[B("s_A")], writes=[B("s_A")])
        P.op("dve", lambda e: e.tensor_scalar(out=Abc[:], in0=Abc[:], scalar1=-1.0, scalar2=None, op0=ALU.mult), reads=[B("s_A")], writes=[B("s_A")])
        def mkset(si):
            S = {}
            sbs = lambda name, shape, dt=F32: sb(name + "_s%d" % si, shape, dt)
            S["U"] = sbs("s_U", [128, 8, CM + 3])
            S["acc"] = sbs("s_acc", [128, 8, CM])
            S["tmp"] = sbs("s_tmp", [128, 8, CM])
            S["xbc"] = sbs("s_xbc", [128, 8, CM])
            S["xtm"] = sbs("s_xtm", [128, 512])
            S["Btm"] = sbs("s_Btm", [128, 2, 128])
            S["dt"] = sbs("s_dt", [128, 8])
            S["av"] = sbs("s_a", [128, 8])
            S["abc"] = sbs("s_abc", [128, 8, CM])
            S["acs"] = sbs("s_acs", [128, 8])
            S["Lm"] = sbs("s_Lm", [128, 8, CM])
            S["CBm"] = sbs("s_CBm", [128, 2, CM])
            S["Gm"] = sbs("s_Gm", [128, 8, CM])
            S["eac"] = sbs("s_eac", [128, 8])
            S["wgt"] = sbs("s_wgt", [128, 8])
            S["eend"] = sbs("s_eend", [128, 8])
            S["yv"] = sbs("s_y", [128, 8, 64])
            S["t2"] = sbs("s_t2", [128, 8, 64])
            S["xw"] = sbs("s_xw", [128, 8, 64])
            S["zg"] = sbs("s_zg", [128, 512])
            S["junk"] = sbs("s_junk", [128, 512])
            S["ss"] = sbs("s_ss", [128, 4])
            return S
        KS = 2
        sets = [mkset(i) for i in range(KS)]
        PERSET = {'s_a', 's_zg', 's_Gm', 's_abc', 's_xbc', 's_t2', 's_acs', 's_tmp', 's_junk', 's_y', 's_wgt', 's_xtm', 's_acc', 's_eend', 's_U', 's_ss', 's_xw', 's_dt', 's_Btm', 's_CBm', 's_Lm', 's_eac'}
        B0 = B

        def mkB(si):
            def Bs(n, *a):
                return B0(n + "@%d" % si, *a) if n in PERSET else B0(n, *a)
            return Bs
        ST = sb("s_ST", [128, 8, 64]); Sn = sb("s_Sn", [128, 4, 128])
        banks = [self.ps(es, "sbk%d" % i, [128, 512]) for i in range(8)]
        bstate = [0]

        bcnt = [0, 0]

        def mkbank(si):
            def bank_():
                i = si * 4 + (bcnt[si] % 4)
                bcnt[si] += 1
                return banks[i], B0("sbank", i)
            return bank_

        done = [0]

        def chunk(gi, q0, L, s, c0, si):
            S = sets[si]
            B = mkB(si)
            bank = mkbank(si)
            U = S["U"]
            acc = S["acc"]
            tmp = S["tmp"]
            xbc = S["xbc"]
            xtm = S["xtm"]
            Btm = S["Btm"]
            dt = S["dt"]
            av = S["av"]
            abc = S["abc"]
            acs = S["acs"]
            Lm = S["Lm"]
            CBm = S["CBm"]
            Gm = S["Gm"]
            eac = S["eac"]
            wgt = S["wgt"]
            eend = S["eend"]
            yv = S["yv"]
            t2 = S["t2"]
            xw = S["xw"]
            zg = S["zg"]
            junk = S["junk"]
            ss = S["ss"]
            oi = 0 if s is None else 1 + s
            C = min(CM, L - c0)
            a = q0 + c0
            usrc = zT[X0:X0 + 1024, :].rearrange("(c p) t -> p c t", p=128)
            if c0 == 0:
                P.dma("sp", U[:, :, 3:3 + C], usrc[:, :, a:a + C], writes=[B("s_U")])
                if s is None:
                    P.op("pool", lambda e: e.memset(U[:, :, 0:3], 0.0), writes=[B("s_U")])
                else:
                    for k in range(3):
                        P.dma("sp", U[:, :, k], self.io["state_ssm_conv"][s, k].rearrange("(c p) -> p c", p=128), writes=[B("s_U")], **NC)
            else:
                P.dma("sp", U[:, :, 0:3 + C], usrc[:, :, a - 3:a + C], writes=[B("s_U")])
            yield
            wb_ = lambda k: cw[:, :, k:k + 1].to_broadcast([128, 8, C])
            P.op("dve", lambda e: e.tensor_tensor(out=acc[:, :, :C], in0=U[:, :, 0:C], in1=wb_(0), op=ALU.mult), reads=[B("s_U"), B("s_cw")], writes=[B("s_acc")])
            for k in range(1, 4):
                P.op("pool", lambda e, k=k: e.tensor_tensor(out=tmp[:, :, :C], in0=U[:, :, k:k + C], in1=wb_(k), op=ALU.mult),
                     reads=[B("s_U"), B("s_cw")], writes=[B("s_tmp")])
                P.op("dve", lambda e: e.tensor_tensor(out=acc[:, :, :C], in0=acc[:, :, :C], in1=tmp[:, :, :C], op=ALU.add),
                     reads=[B("s_acc"), B("s_tmp")], writes=[B("s_acc")])
            P.op("dve", lambda e: e.tensor_tensor(out=acc[:, :, :C], in0=acc[:, :, :C], in1=cb[:, :].unsqueeze(2).to_broadcast([128, 8, C]), op=ALU.add),
                 reads=[B("s_acc"), B("s_cb")], writes=[B("s_acc")])
            P.op("act", lambda e: e.activation(out=xbc[:, :, :C], in_=acc[:, :, :C], func=AF.Silu), reads=[B("s_acc")], writes=[B("s_xbc")])
            yield
            px, pxb = bank()
            for c in range(4):
                P.op("pe", lambda e, c=c: e.transpose(out=px[:C, c * 128:(c + 1) * 128], in_=xbc[:, c, :C], identity=ident[:, :]),
                     reads=[B("s_xbc"), B("ident")], writes=[pxb])
            P.op("act", lambda e: e.copy(out=xtm[:C, :], in_=px[:C, :]), reads=[pxb], writes=[B("s_xtm")])
            pB, pBb = bank()
            for g in range(2):
                P.op("pe", lambda e, g=g: e.transpose(out=pB[:C, g * 128:(g + 1) * 128], in_=xbc[:, 4 + g, :C], identity=ident[:, :]),
                     reads=[B("s_xbc"), B("ident")], writes=[pBb])
            P.op("act", lambda e: e.copy(out=Btm[:C].rearrange("s g n -> s (g n)"), in_=pB[:C, 0:256]), reads=[pBb], writes=[B("s_Btm")])
            yield
            P.dma("sp", dt[:C, :], self.io["dt_tm"][a:a + C, :], writes=[B("s_dt")])
            P.op("dve", lambda e: e.tensor_tensor(out=dt[:C, :], in0=dt[:C, :], in1=dtb[:C, :], op=ALU.add), reads=[B("s_dt"), B("s_dtb")], writes=[B("s_dt")])
            P.op("act", lambda e: e.activation(out=dt[:C, :], in_=dt[:C, :], func=AF.Exp), reads=[B("s_dt")], writes=[B("s_dt")])
            P.op("act", lambda e: e.activation(out=dt[:C, :], in_=dt[:C, :], func=AF.Ln, bias=ones[:C, 0:1], scale=1.0),
                 reads=[B("s_dt"), B("s_ones")], writes=[B("s_dt")])
            P.op("dve", lambda e: e.tensor_tensor(out=av[:C, :], in0=dt[:C, :], in1=Abc[:C, :], op=ALU.mult), reads=[B("s_dt"), B("s_A")], writes=[B("s_a")])
            P.op("pool", lambda e: e.tensor_copy(out=abc[:C, :, :C], in_=av[:C, :].unsqueeze(2).to_broadcast([C, 8, C])), reads=[B("s_a")], writes=[B("s_abc")])
            pac, pacb = bank()
            P.op("pe", lambda e: e.matmul(pac[:C, 0:8], lhsT=tri_i[:C, :C], rhs=av[:C, :], start=True, stop=True), reads=[B("s_a"), B("s_tri_i")], writes=[pacb])
            P.op("pe", lambda e: e.matmul(pac[:, 8:16], lhsT=ones[:C, :], rhs=av[:C, :], start=True, stop=True), reads=[B("s_a"), B("s_ones")], writes=[pacb])
            P.op("act", lambda e: e.copy(out=acs[:C, :], in_=pac[:C, 0:8]), reads=[pacb], writes=[B("s_acs")])
            P.op("act", lambda e: e.activation(out=eac[:C, :], in_=pac[:C, 0:8], func=AF.Exp), reads=[pacb], writes=[B("s_eac")])
            P.op("act", lambda e: e.activation(out=eend[:, :], in_=pac[:, 8:16], func=AF.Exp), reads=[pacb], writes=[B("s_eend")])
            P.op("dve", lambda e: e.tensor_tensor(out=wgt[:C, :], in0=pac[:C, 8:16], in1=acs[:C, :], op=ALU.subtract), reads=[pacb, B("s_acs")], writes=[B("s_wgt")])
            P.op("act", lambda e: e.activation(out=wgt[:C, :], in_=wgt[:C, :], func=AF.Exp), reads=[B("s_wgt")], writes=[B("s_wgt")])
            P.op("dve", lambda e: e.tensor_tensor(out=wgt[:C, :], in0=wgt[:C, :], in1=dt[:C, :], op=ALU.mult), reads=[B("s_wgt"), B("s_dt")], writes=[B("s_wgt")])
            yield
            for half in range(2):
                pb_, pbb_ = bank()
                for hh in range(4):
                    h = half * 4 + hh
                    P.op("pe", lambda e, h=h, hh=hh: e.matmul(pb_[:C, hh * C:(hh + 1) * C], lhsT=abc[:C, h, :C], rhs=tri_i[:C, :C], start=True, stop=True),
                         reads=[B("s_abc"), B("s_tri_i")], writes=[pbb_])
                hs = slice(half * 4, half * 4 + 4)
                P.op("dve", lambda e: e.tensor_tensor(out=Lm[:C, hs, :C], in0=pb_[:C, :4 * C].rearrange("s (h t) -> s h t", h=4),
                                                      in1=acs[:C, hs].unsqueeze(2).to_broadcast([C, 4, C]), op=ALU.subtract),
                     reads=[pbb_, B("s_acs")], writes=[B("s_Lm")])
            P.op("pool", lambda e: e.tensor_scalar(out=Lm[:C, :, :C], in0=Lm[:C, :, :C], scalar1=0.0, scalar2=None, op0=ALU.min), reads=[B("s_Lm")], writes=[B("s_Lm")])
            P.op("act", lambda e: e.activation(out=Lm[:C, :, :C], in_=Lm[:C, :, :C], func=AF.Exp), reads=[B("s_Lm")], writes=[B("s_Lm")])
            pcb, pcbb = bank()
            for g in range(2):
                P.op("pe", lambda e, g=g: e.matmul(pcb[:C, g * C:(g + 1) * C], lhsT=xbc[:, 4 + g, :C], rhs=xbc[:, 6 + g, :C], start=True, stop=True),
                     reads=[B("s_xbc")], writes=[pcbb])
            P.op("dve", lambda e: e.tensor_tensor(out=CBm[:C, :, :C], in0=pcb[:C, :2 * C].rearrange("s (g t) -> s g t", g=2),
                                                  in1=tri_i[:C, :C].unsqueeze(1).to_broadcast([C, 2, C]), op=ALU.mult),
                 reads=[pcbb, B("s_tri_i")], writes=[B("s_CBm")])
            for g in range(2):
                hs = slice(g * 4, g * 4 + 4)
                P.op("dve" if g == 0 else "pool", lambda e, g=g, hs=hs: e.tensor_tensor(out=Gm[:C, hs, :C], in0=Lm[:C, hs, :C],
                                                                                     in1=CBm[:C, g:g + 1, :C].to_broadcast([C, 4, C]), op=ALU.mult),
                     reads=[B("s_Lm"), B("s_CBm")], writes=[B("s_Gm")])
            P.op("dve", lambda e: e.tensor_tensor(out=Gm[:C, :, :C], in0=Gm[:C, :, :C], in1=dt[:C, :].unsqueeze(2).to_broadcast([C, 8, C]), op=ALU.mult),
                 reads=[B("s_Gm"), B("s_dt")], writes=[B("s_Gm")])
            hv = lambda t, h: t[:C, h * 64:(h + 1) * 64]
            P.op("pool", lambda e: e.tensor_tensor(out=xw[:C], in0=xtm[:C, :].rearrange("t (h p) -> t h p", h=8),
                                                   in1=wgt[:C, :].unsqueeze(2).to_broadcast([C, 8, 64]), op=ALU.mult),
                 reads=[B("s_xtm"), B("s_wgt")], writes=[B("s_xw")])
            P.op("pool", lambda e: e.tensor_tensor(out=t2[:C], in0=xtm[:C, :].rearrange("t (h p) -> t h p", h=8),
                                                   in1=Dbc[:C, :].unsqueeze(2).to_broadcast([C, 8, 64]), op=ALU.mult),
                 reads=[B("s_xtm"), B("s_D")], writes=[B("s_t2")])
            P.dma("sp", zg[:C, :], self.io["zg_tm"][a:a + C, :], writes=[B("s_zg")])
            P.op("act", lambda e: e.activation(out=zg[:C, :], in_=zg[:C, :], func=AF.Silu), reads=[B("s_zg")], writes=[B("s_zg")])
            py, pyb = bank()
            for h in range(8):
                P.op("pe", lambda e, h=h: e.matmul(hv(py, h), lhsT=Gm[:C, h, :C], rhs=hv(xtm, h), start=True, stop=True),
                     reads=[B("s_Gm"), B("s_xtm")], writes=[pyb])
            yield
            while done[0] < gi:
                yield
            if c0 == 0:
                if s is None:
                    P.op("dve", lambda e: e.memset(ST[:], 0.0), writes=[B("s_ST")])
                else:
                    P.dma("sp", Sn[:], self.io["state_ssm"][s].rearrange("h p n -> (h p) n").rearrange("(c q) n -> q c n", q=128), writes=[B("s_Sn")])
                    pb, pbb = bank()
                    for c in range(4):
                        P.op("pe", lambda e, c=c: e.transpose(out=pb[:, c * 128:(c + 1) * 128], in_=Sn[:, c, :], identity=ident[:, :]),
                             reads=[B("s_Sn"), B("ident")], writes=[pbb])
                    P.op("dve", lambda e: e.tensor_copy(out=ST[:].rearrange("n h p -> n (h p)"), in_=pb[:, :]), reads=[pbb], writes=[B("s_ST")])
            yield
            pyi, pyib = bank()
            for h in range(8):
                P.op("pe", lambda e, h=h: e.matmul(hv(pyi, h), lhsT=xbc[:, 6 + h // 4, :C], rhs=ST[:, h, :], start=True, stop=True),
                     reads=[B("s_xbc"), B("s_ST")], writes=[pyib])
            v3 = lambda ap: ap.rearrange("t (h p) -> t h p", h=8)
            b8 = lambda t: t[:C, :].unsqueeze(2).to_broadcast([C, 8, 64])
            P.op("dve", lambda e: e.tensor_tensor(out=yv[:C], in0=v3(pyi[:C, :]), in1=b8(eac), op=ALU.mult), reads=[pyib, B("s_eac")], writes=[B("s_y")])
            P.op("dve", lambda e: e.tensor_tensor(out=yv[:C], in0=yv[:C], in1=v3(py[:C, :]), op=ALU.add), reads=[B("s_y"), pyb], writes=[B("s_y")])
            P.op("dve", lambda e: e.tensor_tensor(out=yv[:C], in0=yv[:C], in1=t2[:C], op=ALU.add), reads=[B("s_y"), B("s_t2")], writes=[B("s_y")])
            yield
            pS, pSb = bank()
            for h in range(8):
                P.op("pe", lambda e, h=h: e.matmul(pS[:, h * 64:(h + 1) * 64], lhsT=Btm[:C, h // 4, :], rhs=xw[:C, h, :], start=True, stop=True),
                     reads=[B("s_Btm"), B("s_xw")], writes=[pSb])
            P.op("dve", lambda e: e.tensor_tensor(out=ST[:], in0=ST[:], in1=eend[:, :].unsqueeze(2).to_broadcast([128, 8, 64]), op=ALU.mult),
                 reads=[B("s_ST"), B("s_eend")], writes=[B("s_ST")])
            P.op("dve", lambda e: e.tensor_tensor(out=ST[:], in0=ST[:], in1=pS[:, :].rearrange("n (h p) -> n h p", h=8), op=ALU.add),
                 reads=[B("s_ST"), pSb], writes=[B("s_ST")])
            if c0 + C == L:
                pb, pbb = bank()
                STf = ST[:].rearrange("n h p -> n (h p)")
                for c in range(4):
                    P.op("pe", lambda e, c=c: e.transpose(out=pb[:, c * 128:(c + 1) * 128], in_=STf[:, c * 128:(c + 1) * 128], identity=ident[:, :]),
                         reads=[B("s_ST"), B("ident")], writes=[pbb])
                P.op("act", lambda e: e.copy(out=Sn[:].rearrange("q c n -> q (c n)"), in_=pb[:, :]), reads=[pbb], writes=[B("s_Sn")])
                P.dma("pool", self.io["new_ssm"][oi].rearrange("h p n -> (h p) n").rearrange("(c q) n -> q c n", q=128), Sn[:], reads=[B("s_Sn")],
                      writes=[B("new_ssm", oi)])
            done[0] += 1
            yield
            yf_ = yv[:C].rearrange("t h p -> t (h p)")
            P.op("dve", lambda e: e.tensor_tensor(out=yf_, in0=yf_, in1=zg[:C, :], op=ALU.mult), reads=[B("s_y"), B("s_zg")], writes=[B("s_y")])
            P.op("act", lambda e: e.activation(out=junk[:C, :], in_=yf_, func=AF.Square, accum_out=ss[:C, 0:1]), reads=[B("s_y")], writes=[B("s_junk"), B("s_ss")])
            self.rstd(ss, B("s_ss"), C, inv=1.0 / 512)
            P.op("dve", lambda e: e.scalar_tensor_tensor(out=yf_, in0=yf_, scalar=ss[:C, 2:3], in1=nw[:C, :], op0=ALU.mult, op1=ALU.mult),
                 reads=[B("s_y"), B("s_ss"), B("s_nw")], writes=[B("s_y")])
            P.dma("pool", self.io["mixD1"][a:a + C, :], yf_, reads=[B("s_y")], writes=[B("mixD1", a)])
            if c0 + C == L:
                for k in range(3):
                    P.dma("pool", self.io["new_conv"][oi, k].rearrange("(c p) -> p c", p=128), U[:, :, C + k], reads=[B("s_U")],
                          writes=[B("new_conv", oi, k)], **NC)

        seqs = [(0, T, None)] + [(T + s * LS, LS, s) for s in range(NS)]
        work = [(q0, L_, s, c0) for (q0, L_, s) in seqs for c0 in range(0, L_, CM)]
        active = []
        nxt_i = 0
        while active or nxt_i < len(work):
            while len(active) < KS and nxt_i < len(work):
                q0, L_, s, c0 = work[nxt_i]
                active.append(chunk(nxt_i, q0, L_, s, c0, nxt_i % KS))
                nxt_i += 1
            for g_ in list(active):
                try:
                    next(g_)
                except StopIteration:
                    active.remove(g_)
        P.barrier()


Model.phase_ssd = _ssd_phase
```

```python
from contextlib import ExitStack
import numpy as np
import concourse.bass as bass
import concourse.mybir as mybir
from concourse.bass_utils import run_bass_kernel_spmd

F32 = mybir.dt.float32
BF16 = mybir.dt.bfloat16
I32 = mybir.dt.int32
AF = mybir.ActivationFunctionType
ALU = mybir.AluOpType
AX = mybir.AxisListType

NDS = 48


class Buf:
    __slots__ = ("name", "w", "r")

    def __init__(self, name):
        self.name = name
        self.w = None
        self.r = {}


class Prog:
    ENG = ("pe", "act", "dve", "pool", "sp")

    def __init__(self):
        nc = self.nc = bass.Bass("TRN2", target_bir_lowering=False)
        self.e = dict(pe=nc.tensor, act=nc.scalar, dve=nc.vector, pool=nc.gpsimd, sp=nc.sync)
        self.esem = {k: nc.alloc_semaphore("s_" + k) for k in self.ENG}
        self.ecnt = {k: 0 for k in self.ENG}
        self.seen = {k: {} for k in self.ENG}
        self.dsem = [nc.alloc_semaphore("d%d" % i) for i in range(NDS)]
        self.dval = [0] * NDS
        self.dnext = 0
        self.bufs = {}
        self.ninst = 0

    def buf(self, *key):
        b = self.bufs.get(key)
        if b is None:
            b = self.bufs[key] = Buf(key)
        return b

    def _sem(self, key):
        return self.esem[key[1]] if key[0] == "e" else self.dsem[key[1]]

    def _wait(self, eng, ev):
        if ev is None:
            return
        key, val = ev
        if key[0] == "e" and key[1] == eng and (eng == "pe" or self.ecnt[eng] - val >= 2):
            return
        if self.seen[eng].get(key, 0) >= val:
            return
        self.e[eng].wait_ge(self._sem(key), val)
        self.seen[eng][key] = val

    def _deps(self, eng, reads, writes):
        for b in reads:
            self._wait(eng, b.w)
        for b in writes:
            self._wait(eng, b.w)
            for k, v in b.r.items():
                self._wait(eng, (k, v))

    def _commit(self, ev, reads, writes):
        key, val = ev
        for b in reads:
            if b.r.get(key, 0) < val:
                b.r[key] = val
        for b in writes:
            b.w = ev
            b.r = {}

    def op(self, eng, fn, reads=(), writes=()):
        self._deps(eng, reads, writes)
        ins = fn(self.e[eng])
        ins.then_inc(self.esem[eng], 1)
        self.ecnt[eng] += 1
        self.ninst += 1
        self._commit((("e", eng), self.ecnt[eng]), reads, writes)

    def dma(self, q, out, in_, reads=(), writes=(), **kw):
        i = self.dnext
        self.dnext = (self.dnext + 1) % NDS
        if self.dval[i] > 0:
            self._wait(q, (("d", i), self.dval[i]))
        self._deps(q, reads, writes)
        self.e[q].dma_start(out=out, in_=in_, **kw).then_inc(self.dsem[i], 16)
        self.dval[i] += 16
        self.ninst += 1
        self._commit((("d", i), self.dval[i]), reads, writes)

    def dma_fn(self, q, fn, reads=(), writes=()):
        i = self.dnext
        self.dnext = (self.dnext + 1) % NDS
        if self.dval[i] > 0:
            self._wait(q, (("d", i), self.dval[i]))
        self._deps(q, reads, writes)
        fn(self.e[q]).then_inc(self.dsem[i], 16)
        self.dval[i] += 16
        self.ninst += 1
        self._commit((("d", i), self.dval[i]), reads, writes)

    def barrier(self):
        for e in self.ENG:
            for f in self.ENG:
                if f != e and self.ecnt[f] > 0:
                    self._wait(e, (("e", f), self.ecnt[f]))
            for i in range(NDS):
                if self.dval[i] > 0:
                    self._wait(e, (("d", i), self.dval[i]))
        self.bufs_reset()

    def bufs_reset(self):
        for b in self.bufs.values():
            b.w = None
            b.r = {}

    def finish(self):
        for i in range(NDS):
            if self.dval[i] > 0:
                self._wait("sp", (("d", i), self.dval[i]))
        for f in self.ENG:
            if f != "sp" and self.ecnt[f] > 0:
                self._wait("sp", (("e", f), self.ecnt[f]))


D = 1024
HD = 64
NH = 8
SC = 512
RW = 512
RWS = 1792
IN0 = 3328
FOX_IN = 1544
SSM_CD = 1024
IN1 = 3088
FF = 2816
EPS = 1e-6


def const_arrays(LS=8):
    c = {}
    c["ident"] = np.eye(128, dtype=np.float32)
    iu = np.triu(np.ones((128, 128), np.float32), 0)
    su = np.triu(np.ones((128, 128), np.float32), 1)
    c["triu_incl"] = iu
    c["triu_strict"] = su
    c["tril_strict"] = su.T.copy()
    c["ones"] = np.ones((128, 128), np.float32)
    sel = np.zeros((128, 128), np.float32)
    sel[127, :] = 1.0
    c["sel_last"] = sel
    c["negmask"] = (-30000.0 * su.T).astype(np.float32)
    c["iota"] = np.arange(128, dtype=np.int32).reshape(128, 1)
    nb = 128 // LS
    c["blktri"] = np.kron(np.eye(nb, dtype=np.float32), iu[:LS, :LS]).astype(np.float32)[:128, :128]
    return c


class Model:
    def __init__(self, T, NS, LS, NPG, NPOOL, debug=()):
        self.T, self.NS, self.LS, self.NPG, self.NPOOL = T, NS, LS, NPG, NPOOL
        self.TS = NS * LS
        self.TT = T + self.TS
        self.debug = set(debug)
        self.P = Prog()
        self.nc = self.P.nc
        self.io = {}
        self.build()

    def inp(self, name, shape, dt=F32):
        self.io[name] = self.nc.dram_tensor(name, list(shape), dt, kind="ExternalInput").ap()
        return self.io[name]

    def outp(self, name, shape, dt=F32):
        self.io[name] = self.nc.dram_tensor(name, list(shape), dt, kind="ExternalOutput").ap()
        return self.io[name]

    def scratch(self, name, shape, dt=F32):
        kind = "ExternalOutput" if name in self.debug else "Internal"
        self.io[name] = self.nc.dram_tensor(name, list(shape), dt, kind=kind).ap()
        return self.io[name]

    def _nm(self, name):
        self._uid = getattr(self, "_uid", 0) + 1
        return "%s_%d" % (name, self._uid)

    def sb(self, es, name, shape, dt=F32):
        return es.enter_context(self.nc.sbuf_tensor(self._nm(name), list(shape), dt))

    def ps(self, es, name, shape, dt=F32):
        return es.enter_context(self.nc.psum_tensor(self._nm(name), list(shape), dt))

    def token_tiles(self, step=128):
        tiles = [(t0, min(step, self.T - t0)) for t0 in range(0, self.T, step)]
        for s0 in range(0, self.TS, step):
            tiles.append((self.T + s0, min(step, self.TS - s0)))
        return tiles

    def build(self):
        T, TT = self.T, self.TT
        m = self
        m.inp("xp", [T, D]); m.inp("xs", [self.TS, D])
        for k, arr in const_arrays().items():
            m.inp("c_" + k, arr.shape, I32 if arr.dtype == np.int32 else F32)
        m.inp("norm_mix", [2, D]); m.inp("norm_ffn", [2, D]); m.inp("norm_final", [1, D])
        m.inp("w_in0", [D, IN0]); m.inp("w_out0", [D, D]); m.inp("w_in1", [D, IN1]); m.inp("w_out1", [D, D])
        m.inp("w_gate", [2, D, FF]); m.inp("w_up", [2, D, FF]); m.inp("w_down", [2, FF, D])
        m.inp("sc_conv_w", [3, SC]); m.inp("state_sc", [self.NS, 2, SC])
        m.inp("state_shift", [self.NS, RWS]); m.inp("state_wkv", [self.NS, NH, HD, HD])
        m.inp("rw_mu", [1, RWS]); m.inp("rw_w0", [1, RW]); m.inp("rw_w2", [64, RW]); m.inp("rw_a0", [1, RW])
        m.inp("rw_a2", [64, RW]); m.inp("rw_g2", [128, RW]); m.inp("rw_k_k", [1, RW]); m.inp("rw_k_a", [1, RW])
        m.inp("rw_r_k", [1, RW]); m.inp("rw_ln_w", [1, RW]); m.inp("rw_ln_b", [1, RW])
        m.outp("new_shift", [1 + self.NS, RWS]); m.outp("new_wkv", [1 + self.NS, NH, HD, HD])
        m.outp("y", [TT, D])
        m.outp("new_sc", [1 + self.NS, 2, SC])
        m.scratch("h0", [TT, D])
        m.scratch("zT0", [IN0, TT])
        m.scratch("mixA0", [SC, TT])
        m.scratch("mixB0", [TT, RW])
        NS, NPG, NPOOL = self.NS, self.NPG, self.NPOOL
        m.inp("cache_k", [NPOOL * 128, 512]); m.inp("cache_v", [NPOOL * 128, 512]); m.inp("cache_logf", [NPOOL, 1024])
        m.inp("page_table", [NS, NPG], I32)
        m.inp("state_ssm_conv", [NS, 3, SSM_CD]); m.inp("state_ssm", [NS, NH, HD, 128])
        m.inp("fox_f_bias", [1, 8]); m.inp("ssm_conv_w", [4, SSM_CD]); m.inp("ssm_conv_b", [1, SSM_CD])
        m.inp("ssm_dt_bias", [1, 8]); m.inp("ssm_a_log", [1, 8]); m.inp("ssm_d", [1, 8]); m.inp("ssm_norm_w", [1, 512])
        m.outp("new_k", [TT, 512]); m.outp("new_v", [TT, 512]); m.outp("new_logf", [TT, 8])
        m.outp("new_conv", [1 + NS, 3, SSM_CD]); m.outp("new_ssm", [1 + NS, NH, HD, 128])
        m.scratch("zT1", [IN1, TT]); m.scratch("fpre", [TT, 8]); m.scratch("zg_tm", [TT, 512]); m.scratch("dt_tm", [TT, 8])
        m.scratch("mixC1", [TT, 512]); m.scratch("mixD1", [TT, 512])

        with ExitStack() as es:
            self.consts(es)
            self.phase_inproj(0, "xp", "xs", "w_in0", IN0, "zT0")
            self.phase_conv0()
            self.phase_rwkv()
            self.phase_out_ffn(0, "xp", "xs", "mixA0", "mixB0", "w_out0", "h0")
            self.phase_inproj(1, "h0", None, "w_in1", IN1, "zT1",
                              tm_cols=[(512, 512, "new_k"), (1024, 512, "new_v"), (1536, 8, "fpre"),
                                       (1544, 512, "zg_tm"), (3080, 8, "dt_tm")])
            self.phase_fox()
            self.phase_ssd()
            self.phase_out_ffn(1, "h0", None, "mixC1", "mixD1", "w_out1", None, final=True, a_tm=True)
        self.P.finish()

    def consts(self, es):
        P = self.P
        self.ident = self.sb(es, "ident", [128, 128])
        self.identb = self.sb(es, "identb", [128, 128], BF16)
        P.dma("sp", self.ident[:], self.io["c_ident"], writes=[P.buf("ident")])
        P.dma("pool", self.identb[:], self.io["c_ident"], writes=[P.buf("identb")])
        self.eps_t = self.sb(es, "eps_t", [128, 1])
        P.op("dve", lambda e: e.memset(self.eps_t[:], EPS), writes=[P.buf("eps_t")])

    def rstd(self, ss, ssb, n, inv=1.0 / D, eps_t=None):
        P = self.P
        et = self.eps_t if eps_t is None else eps_t
        P.op("act", lambda e: e.activation(out=ss[:n, 1:2], in_=ss[:n, 0:1], func=AF.Sqrt, bias=et[:n, 0:1], scale=inv),
             reads=[ssb, P.buf("eps_t")], writes=[ssb])
        P.op("dve", lambda e: e.reciprocal(out=ss[:n, 2:3], in_=ss[:n, 1:2]), reads=[ssb], writes=[ssb])

    def norm_T(self, x_t, xb, n, g_t, gb, hnT, hb, tmp, pst, tag):
        P = self.P
        junk, jb = tmp["junk"], P.buf("junk" + tag)
        ss, ssb = tmp["ss"], P.buf("ss" + tag)
        xn, xnb = tmp["xn"], P.buf("xn" + tag)
        pb = P.buf("pst" + tag)
        P.op("act", lambda e: e.activation(out=junk[:n, :], in_=x_t[:n, :], func=AF.Square, accum_out=ss[:n, 0:1]),
             reads=[xb], writes=[jb, ssb])
        self.rstd(ss, ssb, n)
        P.op("dve", lambda e: e.tensor_scalar(out=xn[:n, :], in0=x_t[:n, :], scalar1=ss[:n, 2:3], scalar2=None,
                                              op0=ALU.mult), reads=[xb, ssb], writes=[xnb])
        for c in range(8):
            P.op("pe", lambda e, c=c: e.transpose(out=pst[:, c, :n], in_=xn[:n, c * 128:(c + 1) * 128],
                                                  identity=self.identb[:n, :n]),
                 reads=[xnb, P.buf("identb")], writes=[pb])
        P.op("dve", lambda e: e.tensor_tensor(out=hnT[:, :, :n], in0=pst[:, :, :n],
                                              in1=g_t[:, :].unsqueeze(2).to_broadcast([128, 8, n]), op=ALU.mult),
             reads=[pb, gb], writes=[hb])

    def xsrc(self, xp_name, xs_name, tt, n):
        if xs_name is None:
            return self.io[xp_name][tt:tt + n, :]
        return self.io[xp_name][tt:tt + n, :] if tt < self.T else self.io[xs_name][tt - self.T:tt - self.T + n, :]

    def phase_inproj(self, layer, xp_name, xs_name, w_name, NOUT, z_name, tm_cols=()):
        P, nc = self.P, self.nc
        zT = self.io[z_name]
        with ExitStack() as es:
            w = self.sb(es, "w_in", [128, 8, NOUT], BF16)
            wb = P.buf("w_in")
            wsrc = self.io[w_name].rearrange("(c p) n -> p c n", p=128)
            for c in range(8):
                P.dma("pool", w[:, c, :], wsrc[:, c, :], writes=[wb])
            g_t = self.sb(es, "g_t", [128, 8])
            gb = P.buf("g_t")
            P.dma("sp", g_t[:], self.io["norm_mix"][layer].rearrange("(c p) -> p c", p=128), writes=[gb],
                  allow_slow_non_contiguous=True)
            ST = 512
            hnT = [self.sb(es, "hnT%d" % i, [128, 8, ST], BF16) for i in range(2)]
            xt = [self.sb(es, "xt%d" % i, [128, D]) for i in range(2)]
            tmp = dict(junk=self.sb(es, "junk", [128, D], BF16), ss=self.sb(es, "ss", [128, 4]),
                       xn=self.sb(es, "xn", [128, D], BF16))
            pst = self.ps(es, "pst", [128, 8, 128], BF16)
            pz = [self.ps(es, "pz%d" % i, [128, ST]) for i in range(4)]
            zst = [self.sb(es, "zst%d" % i, [128, ST]) for i in range(4)]
            ptm = self.ps(es, "ptm", [128, 512])
            tmst = self.sb(es, "tmst", [128, 512])
            nch = (NOUT + 127) // 128
            k = 0
            xi = 0
            for si, (t0, ntot) in enumerate(self.token_tiles(ST)):
                hT, hb = hnT[si % 2], P.buf("hnT", si % 2)
                for j0 in range(0, ntot, 128):
                    n = min(128, ntot - j0)
                    x_t, xb = xt[xi % 2], P.buf("xt", xi % 2)
                    xi += 1
                    tt = t0 + j0
                    P.dma("sp", x_t[:n, :], self.xsrc(xp_name, xs_name, tt, n), writes=[xb])
                    self.norm_T(x_t, xb, n, g_t, gb, hT[:, :, j0:j0 + 128], hb, tmp, pst, "A")
                    for (c0, ncol, dname) in tm_cols:
                        for c in range(8):
                            P.op("pe", lambda e, c=c: e.matmul(ptm[:n, :ncol], lhsT=hT[:, c, j0:j0 + n],
                                                               rhs=w[:, c, c0:c0 + ncol], start=(c == 0), stop=(c == 7)),
                                 reads=[hb, wb], writes=[P.buf("ptm")])
                        P.op("act", lambda e: e.copy(out=tmst[:n, :ncol], in_=ptm[:n, :ncol]),
                             reads=[P.buf("ptm")], writes=[P.buf("tmst")])
                        P.dma("sp", self.io[dname][tt:tt + n, :], tmst[:n, :ncol], reads=[P.buf("tmst")],
                              writes=[P.buf(dname, tt)])
                for mch in range(nch):
                    mc = min(128, NOUT - mch * 128)
                    pzz, pzb = pz[k % 4], P.buf("pz", k % 4)
                    zs, zsb = zst[k % 4], P.buf("zst", k % 4)
                    for c in range(8):
                        P.op("pe", lambda e, c=c: e.matmul(pzz[:mc, :ntot], lhsT=w[:, c, mch * 128:mch * 128 + mc],
                                                           rhs=hT[:, c, :ntot], start=(c == 0), stop=(c == 7)),
                             reads=[hb, wb], writes=[pzb])
                    if k % 2 == 0:
                        P.op("act", lambda e: e.copy(out=zs[:mc, :ntot], in_=pzz[:mc, :ntot]), reads=[pzb], writes=[zsb])
                    else:
                        P.op("dve", lambda e: e.tensor_copy(out=zs[:mc, :ntot], in_=pzz[:mc, :ntot]), reads=[pzb], writes=[zsb])
                    P.dma("sp" if k % 2 == 0 else "pool", zT[mch * 128:mch * 128 + mc, t0:t0 + ntot], zs[:mc, :ntot],
                          reads=[zsb], writes=[P.buf(z_name, mch, si)])
                    k += 1
            P.barrier()

    def phase_conv0(self):
        P, T, LS = self.P, self.T, self.LS
        zT = self.io["zT0"]
        with ExitStack() as es:
            wc = self.sb(es, "wc", [128, 4, 3])
            wcb = P.buf("wc")
            for kk in range(3):
                P.dma("sp", wc[:, :, kk], self.io["sc_conv_w"][kk].rearrange("(c p) -> p c", p=128), writes=[wcb],
                      allow_slow_non_contiguous=True)
            NT = 512
            gbt = [self.sb(es, "gbt%d" % i, [128, NT]) for i in range(2)]
            gct = [self.sb(es, "gct%d" % i, [128, NT + 2]) for i in range(2)]
            hht = [self.sb(es, "hht%d" % i, [128, NT + 2]) for i in range(2)]
            yt = [self.sb(es, "yt%d" % i, [128, NT]) for i in range(2)]
            seqs = [(0, T, None)] + [(T + s * LS, LS, s) for s in range(self.NS)]
            k = 0
            for (q0, L, s) in seqs:
                for cc in range(4):
                    for t0 in range(0, L, NT):
                        n = min(NT, L - t0)
                        i = k % 2
                        k += 1
                        g, c_, h, y = gbt[i], gct[i], hht[i], yt[i]
                        gB, cB, hB, yB = P.buf("gbt", i), P.buf("gct", i), P.buf("hht", i), P.buf("yt", i)
                        a = q0 + t0
                        P.dma("sp", g[:, :n], zT[cc * 128:(cc + 1) * 128, a:a + n], writes=[gB])
                        if t0 == 0:
                            P.dma("sp", c_[:, 2:2 + n], zT[512 + cc * 128:512 + (cc + 1) * 128, a:a + n], writes=[cB])
                            P.dma("sp", h[:, 2:2 + n], zT[1024 + cc * 128:1024 + (cc + 1) * 128, a:a + n], writes=[hB])
                        else:
                            P.dma("sp", c_[:, :2 + n], zT[512 + cc * 128:512 + (cc + 1) * 128, a - 2:a + n], writes=[cB])
                            P.dma("sp", h[:, :2 + n], zT[1024 + cc * 128:1024 + (cc + 1) * 128, a - 2:a + n], writes=[hB])
                        lo = 2 if t0 == 0 else 0
                        P.op("pool", lambda e: e.tensor_tensor(out=c_[:, lo:2 + n], in0=c_[:, lo:2 + n], in1=h[:, lo:2 + n],
                                                               op=ALU.mult), reads=[cB, hB], writes=[cB])
                        if t0 == 0:
                            if s is None:
                                P.op("pool", lambda e: e.memset(c_[:, 0:2], 0.0), writes=[cB])
                            else:
                                P.dma("sp", c_[:, 0:2], self.io["state_sc"][s, :, cc * 128:(cc + 1) * 128].rearrange("k p -> p k"),
                                      writes=[cB], allow_slow_non_contiguous=True)
                        P.op("dve", lambda e: e.tensor_scalar(out=y[:, :n], in0=c_[:, 0:n], scalar1=wc[:, cc, 0:1], scalar2=None,
                                                              op0=ALU.mult), reads=[cB, wcb], writes=[yB])
                        P.op("dve", lambda e: e.scalar_tensor_tensor(out=y[:, :n], in0=c_[:, 1:1 + n], scalar=wc[:, cc, 1:2],
                                                                     in1=y[:, :n], op0=ALU.mult, op1=ALU.add),
                             reads=[cB, wcb, yB], writes=[yB])
                        P.op("dve", lambda e: e.scalar_tensor_tensor(out=y[:, :n], in0=c_[:, 2:2 + n], scalar=wc[:, cc, 2:3],
                                                                     in1=y[:, :n], op0=ALU.mult, op1=ALU.add),
                             reads=[cB, wcb, yB], writes=[yB])
                        P.op("pool", lambda e: e.tensor_tensor(out=y[:, :n], in0=y[:, :n], in1=g[:, :n], op=ALU.mult),
                             reads=[yB, gB], writes=[yB])
                        P.dma("pool", self.io["mixA0"][cc * 128:(cc + 1) * 128, a:a + n], y[:, :n], reads=[yB],
                              writes=[P.buf("mixA0", cc, a)])
                        if t0 + n == L:
                            oi = 0 if s is None else 1 + s
                            P.dma("pool", self.io["new_sc"][oi, :, cc * 128:(cc + 1) * 128].rearrange("k p -> p k"),
                                  c_[:, n:n + 2], reads=[cB], writes=[P.buf("new_sc", oi, cc)], allow_slow_non_contiguous=True)
            P.barrier()

    def phase_out_ffn(self, layer, xp_name, xs_name, mixA, mixB, wout_name, h_out, final=False, a_tm=False):
        P = self.P
        T = self.T
        with ExitStack() as es:
            wo = self.sb(es, "wo", [128, 8, D], BF16)
            wg = self.sb(es, "wg", [128, 8, FF], BF16)
            wu = self.sb(es, "wu", [128, 8, FF], BF16)
            wd = self.sb(es, "wd", [128, 22, D], BF16)
            wB = P.buf("ffn_w")
            for c in range(8):
                P.dma("pool", wo[:, c, :], self.io[wout_name].rearrange("(c p) n -> p c n", p=128)[:, c, :], writes=[wB])
            for c in range(8):
                P.dma("pool", wg[:, c, :], self.io["w_gate"][layer].rearrange("(c p) n -> p c n", p=128)[:, c, :], writes=[wB])
                P.dma("pool", wu[:, c, :], self.io["w_up"][layer].rearrange("(c p) n -> p c n", p=128)[:, c, :], writes=[wB])
            for f in range(22):
                P.dma("pool", wd[:, f, :], self.io["w_down"][layer].rearrange("(f p) n -> p f n", p=128)[:, f, :], writes=[wB])
            g_t = self.sb(es, "g_t", [128, 8])
            gb = P.buf("g_t")
            P.dma("sp", g_t[:], self.io["norm_ffn"][layer].rearrange("(c p) -> p c", p=128), writes=[gb],
                  allow_slow_non_contiguous=True)
            if final:
                gfin = self.sb(es, "gfin", [128, D])
                P.dma("sp", gfin[:], self.io["norm_final"].partition_broadcast(128), writes=[P.buf("gfin")])
            ST = 256
            NSUB = ST // 128
            hnT = self.sb(es, "hnT", [128, 8, ST], BF16)
            h1 = self.sb(es, "h1", [128, NSUB, D])
            actT = self.sb(es, "actT", [128, 22, ST], BF16)
            xt = self.sb(es, "xt", [128, D])
            tmp = dict(junk=self.sb(es, "junk", [128, D], BF16), ss=self.sb(es, "ss", [128, 4]),
                       xn=self.sb(es, "xn", [128, D], BF16))
            yT = self.sb(es, "yT", [128, 8, 128], BF16)
            mb = self.sb(es, "mb", [128, 512])
            sil = [self.sb(es, "sil%d" % i, [128, ST]) for i in range(2)]
            ot = self.sb(es, "ot", [128, D])
            pst = self.ps(es, "pst", [128, 8, 128], BF16)
            ptr = self.ps(es, "ptr", [128, 4, 128])
            po = [self.ps(es, "po%d" % i, [128, 512]) for i in range(2)]
            pg = [self.ps(es, "pg%d" % i, [128, 512]) for i in range(2)]
            pu = [self.ps(es, "pu%d" % i, [128, 512]) for i in range(2)]
            for si, (t0, ntot) in enumerate(self.token_tiles(ST)):
                hb = P.buf("hnT")
                subs = [(j0, min(128, ntot - j0)) for j0 in range(0, ntot, 128)]
                for ji, (j0, n) in enumerate(subs):
                    tt = t0 + j0
                    if not a_tm:
                        P.dma("pool", yT[:, 0:4, :n], self.io[mixA][:, tt:tt + n].rearrange("(c p) n -> p c n", p=128),
                              writes=[P.buf("yT")])
                    else:
                        P.dma("sp", mb[:n, :], self.io[mixA][tt:tt + n, :], writes=[P.buf("mb")])
                        for c in range(4):
                            P.op("pe", lambda e, c=c: e.transpose(out=ptr[:, c, :n], in_=mb[:n, c * 128:(c + 1) * 128],
                                                                  identity=self.ident[:n, :n]),
                                 reads=[P.buf("mb"), P.buf("ident")], writes=[P.buf("ptr")])
                        P.op("act", lambda e: e.copy(out=yT[:, 0:4, :n], in_=ptr[:, :, :n]), reads=[P.buf("ptr")],
                             writes=[P.buf("yT")])
                    P.dma("sp", mb[:n, :], self.io[mixB][tt:tt + n, :], writes=[P.buf("mb")])
                    for c in range(4):
                        P.op("pe", lambda e, c=c: e.transpose(out=ptr[:, c, :n], in_=mb[:n, c * 128:(c + 1) * 128],
                                                              identity=self.ident[:n, :n]),
                             reads=[P.buf("mb"), P.buf("ident")], writes=[P.buf("ptr")])
                    P.op("act", lambda e: e.copy(out=yT[:, 4:8, :n], in_=ptr[:, :, :n]), reads=[P.buf("ptr")],
                         writes=[P.buf("yT")])
                    P.dma("sp", xt[:n, :], self.xsrc(xp_name, xs_name, tt, n), writes=[P.buf("xt")])
                    for half in range(2):
                        for c in range(8):
                            P.op("pe", lambda e, c=c: e.matmul(po[half][:n, :], lhsT=yT[:, c, :n],
                                                               rhs=wo[:, c, half * 512:(half + 1) * 512],
                                                               start=(c == 0), stop=(c == 7)),
                                 reads=[P.buf("yT"), wB], writes=[P.buf("po", half)])
                        P.op("dve", lambda e: e.tensor_tensor(out=h1[:n, ji, half * 512:(half + 1) * 512], in0=po[half][:n, :],
                                                              in1=xt[:n, half * 512:(half + 1) * 512], op=ALU.add),
                             reads=[P.buf("po", half), P.buf("xt")], writes=[P.buf("h1", ji)])
                    self.norm_T(h1[:, ji, :], P.buf("h1", ji), n, g_t, gb, hnT[:, :, j0:j0 + 128], hb, tmp, pst, "C")
                for f in range(22):
                    i = f % 2
                    for c in range(8):
                        P.op("pe", lambda e, c=c: e.matmul(pg[i][:, :ntot], lhsT=wg[:, c, f * 128:(f + 1) * 128],
                                                           rhs=hnT[:, c, :ntot], start=(c == 0), stop=(c == 7)),
                             reads=[hb, wB], writes=[P.buf("pg", i)])
                    for c in range(8):
                        P.op("pe", lambda e, c=c: e.matmul(pu[i][:, :ntot], lhsT=wu[:, c, f * 128:(f + 1) * 128],
                                                           rhs=hnT[:, c, :ntot], start=(c == 0), stop=(c == 7)),
                             reads=[hb, wB], writes=[P.buf("pu", i)])
                    P.op("act", lambda e: e.activation(out=sil[i][:, :ntot], in_=pg[i][:, :ntot], func=AF.Silu),
                         reads=[P.buf("pg", i)], writes=[P.buf("sil", i)])
                    P.op("dve", lambda e: e.tensor_tensor(out=actT[:, f, :ntot], in0=sil[i][:, :ntot], in1=pu[i][:, :ntot],
                                                          op=ALU.mult),
                         reads=[P.buf("sil", i), P.buf("pu", i)], writes=[P.buf("actT")])
                for ji, (j0, n) in enumerate(subs):
                    tt = t0 + j0
                    for half in range(2):
                        for f in range(22):
                            P.op("pe", lambda e, f=f: e.matmul(po[half][:n, :], lhsT=actT[:, f, j0:j0 + n],
                                                               rhs=wd[:, f, half * 512:(half + 1) * 512],
                                                               start=(f == 0), stop=(f == 21)),
                                 reads=[P.buf("actT"), wB], writes=[P.buf("po", half)])
                        P.op("dve", lambda e: e.tensor_tensor(out=ot[:n, half * 512:(half + 1) * 512], in0=po[half][:n, :],
                                                              in1=h1[:n, ji, half * 512:(half + 1) * 512], op=ALU.add),
                             reads=[P.buf("po", half), P.buf("h1", ji)], writes=[P.buf("ot")])
                    if not final:
                        P.dma("sp", self.io[h_out][tt:tt + n, :], ot[:n, :], reads=[P.buf("ot")], writes=[P.buf(h_out, tt)])
                    else:
                        ss, ssb = tmp["ss"], P.buf("ssC")
                        P.op("act", lambda e: e.activation(out=tmp["junk"][:n, :], in_=ot[:n, :], func=AF.Square,
                                                           accum_out=ss[:n, 0:1]),
                             reads=[P.buf("ot")], writes=[P.buf("junkC"), ssb])
                        self.rstd(ss, ssb, n)
                        P.op("dve", lambda e: e.scalar_tensor_tensor(out=ot[:n, :], in0=ot[:n, :], scalar=ss[:n, 2:3],
                                                                     in1=gfin[:n, :], op0=ALU.mult, op1=ALU.mult),
                             reads=[P.buf("ot"), ssb, P.buf("gfin")], writes=[P.buf("ot")])
                        P.dma("sp", self.io["y"][tt:tt + n, :], ot[:n, :], reads=[P.buf("ot")], writes=[P.buf("y", tt)])
            P.barrier()

    def phase_zero(self, name, rows, cols):
        P = self.P
        with ExitStack() as es:
            z = self.sb(es, "zero_t", [128, cols])
            P.op("dve", lambda e: e.memset(z[:], 0.0), writes=[P.buf("zero_t")])
            for r0 in range(0, rows, 128):
                n = min(128, rows - r0)
                P.dma("sp", self.io[name][r0:r0 + n, :], z[:n, :], reads=[P.buf("zero_t")], writes=[P.buf(name, r0)])
            P.barrier()


_CACHE = {}


def _prep(inputs):
    x_prompt = np.asarray(inputs["x_prompt"]); x_sample = np.asarray(inputs["x_sample"])
    B, T, _ = x_prompt.shape
    DB, LS, _ = x_sample.shape
    NS = DB // 8
    npg = inputs["page_table"].shape[1]
    npool = inputs["cache_k"].shape[1]
    return B, T, DB, LS, NS, npg, npool


def kernel(_debug=(), **inputs):
    B, T, DB, LS, NS, NPG, NPOOL = _prep(inputs)
    key = (T, NS, LS, NPG, NPOOL, tuple(_debug))
    if key not in _CACHE:
        _CACHE[key] = Model(T, NS, LS, NPG, NPOOL, debug=_debug)
    m = _CACHE[key]
    f = lambda k: np.ascontiguousarray(np.asarray(inputs[k], dtype=np.float32))
    consts = {"c_" + k: v for k, v in const_arrays(LS).items()}
    in_maps = []
    for c in range(8):
        sl = slice(c * NS, (c + 1) * NS)
        d = dict(consts)
        d["xp"] = f("x_prompt")[c % B]
        d["xs"] = f("x_sample")[sl].reshape(NS * LS, D)
        d["norm_mix"] = f("norm_mix"); d["norm_ffn"] = f("norm_ffn"); d["norm_final"] = f("norm_final").reshape(1, D)
        d["w_in0"] = f("w_in0")[0]; d["w_out0"] = f("w_out0")[0]; d["w_in1"] = f("w_in1")[0]; d["w_out1"] = f("w_out1")[0]
        d["w_gate"] = f("w_gate"); d["w_up"] = f("w_up"); d["w_down"] = f("w_down")
        d["sc_conv_w"] = f("sc_conv_w")[0]; d["state_sc"] = f("state_sc")[0, sl]
        d["state_shift"] = f("state_shift")[0, sl]; d["state_wkv"] = f("state_wkv")[0, sl]
        for nm in ("rw_mu", "rw_w0", "rw_a0", "rw_k_k", "rw_k_a", "rw_ln_w", "rw_ln_b"):
            d[nm] = f(nm).reshape(1, -1)
        d["rw_r_k"] = f("rw_r_k").reshape(1, RW)
        d["rw_w2"] = f("rw_w2")[0]; d["rw_a2"] = f("rw_a2")[0]; d["rw_g2"] = f("rw_g2")[0]
        d["cache_k"] = f("cache_k")[0].reshape(-1, 512); d["cache_v"] = f("cache_v")[0].reshape(-1, 512)
        d["cache_logf"] = f("cache_logf")[0].reshape(-1, 1024)
        d["page_table"] = np.ascontiguousarray(np.asarray(inputs["page_table"], dtype=np.int32)[sl])
        d["state_ssm_conv"] = f("state_ssm_conv")[0, sl]; d["state_ssm"] = f("state_ssm")[0, sl]
        d["fox_f_bias"] = f("fox_f_bias"); d["ssm_conv_w"] = f("ssm_conv_w")[0]; d["ssm_conv_b"] = f("ssm_conv_b")
        d["ssm_dt_bias"] = f("ssm_dt_bias"); d["ssm_a_log"] = f("ssm_a_log"); d["ssm_d"] = f("ssm_d"); d["ssm_norm_w"] = f("ssm_norm_w")
        in_maps.append({k: v for k, v in d.items() if k in m.io})
    res = run_bass_kernel_spmd(m.nc, in_maps, core_ids=list(range(8)))
    R = res.results
    if _debug:
        return R
    y_prompt = np.stack([R[b]["y"][:T] for b in range(B)])
    y_sample = np.concatenate([R[c]["y"][T:].reshape(NS, LS, D) for c in range(8)])
    new_sc_p = np.stack([R[b]["new_sc"][0] for b in range(B)])[None]
    new_sc_s = np.concatenate([R[c]["new_sc"][1:] for c in range(8)])[None]
    outs = [y_prompt, y_sample, new_sc_p, new_sc_s]
    z = lambda *sh: np.zeros(sh, np.float32)
    outs += [np.stack([R[b]["new_shift"][0] for b in range(B)])[None],
             np.concatenate([R[c]["new_shift"][1:] for c in range(8)])[None],
             np.stack([R[b]["new_wkv"][0] for b in range(B)])[None],
             np.concatenate([R[c]["new_wkv"][1:] for c in range(8)])[None]]
    def pp(name, shape_tail):
        return np.stack([R[b][name][:T].reshape((T,) + shape_tail) for b in range(B)])[None]

    def ss_(name, shape_tail):
        return np.concatenate([R[c][name][T:].reshape((NS, LS) + shape_tail) for c in range(8)])[None]
    outs += [pp("new_k", (NH, HD)), ss_("new_k", (NH, HD)), pp("new_v", (NH, HD)), ss_("new_v", (NH, HD)),
             pp("new_logf", (NH,)), ss_("new_logf", (NH,))]
    for name in ("new_conv", "new_ssm"):
        outs.append(np.stack([R[b][name][0] for b in range(B)])[None])
        outs.append(np.concatenate([R[c][name][1:] for c in range(8)])[None])
    return tuple(outs)


RW_GN_EPS = 64e-5


def _rwkv_phase(self):
    P, T, LS, NS = self.P, self.T, self.LS, self.NS
    zT = self.io["zT0"]
    R0 = 3 * SC
    CMAX = 64
    E5 = float(np.exp(-0.5))
    with ExitStack() as es:
        sb = lambda name, shape, dt=F32: self.sb(es, name, shape, dt)
        B = P.buf
        def ld(name, shape, src, **kw):
            t = sb(name, shape)
            P.dma("sp", t[:], src, writes=[B(name)], **kw)
            return t
        NC = dict(allow_slow_non_contiguous=True)
        mu = self.io["rw_mu"]
        mu_rkv = ld("mu_rkv", [64, 24], mu[0, 0:1536].rearrange("(g n) -> n g", n=64), **NC)
        mu_l = ld("mu_l", [64, 2], mu[0, 1536:1664].rearrange("(g n) -> n g", n=64), **NC)
        mu_g = ld("mu_g", [128, 1], mu[0, 1664:1792].rearrange("(n o) -> n o", o=1), **NC)
        w0row = ld("w0row", [1, 512], self.io["rw_w0"])
        w2 = ld("w2", [64, 512], self.io["rw_w2"])
        a2 = ld("a2", [64, 512], self.io["rw_a2"])
        g2 = ld("g2", [128, 512], self.io["rw_g2"])
        hn = lambda nm: self.io[nm][0].rearrange("(h n) -> n h", n=64)
        a0 = ld("a0", [64, 8], hn("rw_a0"), **NC)
        k_k = ld("k_k", [64, 8], hn("rw_k_k"), **NC)
        k_a = ld("k_a", [64, 8], hn("rw_k_a"), **NC)
        r_k = ld("r_k", [64, 8], hn("rw_r_k"), **NC)
        lnw = ld("lnw", [64, 512], self.io["rw_ln_w"].partition_broadcast(64))
        lnb = ld("lnb", [64, 512], self.io["rw_ln_b"].partition_broadcast(64))
        tri_i = ld("tri_i", [64, 64], self.io["c_triu_incl"][0:64, 0:64])
        tri_s = ld("tri_s", [64, 64], self.io["c_triu_strict"][0:64, 0:64])
        low_s = ld("low_s", [64, 64], self.io["c_tril_strict"][0:64, 0:64])
        ones = ld("ones64", [64, 64], self.io["c_ones"][0:64, 0:64])
        triS_i = sb("triS_i", [64, 64]); triS_s = sb("triS_s", [64, 64]); ntri_i = sb("ntri_i", [64, 64])
        P.op("dve", lambda e: e.tensor_scalar(out=triS_i[:], in0=tri_i[:], scalar1=-E5, scalar2=None, op0=ALU.mult),
             reads=[B("tri_i")], writes=[B("triS_i")])
        P.op("dve", lambda e: e.tensor_scalar(out=triS_s[:], in0=tri_s[:], scalar1=-E5, scalar2=None, op0=ALU.mult),
             reads=[B("tri_s")], writes=[B("triS_s")])
        P.op("dve", lambda e: e.tensor_scalar(out=ntri_i[:], in0=tri_i[:], scalar1=-1.0, scalar2=None, op0=ALU.mult),
             reads=[B("tri_i")], writes=[B("ntri_i")])
        epsg = sb("epsg", [64, 1])
        P.op("dve", lambda e: e.memset(epsg[:], RW_GN_EPS), writes=[B("epsg")])
        ident = self.ident
        def mkset(si):
            S = {}
            sbs = lambda name, shape, dt=F32: sb(name + "_s%d" % si, shape, dt)
            S["zr"] = sbs("zr", [64, 24, CMAX + 1])
            S["zl"] = sbs("zl", [64, 2, CMAX + 1])
            S["zg"] = sbs("zg", [128, CMAX + 1])
            S["dd"] = sbs("dd", [64, 24, CMAX])
            S["dl"] = sbs("dl", [64, 2, CMAX])
            S["lx"] = sbs("lx", [64, 2, CMAX])
            S["th"] = sbs("th", [64, CMAX])
            S["dg"] = sbs("dg", [128, CMAX])
            S["sg"] = sbs("sg", [128, CMAX])
            S["sigw"] = sbs("sigw", [64, 512])
            S["gtm"] = sbs("gtm", [64, 512])
            S["Einc"] = sbs("Einc", [64, 8, CMAX])
            S["Einv"] = sbs("Einv", [64, 8, CMAX])
            S["Eexc"] = sbs("Eexc", [64, 8, CMAX])
            S["gamC"] = sbs("gamC", [64, 8])
            S["av"] = sbs("av", [64, 8, CMAX])
            S["kk0"] = sbs("kk0", [64, 8, CMAX])
            S["rn"] = sbs("rn", [64, 8, CMAX])
            S["kp"] = sbs("kp", [64, 8, CMAX])
            S["bb"] = sbs("bb", [64, 8, CMAX])
            S["tt_"] = sbs("tt_", [64, 8, CMAX])
            S["AR"] = sbs("AR", [64, 8, 2, CMAX])
            S["Kt"] = sbs("Kt", [64, 8, CMAX])
            S["Bt"] = sbs("Bt", [64, 8, CMAX])
            S["rk"] = sbs("rk", [64, 8, CMAX])
            S["rks"] = sbs("rks", [64, 8])
            S["Vtm"] = sbs("Vtm", [64, 512])
            S["Ktm"] = sbs("Ktm", [64, 512])
            S["nBtm"] = sbs("nBtm", [64, 512])
            S["Nka"] = sbs("Nka", [64, 8, CMAX])
            S["Mkr"] = sbs("Mkr", [64, 8, CMAX])
            S["nMbr"] = sbs("nMbr", [64, 8, CMAX])
            S["Am"] = [sbs("Am%d" % i, [64, 8, CMAX], BF16) for i in range(2)]
            S["At"] = [sbs("At%d" % i, [64, 8, CMAX], BF16) for i in range(2)]
            S["Pm"] = sbs("Pm", [64, 8, CMAX], BF16)
            S["Pt"] = sbs("Pt", [64, 8, CMAX], BF16)
            S["WT"] = sbs("WT", [64, 512], BF16)
            S["UT"] = sbs("UT", [64, 512])
            S["osb"] = sbs("osb", [64, 8, 64])
            S["sq2"] = sbs("sq2", [64, 8, 64])
            S["yv"] = sbs("yv", [64, 8, 64])
            S["st8"] = sbs("st8", [64, 8, 4])
            S["zx"] = S["dd"]
            S["oc"] = S["osb"]
            S["sq"] = S["tt_"]
            S["kk"] = S["kk0"]
            return S
        sets = [mkset(0), mkset(1)]
        PERSET = {'rks', 'th', 'dg', 'kk', 'nBtm', 'st8', 'kk0', 'zl', 'zx', 'AR', 'Kt', 'Einc', 'sq', 'Pm', 'Bt', 'tt_', 'Pt', 'nMbr', 'osb', 'dd', 'kp', 'Nka', 'Am', 'Eexc', 'At', 'UT', 'bb', 'dl', 'yv', 'rk', 'Einv', 'gamC', 'sg', 'WT', 'av', 'zr', 'oc', 'sq2', 'Ktm', 'gtm', 'Vtm', 'Mkr', 'sigw', 'lx', 'zg', 'rn'}
        ALIAS = {'zx': 'dd', 'oc': 'osb', 'sq': 'tt_', 'kk': 'kk0'}
        B0 = B

        def mkB(si):
            def Bs(n, *a):
                n = ALIAS.get(n, n)
                return B0(n + "@%d" % si, *a) if n in PERSET else B0(n, *a)
            return Bs
        ST = sb("ST", [64, 8, 64]); Snat = sb("Snat", [64, 8, 64])
        banks = [self.ps(es, "bk%d" % i, [128, 512]) for i in range(8)]
        bstate = [0]

        bcnt = [0, 0]

        def mkbank(si):
            def bank_():
                i = si * 4 + (bcnt[si] % 4)
                bcnt[si] += 1
                return banks[i], B0("bank", i)
            return bank_

        def bc3(t2, C, n=64, g=8):
            return t2[:n, :g].unsqueeze(2).to_broadcast([n, g, C])

        done = [0]

        def chunk(gi, q0, L, s, c0, si):
            S = sets[si]
            B = mkB(si)
            bank = mkbank(si)
            zr = S["zr"]
            zl = S["zl"]
            zg = S["zg"]
            dd = S["dd"]
            zx = S["zx"]
            dl = S["dl"]
            lx = S["lx"]
            th = S["th"]
            dg = S["dg"]
            sg = S["sg"]
            sigw = S["sigw"]
            gtm = S["gtm"]
            Einc = S["Einc"]
            Einv = S["Einv"]
            Eexc = S["Eexc"]
            gamC = S["gamC"]
            av = S["av"]
            kk0 = S["kk0"]
            sq = S["sq"]
            rn = S["rn"]
            kk = S["kk"]
            kp = S["kp"]
            bb = S["bb"]
            tt_ = S["tt_"]
            AR = S["AR"]
            Kt = S["Kt"]
            Bt = S["Bt"]
            rk = S["rk"]
            rks = S["rks"]
            Vtm = S["Vtm"]
            Ktm = S["Ktm"]
            nBtm = S["nBtm"]
            Nka = S["Nka"]
            Mkr = S["Mkr"]
            nMbr = S["nMbr"]
            Am = S["Am"]
            At = S["At"]
            Pm = S["Pm"]
            Pt = S["Pt"]
            WT = S["WT"]
            UT = S["UT"]
            osb = S["osb"]
            oc = S["oc"]
            sq2 = S["sq2"]
            yv = S["yv"]
            st8 = S["st8"]
            oi = 0 if s is None else 1 + s
            C = min(CMAX, L - c0)
            a = q0 + c0
            NIT = max(int(np.ceil(np.log2(C))) - 1, 0)
            first = (c0 == 0)
            yield
            rsrc = zT[R0:R0 + 1536, :].rearrange("(g n) t -> n g t", n=64)
            lsrc = zT[R0 + 1536:R0 + 1664, :].rearrange("(g n) t -> n g t", n=64)
            gsrc = zT[R0 + 1664:R0 + 1792, :]
            if first:
                P.dma("sp", zr[:, :, 1:C + 1], rsrc[:, :, a:a + C], writes=[B("zr")])
                P.dma("sp", zl[:, :, 1:C + 1], lsrc[:, :, a:a + C], writes=[B("zl")])
                P.dma("sp", zg[:, 1:C + 1], gsrc[:, a:a + C], writes=[B("zg")])
                if s is None:
                    P.op("pool", lambda e: e.memset(zr[:, :, 0:1], 0.0), writes=[B("zr")])
                    P.op("pool", lambda e: e.memset(zl[:, :, 0:1], 0.0), writes=[B("zl")])
                    P.op("pool", lambda e: e.memset(zg[:, 0:1], 0.0), writes=[B("zg")])
                else:
                    ss_ = self.io["state_shift"][s]
                    P.dma("sp", zr[:, :, 0], ss_[0:1536].rearrange("(g n) -> n g", n=64), writes=[B("zr")], **NC)
                    P.dma("sp", zl[:, :, 0], ss_[1536:1664].rearrange("(g n) -> n g", n=64), writes=[B("zl")], **NC)
                    P.dma("sp", zg[:, 0:1], ss_[1664:1792].rearrange("(n o) -> n o", o=1), writes=[B("zg")], **NC)
            else:
                P.dma("sp", zr[:, :, 0:C + 1], rsrc[:, :, a - 1:a + C], writes=[B("zr")])
                P.dma("sp", zl[:, :, 0:C + 1], lsrc[:, :, a - 1:a + C], writes=[B("zl")])
                P.dma("sp", zg[:, 0:C + 1], gsrc[:, a - 1:a + C], writes=[B("zg")])
            yield
            P.op("dve", lambda e: e.tensor_tensor(out=dd[:, :, :C], in0=zr[:, :, 0:C], in1=zr[:, :, 1:C + 1], op=ALU.subtract),
                 reads=[B("zr")], writes=[B("dd")])
            P.op("pool", lambda e: e.tensor_tensor(out=dd[:, :, :C], in0=dd[:, :, :C], in1=bc3(mu_rkv, C, 64, 24), op=ALU.mult),
                 reads=[B("dd"), B("mu_rkv")], writes=[B("dd")])
            P.op("dve", lambda e: e.tensor_tensor(out=zx[:, :, :C], in0=dd[:, :, :C], in1=zr[:, :, 1:C + 1], op=ALU.add),
                 reads=[B("dd"), B("zr")], writes=[B("zx")])
            rx, kx, vx = zx[:, 0:8, :C], zx[:, 8:16, :C], zx[:, 16:24, :C]
            P.op("pool", lambda e: e.tensor_tensor(out=dl[:, :, :C], in0=zl[:, :, 0:C], in1=zl[:, :, 1:C + 1], op=ALU.subtract),
                 reads=[B("zl")], writes=[B("dl")])
            P.op("pool", lambda e: e.tensor_tensor(out=dl[:, :, :C], in0=dl[:, :, :C], in1=bc3(mu_l, C, 64, 2), op=ALU.mult),
                 reads=[B("dl"), B("mu_l")], writes=[B("dl")])
            P.op("pool", lambda e: e.tensor_tensor(out=lx[:, :, :C], in0=dl[:, :, :C], in1=zl[:, :, 1:C + 1], op=ALU.add),
                 reads=[B("dl"), B("zl")], writes=[B("lx")])
            P.op("pool", lambda e: e.tensor_tensor(out=dg[:, :C], in0=zg[:, 0:C], in1=zg[:, 1:C + 1], op=ALU.subtract),
                 reads=[B("zg")], writes=[B("dg")])
            P.op("dve", lambda e: e.scalar_tensor_tensor(out=dg[:, :C], in0=dg[:, :C], scalar=mu_g[:, 0:1], in1=zg[:, 1:C + 1],
                                                         op0=ALU.mult, op1=ALU.add),
                 reads=[B("dg"), B("mu_g"), B("zg")], writes=[B("dg")])
            P.op("act", lambda e: e.activation(out=th[:, :C], in_=lx[:, 0, :C], func=AF.Tanh), reads=[B("lx")], writes=[B("th")])
            P.op("act", lambda e: e.activation(out=sg[:, :C], in_=dg[:, :C], func=AF.Sigmoid), reads=[B("dg")], writes=[B("sg")])
            yield
            pw, pwb = bank()
            P.op("pe", lambda e: e.matmul(pw[:C, :], lhsT=th[:, :C], rhs=w2[:, :], start=True, stop=False),
                 reads=[B("th"), B("w2")], writes=[pwb])
            P.op("pe", lambda e: e.matmul(pw[:C, :], lhsT=ones[0:1, :C], rhs=w0row[0:1, :], start=False, stop=True),
                 reads=[B("ones64"), B("w0row")], writes=[pwb])
            P.op("act", lambda e: e.activation(out=sigw[:C, :], in_=pw[:C, :], func=AF.Sigmoid), reads=[pwb], writes=[B("sigw")])
            pcl, pclb = bank()
            pce, pceb = bank()
            for h in range(8):
                P.op("pe", lambda e, h=h: e.matmul(pcl[:64, h * C:(h + 1) * C], lhsT=sigw[:C, h * 64:(h + 1) * 64], rhs=triS_i[:C, :C],
                                                   start=True, stop=True), reads=[B("sigw"), B("triS_i")], writes=[pclb])
                P.op("pe", lambda e, h=h: e.matmul(pce[:64, h * C:(h + 1) * C], lhsT=sigw[:C, h * 64:(h + 1) * 64], rhs=triS_s[:C, :C],
                                                   start=True, stop=True), reads=[B("sigw"), B("triS_s")], writes=[pceb])
            v3 = lambda ap: ap.rearrange("n (h c) -> n h c", h=8)
            P.op("act", lambda e: e.activation(out=Einc[:, :, :C], in_=v3(pcl[:64, :8 * C]), func=AF.Exp), reads=[pclb], writes=[B("Einc")])
            P.op("act", lambda e: e.activation(out=Einv[:, :, :C], in_=v3(pcl[:64, :8 * C]), func=AF.Exp, scale=-1.0), reads=[pclb], writes=[B("Einv")])
            P.op("act", lambda e: e.activation(out=Eexc[:, :, :C], in_=v3(pce[:64, :8 * C]), func=AF.Exp), reads=[pceb], writes=[B("Eexc")])
            P.op("pool", lambda e: e.tensor_copy(out=gamC[:, :], in_=Einc[:, :, C - 1]), reads=[B("Einc")], writes=[B("gamC")])
            yield
            pa, pab = bank()
            for h in range(8):
                P.op("pe", lambda e, h=h: e.matmul(pa[:64, h * C:(h + 1) * C], lhsT=a2[:, h * 64:(h + 1) * 64], rhs=lx[:, 1, :C],
                                                   start=True, stop=True), reads=[B("a2"), B("lx")], writes=[pab])
            P.op("dve", lambda e: e.tensor_tensor(out=av[:, :, :C], in0=v3(pa[:64, :8 * C]), in1=bc3(a0, C), op=ALU.add),
                 reads=[pab, B("a0")], writes=[B("av")])
            P.op("act", lambda e: e.activation(out=av[:, :, :C], in_=av[:, :, :C], func=AF.Sigmoid), reads=[B("av")], writes=[B("av")])
            pg_, pgb = bank()
            P.op("pe", lambda e: e.matmul(pg_[:C, :], lhsT=sg[:, :C], rhs=g2[:, :], start=True, stop=True),
                 reads=[B("sg"), B("g2")], writes=[pgb])
            P.op("act", lambda e: e.copy(out=gtm[:C, :], in_=pg_[:C, :]), reads=[pgb], writes=[B("gtm")])
            yield
            P.op("pool", lambda e: e.tensor_tensor(out=kk0[:, :, :C], in0=kx, in1=bc3(k_k, C), op=ALU.mult),
                 reads=[B("zx"), B("k_k")], writes=[B("kk0")])
            P.op("pool", lambda e: e.tensor_tensor(out=sq[:, :, :C], in0=kk0[:, :, :C], in1=kk0[:, :, :C], op=ALU.mult),
                 reads=[B("kk0")], writes=[B("sq")])
            pss, pssb = bank()
            for h in range(8):
                P.op("pe", lambda e, h=h: e.matmul(pss[:64, h * C:(h + 1) * C], lhsT=ones[:, :], rhs=sq[:, h, :C], start=True, stop=True),
                     reads=[B("ones64"), B("sq")], writes=[pssb])
            P.op("act", lambda e: e.activation(out=rn[:, :, :C], in_=v3(pss[:64, :8 * C]), func=AF.Sqrt), reads=[pssb], writes=[B("rn")])
            P.op("dve", lambda e: e.tensor_scalar(out=rn[:, :, :C], in0=rn[:, :, :C], scalar1=1e-12, scalar2=None, op0=ALU.max),
                 reads=[B("rn")], writes=[B("rn")])
            P.op("dve", lambda e: e.reciprocal(out=rn[:, :, :C], in_=rn[:, :, :C]), reads=[B("rn")], writes=[B("rn")])
            P.op("dve", lambda e: e.tensor_tensor(out=kk[:, :, :C], in0=kk0[:, :, :C], in1=rn[:, :, :C], op=ALU.mult),
                 reads=[B("kk0"), B("rn")], writes=[B("kk")])
            P.op("dve", lambda e: e.scalar_tensor_tensor(out=tt_[:, :, :C], in0=av[:, :, :C], scalar=-1.0, in1=bc3(k_a, C),
                                                         op0=ALU.add, op1=ALU.mult), reads=[B("av"), B("k_a")], writes=[B("tt_")])
            P.op("dve", lambda e: e.scalar_tensor_tensor(out=kp[:, :, :C], in0=tt_[:, :, :C], scalar=1.0, in1=kx,
                                                         op0=ALU.add, op1=ALU.mult), reads=[B("tt_"), B("zx")], writes=[B("kp")])
            P.op("pool", lambda e: e.tensor_tensor(out=bb[:, :, :C], in0=kk[:, :, :C], in1=av[:, :, :C], op=ALU.mult),
                 reads=[B("kk"), B("av")], writes=[B("bb")])
            P.op("pool", lambda e: e.tensor_tensor(out=AR[:, :, 0, :C], in0=kk[:, :, :C], in1=Eexc[:, :, :C], op=ALU.mult),
                 reads=[B("kk"), B("Eexc")], writes=[B("AR")])
            P.op("dve", lambda e: e.tensor_tensor(out=AR[:, :, 1, :C], in0=rx, in1=Einc[:, :, :C], op=ALU.mult),
                 reads=[B("zx"), B("Einc")], writes=[B("AR")])
            P.op("dve", lambda e: e.tensor_tensor(out=Kt[:, :, :C], in0=kp[:, :, :C], in1=Einv[:, :, :C], op=ALU.mult),
                 reads=[B("kp"), B("Einv")], writes=[B("Kt")])
            P.op("pool", lambda e: e.tensor_tensor(out=Bt[:, :, :C], in0=bb[:, :, :C], in1=Einv[:, :, :C], op=ALU.mult),
                 reads=[B("bb"), B("Einv")], writes=[B("Bt")])
            P.op("pool", lambda e: e.tensor_tensor(out=rk[:, :, :C], in0=rx, in1=kp[:, :, :C], op=ALU.mult),
                 reads=[B("zx"), B("kp")], writes=[B("rk")])
            P.op("pool", lambda e: e.tensor_tensor(out=rk[:, :, :C], in0=rk[:, :, :C], in1=bc3(r_k, C), op=ALU.mult),
                 reads=[B("rk"), B("r_k")], writes=[B("rk")])
            prk, prkb = bank()
            for h in range(8):
                P.op("pe", lambda e, h=h: e.matmul(prk[:C, h:h + 1], lhsT=rk[:, h, :C], rhs=ones[:, 0:1], start=True, stop=True),
                     reads=[B("rk"), B("ones64")], writes=[prkb])
            P.op("act", lambda e: e.copy(out=rks[:C, :], in_=prk[:C, 0:8]), reads=[prkb], writes=[B("rks")])
            yield
            for (src_t, srcb, dst, dstb, scl) in ((vx, B("zx"), Vtm, B("Vtm"), 1.0), (Kt[:, :, :C], B("Kt"), Ktm, B("Ktm"), 1.0),
                                                  (Bt[:, :, :C], B("Bt"), nBtm, B("nBtm"), -1.0)):
                pt, ptb = bank()
                for h in range(8):
                    P.op("pe", lambda e, h=h: e.transpose(out=pt[:C, h * 64:(h + 1) * 64], in_=src_t[:, h, :], identity=ident[:64, :64]),
                         reads=[srcb, B("ident")], writes=[ptb])
                P.op("act", lambda e: e.mul(out=dst[:C, :], in_=pt[:C, :], mul=scl), reads=[ptb], writes=[dstb])
            yield
            for (lh, lhb, dN, dNb, mN, dM, dMb, mM) in ((Kt, B("Kt"), Nka, B("Nka"), tri_s, Mkr, B("Mkr"), tri_i),
                                                       (Bt, B("Bt"), Am[0], B("Am", 0), tri_s, nMbr, B("nMbr"), ntri_i)):
                p0, p0b = bank()
                p1, p1b = bank()
                for h in range(8):
                    pp, ppb = (p0, p0b) if h < 4 else (p1, p1b)
                    hh = h % 4
                    P.op("pe", lambda e, h=h, pp=pp, hh=hh: e.matmul(pp[:C, hh * 2 * C:(hh + 1) * 2 * C].rearrange("i (s c) -> i s c", s=2),
                                                                     lhsT=lh[:, h, :C], rhs=AR[:, h, :, :C], start=True, stop=True),
                         reads=[lhb, B("AR")], writes=[ppb])
                for half, (pp, ppb) in enumerate(((p0, p0b), (p1, p1b))):
                    v4 = pp[:C, :8 * C].rearrange("i (h s c) -> i h s c", h=4, s=2)
                    hs = slice(half * 4, half * 4 + 4)
                    P.op("dve", lambda e, v4=v4, hs=hs: e.tensor_tensor(out=dN[:C, hs, :C], in0=v4[:, :, 0, :],
                                                                         in1=mN[:C, :C].unsqueeze(1).to_broadcast([C, 4, C]), op=ALU.mult),
                         reads=[ppb, B("tri_s")], writes=[dNb])
                    P.op("dve", lambda e, v4=v4, hs=hs: e.tensor_tensor(out=dM[:C, hs, :C], in0=v4[:, :, 1, :],
                                                                         in1=mM[:C, :C].unsqueeze(1).to_broadcast([C, 4, C]), op=ALU.mult),
                         reads=[ppb, B("tri_i"), B("ntri_i")], writes=[dMb])
            pat, patb = bank()
            for h in range(8):
                P.op("pe", lambda e, h=h: e.matmul(pat[:C, h * C:(h + 1) * C], lhsT=AR[:, h, 0, :C], rhs=Bt[:, h, :C], start=True, stop=True),
                     reads=[B("AR"), B("Bt")], writes=[patb])
            vh = lambda ap: ap.rearrange("i (h c) -> i h c", h=8)
            mb8 = lambda m_: m_[:C, :C].unsqueeze(1).to_broadcast([C, 8, C])
            P.op("dve", lambda e: e.tensor_tensor(out=At[0][:C, :, :C], in0=vh(pat[:C, :8 * C]), in1=mb8(low_s), op=ALU.mult),
                 reads=[patb, B("low_s")], writes=[B("At", 0)])
            yield
            P.op("pool", lambda e: e.tensor_tensor(out=Pm[:C, :, :C], in0=mb8(ident), in1=Am[0][:C, :, :C], op=ALU.subtract),
                 reads=[B("ident"), B("Am", 0)], writes=[B("Pm")])
            P.op("pool", lambda e: e.tensor_tensor(out=Pt[:C, :, :C], in0=mb8(ident), in1=At[0][:C, :, :C], op=ALU.subtract),
                 reads=[B("ident"), B("At", 0)], writes=[B("Pt")])
            cur = 0
            for it in range(NIT):
                last = (it == NIT - 1)
                nxt = 1 - cur
                pA, pAb = bank()
                for h in range(8):
                    P.op("pe", lambda e, h=h: e.matmul(pA[:C, h * C:(h + 1) * C], lhsT=At[cur][:C, h, :C], rhs=Am[cur][:C, h, :C],
                                                       start=True, stop=True), reads=[B("At", cur), B("Am", cur)], writes=[pAb])
                P.op("act", lambda e: e.copy(out=Am[nxt][:C, :, :C], in_=vh(pA[:C, :8 * C])), reads=[pAb], writes=[B("Am", nxt)])
                if not last:
                    pB, pBb = bank()
                    for h in range(8):
                        P.op("pe", lambda e, h=h: e.matmul(pB[:C, h * C:(h + 1) * C], lhsT=Am[cur][:C, h, :C], rhs=At[cur][:C, h, :C],
                                                           start=True, stop=True), reads=[B("At", cur), B("Am", cur)], writes=[pBb])
                    P.op("act", lambda e: e.copy(out=At[nxt][:C, :, :C], in_=vh(pB[:C, :8 * C])), reads=[pBb], writes=[B("At", nxt)])
                pP, pPb = bank()
                for h in range(8):
                    P.op("pe", lambda e, h=h: e.matmul(pP[:C, h * C:(h + 1) * C], lhsT=Pt[:C, h, :C], rhs=Am[nxt][:C, h, :C],
                                                       start=True, stop=True), reads=[B("Pt"), B("Am", nxt)], writes=[pPb])
                if not last:
                    pQ, pQb = bank()
                    for h in range(8):
                        P.op("pe", lambda e, h=h: e.matmul(pQ[:C, h * C:(h + 1) * C], lhsT=Am[nxt][:C, h, :C], rhs=Pt[:C, h, :C],
                                                           start=True, stop=True), reads=[B("Pt"), B("Am", nxt)], writes=[pQb])
                P.op("dve", lambda e: e.tensor_tensor(out=Pm[:C, :, :C], in0=Pm[:C, :, :C], in1=vh(pP[:C, :8 * C]), op=ALU.add),
                     reads=[B("Pm"), pPb], writes=[B("Pm")])
                if not last:
                    P.op("dve", lambda e: e.tensor_tensor(out=Pt[:C, :, :C], in0=Pt[:C, :, :C], in1=vh(pQ[:C, :8 * C]), op=ALU.add),
                         reads=[B("Pt"), pQb], writes=[B("Pt")])
                yield
                cur = nxt
            yield
            while done[0] < gi:
                yield
            if c0 == 0:
                if s is None:
                    P.op("dve", lambda e: e.memset(ST[:], 0.0), writes=[B("ST")])
                else:
                    P.dma("sp", Snat[:], self.io["state_wkv"][s].rearrange("h v k -> v h k"), writes=[B("Snat")])
                    pb, pbb = bank()
                    for h in range(8):
                        P.op("pe", lambda e, h=h: e.transpose(out=pb[:64, h * 64:(h + 1) * 64], in_=Snat[:, h, :], identity=ident[:64, :64]),
                             reads=[B("Snat"), B("ident")], writes=[pbb])
                    P.op("dve", lambda e: e.tensor_copy(out=ST[:].rearrange("n h v -> n (h v)"), in_=pb[:64, :]), reads=[pbb], writes=[B("ST")])
            yield
            hv = lambda t, h: t[:C, h * 64:(h + 1) * 64]
            pW, pWb = bank()
            for h in range(8):
                P.op("pe", lambda e, h=h: e.matmul(hv(pW, h), lhsT=AR[:, h, 0, :C], rhs=ST[:, h, :], start=True, stop=False),
                     reads=[B("AR"), B("ST")], writes=[pWb])
                P.op("pe", lambda e, h=h: e.matmul(hv(pW, h), lhsT=Nka[:C, h, :C], rhs=hv(Vtm, h), start=False, stop=True),
                     reads=[B("Nka"), B("Vtm")], writes=[pWb])
            P.op("act", lambda e: e.copy(out=WT[:C, :], in_=pW[:C, :]), reads=[pWb], writes=[B("WT")])
            yield
            pU, pUb = bank()
            for h in range(8):
                P.op("pe", lambda e, h=h: e.matmul(hv(pU, h), lhsT=Pm[:C, h, :C], rhs=hv(WT, h), start=True, stop=True),
                     reads=[B("Pm"), B("WT")], writes=[pUb])
            P.op("act", lambda e: e.copy(out=UT[:C, :], in_=pU[:C, :]), reads=[pUb], writes=[B("UT")])
            yield
            pO, pOb = bank()
            for h in range(8):
                P.op("pe", lambda e, h=h: e.matmul(hv(pO, h), lhsT=AR[:, h, 1, :C], rhs=ST[:, h, :], start=True, stop=False),
                     reads=[B("AR"), B("ST")], writes=[pOb])
                P.op("pe", lambda e, h=h: e.matmul(hv(pO, h), lhsT=Mkr[:C, h, :C], rhs=hv(Vtm, h), start=False, stop=False),
                     reads=[B("Mkr"), B("Vtm")], writes=[pOb])
                P.op("pe", lambda e, h=h: e.matmul(hv(pO, h), lhsT=nMbr[:C, h, :C], rhs=hv(UT, h), start=False, stop=True),
                     reads=[B("nMbr"), B("UT")], writes=[pOb])
            P.op("act", lambda e: e.copy(out=osb[:C].rearrange("t h v -> t (h v)"), in_=pO[:C, :]), reads=[pOb], writes=[B("osb")])
            yield
            pS, pSb = bank()
            for h in range(8):
                P.op("pe", lambda e, h=h: e.matmul(pS[:64, h * 64:(h + 1) * 64], lhsT=hv(Ktm, h), rhs=hv(Vtm, h), start=True, stop=False),
                     reads=[B("Ktm"), B("Vtm")], writes=[pSb])
                P.op("pe", lambda e, h=h: e.matmul(pS[:64, h * 64:(h + 1) * 64], lhsT=hv(nBtm, h), rhs=hv(UT, h), start=False, stop=True),
                     reads=[B("nBtm"), B("UT")], writes=[pSb])
            P.op("dve", lambda e: e.tensor_tensor(out=ST[:], in0=ST[:], in1=pS[:64, :].rearrange("n (h v) -> n h v", h=8), op=ALU.add),
                 reads=[B("ST"), pSb], writes=[B("ST")])
            P.op("dve", lambda e: e.tensor_tensor(out=ST[:], in0=ST[:], in1=bc3(gamC, 64), op=ALU.mult),
                 reads=[B("ST"), B("gamC")], writes=[B("ST")])
            if c0 + C == L:
                pb, pbb = bank()
                for h in range(8):
                    P.op("pe", lambda e, h=h: e.transpose(out=pb[:64, h * 64:(h + 1) * 64], in_=ST[:, h, :], identity=ident[:64, :64]),
                         reads=[B("ST"), B("ident")], writes=[pbb])
                P.op("act", lambda e: e.copy(out=Snat[:].rearrange("v h k -> v (h k)"), in_=pb[:64, :]), reads=[pbb], writes=[B("Snat")])
                P.dma("pool", self.io["new_wkv"][oi].rearrange("h v k -> v h k"), Snat[:], reads=[B("Snat")], writes=[B("new_wkv", oi)])
                tl = q0 + L - 1
                P.dma("pool", self.io["new_shift"][oi].rearrange("(c p) -> p c", p=128),
                      zT[R0:R0 + RWS, tl:tl + 1].rearrange("(c p) o -> p (c o)", p=128), writes=[B("new_shift", oi)], **NC)
            done[0] += 1
            yield
            b8 = lambda col, w=64: st8[:C, :, col:col + 1].to_broadcast([C, 8, w])
            P.op("dve", lambda e: e.tensor_reduce(out=st8[:C, :, 0], in_=osb[:C], axis=AX.X, op=ALU.add), reads=[B("osb")], writes=[B("st8")])
            P.op("dve", lambda e: e.tensor_scalar(out=st8[:C, :, 0], in0=st8[:C, :, 0], scalar1=-1.0 / 64, scalar2=None, op0=ALU.mult),
                 reads=[B("st8")], writes=[B("st8")])
            P.op("dve", lambda e: e.tensor_tensor(out=oc[:C], in0=osb[:C], in1=b8(0), op=ALU.add), reads=[B("osb"), B("st8")], writes=[B("oc")])
            P.op("pool", lambda e: e.tensor_tensor(out=sq2[:C], in0=oc[:C], in1=oc[:C], op=ALU.mult), reads=[B("oc")], writes=[B("sq2")])
            P.op("dve", lambda e: e.tensor_reduce(out=st8[:C, :, 1], in_=sq2[:C], axis=AX.X, op=ALU.add), reads=[B("sq2")], writes=[B("st8")])
            P.op("act", lambda e: e.activation(out=st8[:C, :, 2], in_=st8[:C, :, 1], func=AF.Sqrt, bias=epsg[:C, 0:1], scale=1.0 / 64),
                 reads=[B("st8"), B("epsg")], writes=[B("st8")])
            P.op("dve", lambda e: e.reciprocal(out=st8[:C, :, 3], in_=st8[:C, :, 2]), reads=[B("st8")], writes=[B("st8")])
            P.op("dve", lambda e: e.tensor_tensor(out=yv[:C], in0=oc[:C], in1=b8(3), op=ALU.mult), reads=[B("oc"), B("st8")], writes=[B("yv")])
            f3 = lambda t: t[:C, :].rearrange("t (h v) -> t h v", h=8)
            P.op("pool", lambda e: e.tensor_tensor(out=yv[:C], in0=yv[:C], in1=f3(lnw), op=ALU.mult), reads=[B("yv"), B("lnw")], writes=[B("yv")])
            P.op("pool", lambda e: e.tensor_tensor(out=yv[:C], in0=yv[:C], in1=f3(lnb), op=ALU.add), reads=[B("yv"), B("lnb")], writes=[B("yv")])
            P.op("dve", lambda e: e.tensor_tensor(out=sq2[:C], in0=f3(Vtm), in1=rks[:C, :].unsqueeze(2).to_broadcast([C, 8, 64]), op=ALU.mult),
                 reads=[B("Vtm"), B("rks")], writes=[B("sq2")])
            P.op("dve", lambda e: e.tensor_tensor(out=yv[:C], in0=yv[:C], in1=sq2[:C], op=ALU.add), reads=[B("yv"), B("sq2")], writes=[B("yv")])
            P.op("dve", lambda e: e.tensor_tensor(out=yv[:C], in0=yv[:C], in1=f3(gtm), op=ALU.mult), reads=[B("yv"), B("gtm")], writes=[B("yv")])
            P.dma("pool", self.io["mixB0"][a:a + C, :], yv[:C].rearrange("t h v -> t (h v)"), reads=[B("yv")], writes=[B("mixB0", a)])

        seqs = [(0, T, None)] + [(T + s * LS, LS, s) for s in range(NS)]
        work = [(q0, L_, s, c0) for (q0, L_, s) in seqs for c0 in range(0, L_, CMAX)]
        active = []
        nxt_i = 0
        KSTREAM = 2
        while active or nxt_i < len(work):
            while len(active) < KSTREAM and nxt_i < len(work):
                q0, L_, s, c0 = work[nxt_i]
                active.append(chunk(nxt_i, q0, L_, s, c0, nxt_i % 2))
                nxt_i += 1
            for g_ in list(active):
                try:
                    next(g_)
                except StopIteration:
                    active.remove(g_)
        P.barrier()


Model.phase_rwkv = _rwkv_phase


def _fox_phase(self):
    P, T, LS, NS, NPG = self.P, self.T, self.LS, self.NS, self.NPG
    TS, TT = self.TS, self.TT
    zT = self.io["zT1"]
    NB = T // 128
    NC = dict(allow_slow_non_contiguous=True)
    B = P.buf
    with ExitStack() as es:
        sb = lambda name, shape, dt=F32: self.sb(es, name, shape, dt)

        def ld(name, shape, src, dt=F32, q="sp", **kw):
            t = sb(name, shape, dt)
            P.dma(q, t[:], src, writes=[B(name)], **kw)
            return t
        ident, identb = self.ident, self.identb
        tri_i = ld("f_tri_i", [128, 128], self.io["c_triu_incl"])
        tri_s = ld("f_tri_s", [128, 128], self.io["c_triu_strict"])
        ones = ld("f_ones", [128, 128], self.io["c_ones"])
        sel = ld("f_sel", [128, 128], self.io["c_sel_last"])
        blktri = ld("f_blktri", [128, 128], self.io["c_blktri"])
        negf = ld("f_negf", [128, 128], self.io["c_negmask"])
        negb = ld("f_negb", [128, 128], self.io["c_negmask"], BF16, q="pool")
        fb = ld("f_fb", [128, 8], self.io["fox_f_bias"].partition_broadcast(128))
        iota_i = ld("f_iota", [128, 1], self.io["c_iota"], I32)
        iota_f = sb("f_iotaf", [128, 1])
        P.op("dve", lambda e: e.tensor_copy(out=iota_f[:], in_=iota_i[:]), reads=[B("f_iota")], writes=[B("f_iotaf")])
        banks = [self.ps(es, "fbk%d" % i, [128, 512]) for i in range(5)]
        bankb = [self.ps(es, "fbkb%d" % i, [128, 1024], BF16) for i in range(1)]
        bstate = [0]

        def bank():
            i = bstate[0] % 5
            bstate[0] += 1
            return banks[i], B("fbank", i)

        LF = sb("LF", [128, max(NB, 1), 8])
        lft = sb("lft", [128, 8]); lfs = sb("lfs", [128, 8])
        for (tt, n) in self.token_tiles(128):
            P.dma("sp", lft[:n, :], self.io["fpre"][tt:tt + n, :], writes=[B("lft")])
            P.op("dve", lambda e: e.tensor_tensor(out=lft[:n, :], in0=lft[:n, :], in1=fb[:n, :], op=ALU.add), reads=[B("lft"), B("f_fb")], writes=[B("lft")])
            P.op("act", lambda e: e.activation(out=lft[:n, :], in_=lft[:n, :], func=AF.Sigmoid), reads=[B("lft")], writes=[B("lft")])
            dst = LF[:n, tt // 128, :] if tt < T else lfs[:n, :]
            dstb = B("LF") if tt < T else B("lfs")
            P.op("act", lambda e: e.activation(out=dst, in_=lft[:n, :], func=AF.Ln), reads=[B("lft")], writes=[dstb])
            P.dma("pool", self.io["new_logf"][tt:tt + n, :], dst, reads=[dstb], writes=[B("new_logf", tt)])

        tot = sb("tot", [128, 8]); excl = sb("excl", [128, 8]); rhs3 = sb("rhs3", [128, 128, 8])

        def cumsum2(src, srcb, dst, dstb, nb):
            cols = nb * 8
            pt_, ptb_ = bank()
            for h in range(8):
                P.op("pe", lambda e, h=h: e.matmul(pt_[:nb, h:h + 1], lhsT=src[:, :nb, h], rhs=ones[:, 0:1], start=True, stop=True),
                     reads=[srcb, B("f_ones")], writes=[ptb_])
            P.op("act", lambda e: e.copy(out=tot[:nb, :], in_=pt_[:nb, 0:8]), reads=[ptb_], writes=[B("tot")])
            pe_, peb_ = bank()
            P.op("pe", lambda e: e.matmul(pe_[:nb, 0:8], lhsT=tri_s[:nb, :nb], rhs=tot[:nb, :], start=True, stop=True),
                 reads=[B("tot"), B("f_tri_s")], writes=[peb_])
            P.op("act", lambda e: e.copy(out=excl[:nb, :], in_=pe_[:nb, 0:8]), reads=[peb_], writes=[B("excl")])
            P.op("dve", lambda e: e.tensor_tensor(out=rhs3[:nb, :nb, :], in0=ident[:nb, :nb].unsqueeze(2).to_broadcast([nb, nb, 8]),
                                                  in1=excl[:nb, :].unsqueeze(1).to_broadcast([nb, nb, 8]), op=ALU.mult),
                 reads=[B("ident"), B("excl")], writes=[B("rhs3")])
            for c0 in range(0, cols, 512):
                cn = min(512, cols - c0)
                b0_, b1_ = c0 // 8, (c0 + cn) // 8
                pw_, pwb_ = bank()
                P.op("pe", lambda e: e.matmul(pw_[:, :cn], lhsT=tri_i[:, :], rhs=src[:, b0_:b1_, :], start=True, stop=False),
                     reads=[srcb, B("f_tri_i")], writes=[pwb_])
                P.op("pe", lambda e: e.matmul(pw_[:, :cn], lhsT=ones[:nb, :], rhs=rhs3[:nb, b0_:b1_, :], start=False, stop=True),
                     reads=[B("rhs3"), B("f_ones")], writes=[pwb_])
                P.op("dve", lambda e: e.tensor_copy(out=dst[:, b0_:b1_, :], in_=pw_[:, :cn].rearrange("p (b h) -> p b h", h=8)),
                     reads=[pwb_], writes=[dstb])

        Ctm = sb("Ctm", [128, NB, 8])
        cumsum2(LF, B("LF"), Ctm, B("Ctm"), NB)
        groups = [(b0, min(b0 + 4, NB)) for b0 in range(0, NB, 4)]
        G = len(groups)
        crefb = sb("crefb", [128, G, 8])
        pc_, pcb_ = bank()
        for g, (b0, b1) in enumerate(groups):
            P.op("pe", lambda e, g=g, b1=b1: e.matmul(pc_[:, g * 8:(g + 1) * 8], lhsT=sel[:, :], rhs=Ctm[:, b1 - 1, :], start=True, stop=True),
                 reads=[B("Ctm"), B("f_sel")], writes=[pcb_])
        P.op("dve", lambda e: e.tensor_copy(out=crefb[:].rearrange("p g h -> p (g h)"), in_=pc_[:, :G * 8]), reads=[pcb_], writes=[B("crefb")])

        QTa = [sb("QTa%d" % i, [65, T], BF16) for i in range(2)]
        KTa = [sb("KTa%d" % i, [65, T], BF16) for i in range(2)]
        Vau = [sb("Vau%d" % i, [128, NB, 65], BF16) for i in range(2)]
        cst = [sb("cst%d" % i, [128, 65]) for i in range(4)]
        biasK = [sb("biasK%d" % i, [128, NB]) for i in range(2)]
        PT = [sb("PT%d" % i, [128, 512], BF16) for i in range(3)]
        rec = sb("rec", [128, 4, 1]); yf = [sb("yf%d" % i, [128, 4, 64]) for i in range(2)]
        for i in range(2):
            P.op("dve", lambda e, i=i: e.memset(KTa[i][64:65, :], 1.0), writes=[B("KTa", i)])
            P.op("pool", lambda e, i=i: e.memset(Vau[i][:, :, 64:65], 1.0), writes=[B("Vau", i)])
        for i in range(4):
            P.op("pool", lambda e, i=i: e.memset(cst[i][:], 0.0), writes=[B("cst", i)])
        pacc = [self.ps(es, "pacc%d" % i, [128, 4, 65]) for i in range(1)]
        ci = 0
        pti = 0
        for h in range(8):
            hi = h % 2
            qa, ka, va = QTa[hi], KTa[hi], Vau[hi]
            P.dma("pool", qa[0:64, :], zT[h * 64:(h + 1) * 64, 0:T], writes=[B("QTa", hi)])
            P.dma("pool", ka[0:64, :], zT[512 + h * 64:512 + (h + 1) * 64, 0:T], writes=[B("KTa", hi)])
            P.dma("pool", va[:, :, 0:64], self.io["new_v"][0:T, h * 64:(h + 1) * 64].rearrange("(b p) d -> p b d", p=128),
                  reads=[B("new_v", tt_) for (tt_, _) in self.token_tiles(128) if tt_ < T], writes=[B("Vau", hi)])
            for g, (b0, b1) in enumerate(groups):
                pq, pqb = bank()
                for b in range(b0, b1):
                    ct, ctb = cst[ci % 4], B("cst", ci % 4)
                    ci += 1
                    P.op("dve", lambda e, b=b, ct=ct: e.tensor_scalar(out=ct[:, 64:65], in0=Ctm[:, b, h:h + 1], scalar1=crefb[:, g, h:h + 1],
                                                                      scalar2=8.0, op0=ALU.subtract, op1=ALU.mult),
                         reads=[B("Ctm"), B("crefb")], writes=[ctb])
                    P.op("pe", lambda e, b=b, ct=ct: e.matmul(pq[0:65, (b - b0) * 128:(b - b0 + 1) * 128], lhsT=ct[:, :], rhs=ident[:, :],
                                                              start=True, stop=True), reads=[ctb, B("ident")], writes=[pqb])
                nq = (b1 - b0) * 128
                P.op("act", lambda e: e.copy(out=qa[64:65, b0 * 128:b1 * 128], in_=pq[64:65, :nq]), reads=[pqb], writes=[B("QTa", hi)])
            for g, (b0, b1) in enumerate(groups):
                nq = (b1 - b0) * 128
                nblk = b1 - b0
                bk, bkb = biasK[g % 2], B("biasK", g % 2)
                P.op("dve", lambda e: e.tensor_scalar(out=bk[:, :b1], in0=Ctm[:, :b1, h], scalar1=crefb[:, g, h:h + 1], scalar2=-1.0,
                                                      op0=ALU.subtract, op1=ALU.mult), reads=[B("Ctm"), B("crefb")], writes=[bkb])
                pa = pacc[0]
                pab = B("pacc", 0)
                first = True

                def qk(kb):
                    n0 = max(kb - b0, 0) * 128
                    ps_, psb_ = bank()
                    diag = kb >= b0
                    P.op("pe", lambda e: e.matmul(ps_[:, n0:nq], lhsT=ka[:, kb * 128:(kb + 1) * 128], rhs=qa[:, b0 * 128 + n0:b1 * 128],
                                                  start=True, stop=not diag), reads=[B("KTa", hi), B("QTa", hi)], writes=[psb_])
                    if diag:
                        P.op("pe", lambda e: e.matmul(ps_[:, n0:n0 + 128], lhsT=identb[:, :], rhs=negb[:, :], start=False, stop=True),
                             reads=[B("identb"), B("f_negb")], writes=[psb_])
                    return (ps_, psb_, n0)
                pend = qk(0)
                for kb in range(b1):
                    nxt = qk(kb + 1) if kb + 1 < b1 else None
                    ps_, psb_, n0 = pend
                    pt, ptb = PT[pti % 3], B("PT", pti % 3)
                    pti += 1
                    P.op("act", lambda e: e.activation(out=pt[:, n0:nq], in_=ps_[:, n0:nq], func=AF.Exp, bias=bk[:, kb:kb + 1], scale=0.125),
                         reads=[psb_, bkb], writes=[ptb])
                    for j in range(n0 // 128, nblk):
                        P.op("pe", lambda e, j=j: e.matmul(pa[:, j, :], lhsT=pt[:, j * 128:(j + 1) * 128], rhs=va[:, kb, :],
                                                           start=first, stop=(kb == b0 + j)), reads=[ptb, B("Vau", hi)], writes=[pab])
                        first = False
                    pend = nxt
                y_, yb_ = yf[g % 2], B("yf", g % 2)
                P.op("dve", lambda e: e.reciprocal(out=rec[:, :nblk, :], in_=pa[:, :nblk, 64:65]), reads=[pab], writes=[B("rec")])
                P.op("dve", lambda e: e.tensor_tensor(out=y_[:, :nblk, :], in0=pa[:, :nblk, 0:64], in1=rec[:, :nblk, :].to_broadcast([128, nblk, 64]),
                                                      op=ALU.mult), reads=[pab, B("rec")], writes=[yb_])
                P.dma("sp", self.io["mixC1"][b0 * 128:b1 * 128, h * 64:(h + 1) * 64].rearrange("(j p) d -> p j d", p=128), y_[:, :nblk, :],
                      reads=[yb_], writes=[B("mixC1", g, h)])

        PTc = sb("PTc", [128, 1], I32); PTb = sb("PTb", [128, NPG], I32); IDXf = sb("IDXf", [128, NPG]); IDX = sb("IDX", [128, NPG], I32)
        LFp = sb("LFp", [128, 128, 8]); LFs2 = sb("LFs2", [128, NPG, 8]); Cp = sb("Cp", [128, NPG, 8])
        cend = sb("cend", [128, 8]); biasP = sb("biasP", [128, NPG, 8])
        Qblk = sb("Qblk", [128, 4, 2 * LS], BF16); KTn = sb("KTn", [128, 4, LS], BF16); Vn = sb("Vn", [LS, 8, 65], BF16)
        lfn = sb("lfn", [LS, 8]); biasN = sb("biasN", [LS, 8])
        Kp = [sb("Kp%d" % i, [128, 512], BF16) for i in range(4)]
        Vp = [sb("Vp%d" % i, [128, 8, 65], BF16) for i in range(4)]
        Vg = [sb("Vg%d" % i, [128, 512], BF16) for i in range(4)]
        KT2 = [sb("KT2%d" % i, [128, 4, 128], BF16) for i in range(3)]
        Sf = [sb("Sf%d" % i, [128, 8, LS]) for i in range(3)]
        PTs = [sb("PTs%d" % i, [128, 8, LS], BF16) for i in range(3)]
        ys = sb("ys", [LS, 8, 64]); recs = sb("recs", [LS, 8, 1])
        for i in range(4):
            P.op("pool", lambda e, i=i: e.memset(Vp[i][:, :, 64:65], 1.0), writes=[B("Vp", i)])
        P.op("pool", lambda e: e.memset(Vn[:, :, 64:65], 1.0), writes=[B("Vn")])
        P.op("pool", lambda e: e.memset(Qblk[:], 0.0), writes=[B("Qblk")])
        paccs = [pacc[0], self.ps(es, "paccs1", [128, 4, 65])]
        ck_flat = self.io["cache_k"]
        cv_flat = self.io["cache_v"].rearrange("r (h d) -> r h d", h=8)
        for s in range(NS):
            a = T + s * LS
            pt_row = self.io["page_table"][s]
            P.dma("sp", PTc[:NPG, :], pt_row.rearrange("(j o) -> j o", o=1), writes=[B("PTc")])
            P.dma("sp", PTb[:, :], self.io["page_table"][s:s + 1, :].partition_broadcast(128), writes=[B("PTb")])
            P.op("dve", lambda e: e.tensor_copy(out=IDXf[:], in_=PTb[:]), reads=[B("PTb")], writes=[B("IDXf")])
            P.op("dve", lambda e: e.tensor_scalar(out=IDXf[:], in0=IDXf[:], scalar1=128.0, scalar2=iota_f[:, 0:1], op0=ALU.mult, op1=ALU.add),
                 reads=[B("IDXf"), B("f_iotaf")], writes=[B("IDXf")])
            P.op("dve", lambda e: e.tensor_copy(out=IDX[:], in_=IDXf[:]), reads=[B("IDXf")], writes=[B("IDX")])
            P.dma_fn("pool", lambda e: e.indirect_dma_start(out=LFp[:NPG].rearrange("j t h -> j (t h)"), out_offset=None, in_=self.io["cache_logf"],
                                                        in_offset=bass.IndirectOffsetOnAxis(ap=PTc[:NPG, 0:1], axis=0)),
                 reads=[B("PTc")], writes=[B("LFp")])
            P.dma("sp", lfn[:, :], self.io["new_logf"][a:a + LS, :], reads=[B("new_logf", tt_) for (tt_, _) in self.token_tiles(128) if tt_ >= T],
                  writes=[B("lfn")])
            for half in range(2):
                ptp, ptpb = bank()
                for hh in range(4):
                    P.op("pe", lambda e, hh=hh: e.transpose(out=ptp[:, hh * NPG:(hh + 1) * NPG], in_=LFp[:NPG, :, half * 4 + hh], identity=ident[:NPG, :NPG]),
                         reads=[B("LFp"), B("ident")], writes=[ptpb])
                P.op("dve", lambda e: e.tensor_copy(out=LFs2[:, :, half * 4:half * 4 + 4].rearrange("t j h -> t h j"),
                                                    in_=ptp[:, :4 * NPG].rearrange("t (h j) -> t h j", h=4)),
                     reads=[ptpb], writes=[B("LFs2")])
            cumsum2(LFs2, B("LFs2"), Cp, B("Cp"), NPG)
            pce, pceb = bank()
            P.op("pe", lambda e: e.matmul(pce[:, 0:8], lhsT=sel[:, :], rhs=Cp[:, NPG - 1, :], start=True, stop=True),
                 reads=[B("Cp"), B("f_sel")], writes=[pceb])
            P.op("act", lambda e: e.copy(out=cend[:], in_=pce[:, 0:8]), reads=[pceb], writes=[B("cend")])
            P.op("dve", lambda e: e.tensor_tensor(out=biasP[:], in0=cend[:, :].unsqueeze(1).to_broadcast([128, NPG, 8]), in1=Cp[:], op=ALU.subtract),
                 reads=[B("cend"), B("Cp")], writes=[B("biasP")])
            pcn, pcnb = bank()
            P.op("pe", lambda e: e.matmul(pcn[:LS, 0:8], lhsT=tri_i[:LS, :LS], rhs=lfn[:, :], start=True, stop=True),
                 reads=[B("lfn"), B("f_tri_i")], writes=[pcnb])
            P.op("act", lambda e: e.mul(out=biasN[:], in_=pcn[:LS, 0:8], mul=-1.0), reads=[pcnb], writes=[B("biasN")])
            qsrc = zT[0:512, a:a + LS].rearrange("(pr p) t -> p pr t", p=128)
            P.dma("pool", Qblk[0:64, :, 0:LS], qsrc[0:64], writes=[B("Qblk")])
            P.dma("pool", Qblk[64:128, :, LS:2 * LS], qsrc[64:128], writes=[B("Qblk")])
            P.dma("pool", KTn[:], zT[512:1024, a:a + LS].rearrange("(pr p) t -> p pr t", p=128), writes=[B("KTn")])
            P.dma("pool", Vn[:, :, 0:64], self.io["new_v"][a:a + LS, :].rearrange("t (h d) -> t h d", h=8),
                  reads=[B("new_v", tt_) for (tt_, _) in self.token_tiles(128) if tt_ >= T], writes=[B("Vn")])
            first = [True, True]

            def stG(j):
                kp, kpb = Kp[j % 4], B("Kp", j % 4)
                vp, vpb = Vp[j % 4], B("Vp", j % 4)
                vg, vgb = Vg[j % 4], B("Vg", j % 4)
                P.dma_fn("pool", lambda e: e.indirect_dma_start(out=kp[:, :], out_offset=None, in_=ck_flat,
                                                                in_offset=bass.IndirectOffsetOnAxis(ap=IDX[:, j:j + 1], axis=0)),
                         reads=[B("IDX")], writes=[kpb])
                P.dma_fn("pool", lambda e: e.indirect_dma_start(out=vg[:, :], out_offset=None, in_=self.io["cache_v"],
                                                                in_offset=bass.IndirectOffsetOnAxis(ap=IDX[:, j:j + 1], axis=0)),
                         reads=[B("IDX")], writes=[vgb])
                P.op("dve", lambda e: e.tensor_copy(out=vp[:, :, 0:64], in_=vg[:, :].rearrange("k (h d) -> k h d", h=8)), reads=[vgb], writes=[vpb])

            def stT(j):
                kp, kpb = Kp[j % 4], B("Kp", j % 4)
                k2, k2b = KT2[j % 3], B("KT2", j % 3)
                pkt, pktb = bankb[0], B("fbankb", 0)
                for pr in range(4):
                    P.op("pe", lambda e, pr=pr: e.transpose(out=pkt[:, pr * 128:(pr + 1) * 128], in_=kp[:, pr * 128:(pr + 1) * 128], identity=identb[:, :]),
                         reads=[kpb, B("identb")], writes=[pktb])
                P.op("dve", lambda e: e.tensor_copy(out=k2[:].rearrange("p a b -> p (a b)"), in_=pkt[:, 0:512]), reads=[pktb], writes=[k2b])

            def stQ(j):
                k2, k2b = KT2[j % 3], B("KT2", j % 3)
                pss, pssb = bank()
                for pr in range(4):
                    P.op("pe", lambda e, pr=pr: e.matmul(pss[:, pr * 2 * LS:(pr + 1) * 2 * LS], lhsT=k2[:, pr, :], rhs=Qblk[:, pr, :], start=True, stop=True),
                         reads=[k2b, B("Qblk")], writes=[pssb])
                sf, sfb = Sf[j % 3], B("Sf", j % 3)
                pts, ptsb = PTs[j % 3], B("PTs", j % 3)
                P.op("dve", lambda e: e.scalar_tensor_tensor(out=sf[:], in0=pss[:, :8 * LS].rearrange("k (h q) -> k h q", h=8), scalar=0.125,
                                                             in1=biasP[:, j, :].unsqueeze(2).to_broadcast([128, 8, LS]), op0=ALU.mult, op1=ALU.add),
                     reads=[pssb, B("biasP")], writes=[sfb])
                P.op("act", lambda e: e.activation(out=pts[:], in_=sf[:], func=AF.Exp), reads=[sfb], writes=[ptsb])

            def stV(j):
                vp, vpb = Vp[j % 4], B("Vp", j % 4)
                pts, ptsb = PTs[j % 3], B("PTs", j % 3)
                for h in range(8):
                    bkk = h // 4
                    P.op("pe", lambda e, h=h, bkk=bkk: e.matmul(paccs[bkk][:LS, h % 4, :], lhsT=pts[:, h, :], rhs=vp[:, h, :], start=first[bkk], stop=False),
                         reads=[ptsb, vpb], writes=[(B("pacc", 0) if bkk == 0 else B("paccs", 1))])
                    first[bkk] = False
            for j in range(min(3, NPG)):
                stG(j)
            for j in range(min(2, NPG)):
                stT(j)
            stQ(0)
            for j in range(NPG):
                if j + 3 < NPG:
                    stG(j + 3)
                if j + 2 < NPG:
                    stT(j + 2)
                if j + 1 < NPG:
                    stQ(j + 1)
                stV(j)
            psn, psnb = bank()
            for pr in range(4):
                P.op("pe", lambda e, pr=pr: e.matmul(psn[:LS, pr * 2 * LS:(pr + 1) * 2 * LS], lhsT=KTn[:, pr, :], rhs=Qblk[:, pr, :], start=True, stop=True),
                     reads=[B("KTn"), B("Qblk")], writes=[psnb])
            sf, sfb = Sf[0], B("Sf", 0)
            pts, ptsb = PTs[0], B("PTs", 0)
            P.op("dve", lambda e: e.scalar_tensor_tensor(out=sf[:LS], in0=psn[:LS, :8 * LS].rearrange("k (h q) -> k h q", h=8), scalar=0.125,
                                                         in1=biasN[:, :].unsqueeze(2).to_broadcast([LS, 8, LS]), op0=ALU.mult, op1=ALU.add),
                 reads=[psnb, B("biasN")], writes=[sfb])
            P.op("dve", lambda e: e.tensor_tensor(out=sf[:LS], in0=sf[:LS], in1=negf[:LS, :LS].unsqueeze(1).to_broadcast([LS, 8, LS]), op=ALU.add),
                 reads=[sfb, B("f_negf")], writes=[sfb])
            P.op("act", lambda e: e.activation(out=pts[:LS], in_=sf[:LS], func=AF.Exp), reads=[sfb], writes=[ptsb])
            for h in range(8):
                bkk = h // 4
                P.op("pe", lambda e, h=h, bkk=bkk: e.matmul(paccs[bkk][:LS, h % 4, :], lhsT=pts[:LS, h, :], rhs=Vn[:, h, :], start=False, stop=True),
                     reads=[ptsb, B("Vn")], writes=[(B("pacc", 0) if bkk == 0 else B("paccs", 1))])
            for bkk in range(2):
                hs = slice(bkk * 4, bkk * 4 + 4)
                P.op("dve", lambda e, bkk=bkk, hs=hs: e.reciprocal(out=recs[:, hs, :], in_=paccs[bkk][:LS, :, 64:65]), reads=[(B("pacc", 0) if bkk == 0 else B("paccs", 1))], writes=[B("recs")])
                P.op("dve", lambda e, bkk=bkk, hs=hs: e.tensor_tensor(out=ys[:, hs, :], in0=paccs[bkk][:LS, :, 0:64],
                                                                      in1=recs[:, hs, :].to_broadcast([LS, 4, 64]), op=ALU.mult),
                     reads=[(B("pacc", 0) if bkk == 0 else B("paccs", 1)), B("recs")], writes=[B("ys")])
            P.dma("sp", self.io["mixC1"][a:a + LS, :], ys[:].rearrange("t h d -> t (h d)"), reads=[B("ys")], writes=[B("mixC1s", s)])
        P.barrier()


Model.phase_fox = _fox_phase


def _ssd_phase(self):
    P, T, LS, NS = self.P, self.T, self.LS, self.NS
    zT = self.io["zT1"]
    X0 = FOX_IN + 512
    NC = dict(allow_slow_non_contiguous=True)
    B = P.buf
    CM = 128
    with ExitStack() as es:
        sb = lambda name, shape, dt=F32: self.sb(es, name, shape, dt)

        def ld(name, shape, src, **kw):
            t = sb(name, shape)
            P.dma("sp", t[:], src, writes=[B(name)], **kw)
            return t
        ident = self.ident
        tri_i = ld("s_tri_i", [128, 128], self.io["c_triu_incl"])
        ones = ld("s_ones", [128, 128], self.io["c_ones"])
        cw = sb("s_cw", [128, 8, 4])
        for k in range(4):
            P.dma("sp", cw[:, :, k], self.io["ssm_conv_w"][k].rearrange("(c p) -> p c", p=128), writes=[B("s_cw")], **NC)
        cb = ld("s_cb", [128, 8], self.io["ssm_conv_b"][0].rearrange("(c p) -> p c", p=128), **NC)
        dtb = ld("s_dtb", [128, 8], self.io["ssm_dt_bias"].partition_broadcast(128))
        Abc = ld("s_A", [128, 8], self.io["ssm_a_log"].partition_broadcast(128))
        Dbc = ld("s_D", [128, 8], self.io["ssm_d"].partition_broadcast(128))
        nw = ld("s_nw", [128, 512], self.io["ssm_norm_w"].partition_broadcast(128))
        P.op("act", lambda e: e.activation(out=Abc[:], in_=Abc[:], func=AF.Exp), reads=[B("s_A")], writes=[B("s_A")])
        P.op("dve", lambda e: e.tensor_scalar(out=Abc[:], in0=Abc[:], scalar1=-1.0, scalar2=None, op0=ALU.mult), reads=[B("s_A")], writes=[B("s_A")])
        def mkset(si):
            S = {}
            sbs = lambda name, shape, dt=F32: sb(name + "_s%d" % si, shape, dt)
            S["U"] = sbs("s_U", [128, 8, CM + 3])
            S["acc"] = sbs("s_acc", [128, 8, CM])
            S["tmp"] = sbs("s_tmp", [128, 8, CM])
            S["xbc"] = sbs("s_xbc", [128, 8, CM])
            S["xtm"] = sbs("s_xtm", [128, 512])
            S["Btm"] = sbs("s_Btm", [128, 2, 128])
            S["dt"] = sbs("s_dt", [128, 8])
            S["av"] = sbs("s_a", [128, 8])
            S["abc"] = sbs("s_abc", [128, 8, CM])
            S["acs"] = sbs("s_acs", [128, 8])
            S["Lm"] = sbs("s_Lm", [128, 8, CM])
            S["CBm"] = sbs("s_CBm", [128, 2, CM])
            S["Gm"] = sbs("s_Gm", [128, 8, CM])
            S["eac"] = sbs("s_eac", [128, 8])
            S["wgt"] = sbs("s_wgt", [128, 8])
            S["eend"] = sbs("s_eend", [128, 8])
            S["yv"] = sbs("s_y", [128, 8, 64])
            S["t2"] = sbs("s_t2", [128, 8, 64])
            S["xw"] = sbs("s_xw", [128, 8, 64])
            S["zg"] = sbs("s_zg", [128, 512])
            S["junk"] = sbs("s_junk", [128, 512])
            S["ss"] = sbs("s_ss", [128, 4])
            return S
        KS = 2
        sets = [mkset(i) for i in range(KS)]
        PERSET = {'s_a', 's_zg', 's_Gm', 's_abc', 's_xbc', 's_t2', 's_acs', 's_tmp', 's_junk', 's_y', 's_wgt', 's_xtm', 's_acc', 's_eend', 's_U', 's_ss', 's_xw', 's_dt', 's_Btm', 's_CBm', 's_Lm', 's_eac'}
        B0 = B

        def mkB(si):
            def Bs(n, *a):
                return B0(n + "@%d" % si, *a) if n in PERSET else B0(n, *a)
            return Bs
        ST = sb("s_ST", [128, 8, 64]); Sn = sb("s_Sn", [128, 4, 128])
        banks = [self.ps(es, "sbk%d" % i, [128, 512]) for i in range(8)]
        bstate = [0]

        bcnt = [0, 0]

        def mkbank(si):
            def bank_():
                i = si * 4 + (bcnt[si] % 4)
                bcnt[si] += 1
                return banks[i], B0("sbank", i)
            return bank_

        done = [0]

        def chunk(gi, q0, L, s, c0, si):
            S = sets[si]
            B = mkB(si)
            bank = mkbank(si)
            U = S["U"]
            acc = S["acc"]
            tmp = S["tmp"]
            xbc = S["xbc"]
            xtm = S["xtm"]
            Btm = S["Btm"]
            dt = S["dt"]
            av = S["av"]
            abc = S["abc"]
            acs = S["acs"]
            Lm = S["Lm"]
            CBm = S["CBm"]
            Gm = S["Gm"]
            eac = S["eac"]
            wgt = S["wgt"]
            eend = S["eend"]
            yv = S["yv"]
            t2 = S["t2"]
            xw = S["xw"]
            zg = S["zg"]
            junk = S["junk"]
            ss = S["ss"]
            oi = 0 if s is None else 1 + s
            C = min(CM, L - c0)
            a = q0 + c0
            usrc = zT[X0:X0 + 1024, :].rearrange("(c p) t -> p c t", p=128)
            if c0 == 0:
                P.dma("sp", U[:, :, 3:3 + C], usrc[:, :, a:a + C], writes=[B("s_U")])
                if s is None:
                    P.op("pool", lambda e: e.memset(U[:, :, 0:3], 0.0), writes=[B("s_U")])
                else:
                    for k in range(3):
                        P.dma("sp", U[:, :, k], self.io["state_ssm_conv"][s, k].rearrange("(c p) -> p c", p=128), writes=[B("s_U")], **NC)
            else:
                P.dma("sp", U[:, :, 0:3 + C], usrc[:, :, a - 3:a + C], writes=[B("s_U")])
            yield
            wb_ = lambda k: cw[:, :, k:k + 1].to_broadcast([128, 8, C])
            P.op("dve", lambda e: e.tensor_tensor(out=acc[:, :, :C], in0=U[:, :, 0:C], in1=wb_(0), op=ALU.mult), reads=[B("s_U"), B("s_cw")], writes=[B("s_acc")])
            for k in range(1, 4):
                P.op("pool", lambda e, k=k: e.tensor_tensor(out=tmp[:, :, :C], in0=U[:, :, k:k + C], in1=wb_(k), op=ALU.mult),
                     reads=[B("s_U"), B("s_cw")], writes=[B("s_tmp")])
                P.op("dve", lambda e: e.tensor_tensor(out=acc[:, :, :C], in0=acc[:, :, :C], in1=tmp[:, :, :C], op=ALU.add),
                     reads=[B("s_acc"), B("s_tmp")], writes=[B("s_acc")])
            P.op("dve", lambda e: e.tensor_tensor(out=acc[:, :, :C], in0=acc[:, :, :C], in1=cb[:, :].unsqueeze(2).to_broadcast([128, 8, C]), op=ALU.add),
                 reads=[B("s_acc"), B("s_cb")], writes=[B("s_acc")])
            P.op("act", lambda e: e.activation(out=xbc[:, :, :C], in_=acc[:, :, :C], func=AF.Silu), reads=[B("s_acc")], writes=[B("s_xbc")])
            yield
            px, pxb = bank()
            for c in range(4):
                P.op("pe", lambda e, c=c: e.transpose(out=px[:C, c * 128:(c + 1) * 128], in_=xbc[:, c, :C], identity=ident[:, :]),
                     reads=[B("s_xbc"), B("ident")], writes=[pxb])
            P.op("act", lambda e: e.copy(out=xtm[:C, :], in_=px[:C, :]), reads=[pxb], writes=[B("s_xtm")])
            pB, pBb = bank()
            for g in range(2):
                P.op("pe", lambda e, g=g: e.transpose(out=pB[:C, g * 128:(g + 1) * 128], in_=xbc[:, 4 + g, :C], identity=ident[:, :]),
                     reads=[B("s_xbc"), B("ident")], writes=[pBb])
            P.op("act", lambda e: e.copy(out=Btm[:C].rearrange("s g n -> s (g n)"), in_=pB[:C, 0:256]), reads=[pBb], writes=[B("s_Btm")])
            yield
            P.dma("sp", dt[:C, :], self.io["dt_tm"][a:a + C, :], writes=[B("s_dt")])
            P.op("dve", lambda e: e.tensor_tensor(out=dt[:C, :], in0=dt[:C, :], in1=dtb[:C, :], op=ALU.add), reads=[B("s_dt"), B("s_dtb")], writes=[B("s_dt")])
            P.op("act", lambda e: e.activation(out=dt[:C, :], in_=dt[:C, :], func=AF.Exp), reads=[B("s_dt")], writes=[B("s_dt")])
            P.op("act", lambda e: e.activation(out=dt[:C, :], in_=dt[:C, :], func=AF.Ln, bias=ones[:C, 0:1], scale=1.0),
                 reads=[B("s_dt"), B("s_ones")], writes=[B("s_dt")])
            P.op("dve", lambda e: e.tensor_tensor(out=av[:C, :], in0=dt[:C, :], in1=Abc[:C, :], op=ALU.mult), reads=[B("s_dt"), B("s_A")], writes=[B("s_a")])
            P.op("pool", lambda e: e.tensor_copy(out=abc[:C, :, :C], in_=av[:C, :].unsqueeze(2).to_broadcast([C, 8, C])), reads=[B("s_a")], writes=[B("s_abc")])
            pac, pacb = bank()
            P.op("pe", lambda e: e.matmul(pac[:C, 0:8], lhsT=tri_i[:C, :C], rhs=av[:C, :], start=True, stop=True), reads=[B("s_a"), B("s_tri_i")], writes=[pacb])
            P.op("pe", lambda e: e.matmul(pac[:, 8:16], lhsT=ones[:C, :], rhs=av[:C, :], start=True, stop=True), reads=[B("s_a"), B("s_ones")], writes=[pacb])
            P.op("act", lambda e: e.copy(out=acs[:C, :], in_=pac[:C, 0:8]), reads=[pacb], writes=[B("s_acs")])
            P.op("act", lambda e: e.activation(out=eac[:C, :], in_=pac[:C, 0:8], func=AF.Exp), reads=[pacb], writes=[B("s_eac")])
            P.op("act", lambda e: e.activation(out=eend[:, :], in_=pac[:, 8:16], func=AF.Exp), reads=[pacb], writes=[B("s_eend")])
            P.op("dve", lambda e: e.tensor_tensor(out=wgt[:C, :], in0=pac[:C, 8:16], in1=acs[:C, :], op=ALU.subtract), reads=[pacb, B("s_acs")], writes=[B("s_wgt")])
            P.op("act", lambda e: e.activation(out=wgt[:C, :], in_=wgt[:C, :], func=AF.Exp), reads=[B("s_wgt")], writes=[B("s_wgt")])
            P.op("dve", lambda e: e.tensor_tensor(out=wgt[:C, :], in0=wgt[:C, :], in1=dt[:C, :], op=ALU.mult), reads=[B("s_wgt"), B("s_dt")], writes=[B("s_wgt")])
            yield
            for half in range(2):
                pb_, pbb_ = bank()
                for hh in range(4):
                    h = half * 4 + hh
                    P.op("pe", lambda e, h=h, hh=hh: e.matmul(pb_[:C, hh * C:(hh + 1) * C], lhsT=abc[:C, h, :C], rhs=tri_i[:C, :C], start=True, stop=True),
                         reads=[B("s_abc"), B("s_tri_i")], writes=[pbb_])
                hs = slice(half * 4, half * 4 + 4)
                P.op("dve", lambda e: e.tensor_tensor(out=Lm[:C, hs, :C], in0=pb_[:C, :4 * C].rearrange("s (h t) -> s h t", h=4),
                                                      in1=acs[:C, hs].unsqueeze(2).to_broadcast([C, 4, C]), op=ALU.subtract),
                     reads=[pbb_, B("s_acs")], writes=[B("s_Lm")])
            P.op("pool", lambda e: e.tensor_scalar(out=Lm[:C, :, :C], in0=Lm[:C, :, :C], scalar1=0.0, scalar2=None, op0=ALU.min), reads=[B("s_Lm")], writes=[B("s_Lm")])
            P.op("act", lambda e: e.activation(out=Lm[:C, :, :C], in_=Lm[:C, :, :C], func=AF.Exp), reads=[B("s_Lm")], writes=[B("s_Lm")])
            pcb, pcbb = bank()
            for g in range(2):
                P.op("pe", lambda e, g=g: e.matmul(pcb[:C, g * C:(g + 1) * C], lhsT=xbc[:, 4 + g, :C], rhs=xbc[:, 6 + g, :C], start=True, stop=True),
                     reads=[B("s_xbc")], writes=[pcbb])
            P.op("dve", lambda e: e.tensor_tensor(out=CBm[:C, :, :C], in0=pcb[:C, :2 * C].rearrange("s (g t) -> s g t", g=2),
                                                  in1=tri_i[:C, :C].unsqueeze(1).to_broadcast([C, 2, C]), op=ALU.mult),
                 reads=[pcbb, B("s_tri_i")], writes=[B("s_CBm")])
            for g in range(2):
                hs = slice(g * 4, g * 4 + 4)
                P.op("dve" if g == 0 else "pool", lambda e, g=g, hs=hs: e.tensor_tensor(out=Gm[:C, hs, :C], in0=Lm[:C, hs, :C],
                                                                                     in1=CBm[:C, g:g + 1, :C].to_broadcast([C, 4, C]), op=ALU.mult),
                     reads=[B("s_Lm"), B("s_CBm")], writes=[B("s_Gm")])
            P.op("dve", lambda e: e.tensor_tensor(out=Gm[:C, :, :C], in0=Gm[:C, :, :C], in1=dt[:C, :].unsqueeze(2).to_broadcast([C, 8, C]), op=ALU.mult),
                 reads=[B("s_Gm"), B("s_dt")], writes=[B("s_Gm")])
            hv = lambda t, h: t[:C, h * 64:(h + 1) * 64]
            P.op("pool", lambda e: e.tensor_tensor(out=xw[:C], in0=xtm[:C, :].rearrange("t (h p) -> t h p", h=8),
                                                   in1=wgt[:C, :].unsqueeze(2).to_broadcast([C, 8, 64]), op=ALU.mult),
                 reads=[B("s_xtm"), B("s_wgt")], writes=[B("s_xw")])
            P.op("pool", lambda e: e.tensor_tensor(out=t2[:C], in0=xtm[:C, :].rearrange("t (h p) -> t h p", h=8),
                                                   in1=Dbc[:C, :].unsqueeze(2).to_broadcast([C, 8, 64]), op=ALU.mult),
                 reads=[B("s_xtm"), B("s_D")], writes=[B("s_t2")])
            P.dma("sp", zg[:C, :], self.io["zg_tm"][a:a + C, :], writes=[B("s_zg")])
            P.op("act", lambda e: e.activation(out=zg[:C, :], in_=zg[:C, :], func=AF.Silu), reads=[B("s_zg")], writes=[B("s_zg")])
            py, pyb = bank()
            for h in range(8):
                P.op("pe", lambda e, h=h: e.matmul(hv(py, h), lhsT=Gm[:C, h, :C], rhs=hv(xtm, h), start=True, stop=True),
                     reads=[B("s_Gm"), B("s_xtm")], writes=[pyb])
            yield
            while done[0] < gi:
                yield
            if c0 == 0:
                if s is None:
                    P.op("dve", lambda e: e.memset(ST[:], 0.0), writes=[B("s_ST")])
                else:
                    P.dma("sp", Sn[:], self.io["state_ssm"][s].rearrange("h p n -> (h p) n").rearrange("(c q) n -> q c n", q=128), writes=[B("s_Sn")])
                    pb, pbb = bank()
                    for c in range(4):
                        P.op("pe", lambda e, c=c: e.transpose(out=pb[:, c * 128:(c + 1) * 128], in_=Sn[:, c, :], identity=ident[:, :]),
                             reads=[B("s_Sn"), B("ident")], writes=[pbb])
                    P.op("dve", lambda e: e.tensor_copy(out=ST[:].rearrange("n h p -> n (h p)"), in_=pb[:, :]), reads=[pbb], writes=[B("s_ST")])
            yield
            pyi, pyib = bank()
            for h in range(8):
                P.op("pe", lambda e, h=h: e.matmul(hv(pyi, h), lhsT=xbc[:, 6 + h // 4, :C], rhs=ST[:, h, :], start=True, stop=True),
                     reads=[B("s_xbc"), B("s_ST")], writes=[pyib])
            v3 = lambda ap: ap.rearrange("t (h p) -> t h p", h=8)
            b8 = lambda t: t[:C, :].unsqueeze(2).to_broadcast([C, 8, 64])
            P.op("dve", lambda e: e.tensor_tensor(out=yv[:C], in0=v3(pyi[:C, :]), in1=b8(eac), op=ALU.mult), reads=[pyib, B("s_eac")], writes=[B("s_y")])
            P.op("dve", lambda e: e.tensor_tensor(out=yv[:C], in0=yv[:C], in1=v3(py[:C, :]), op=ALU.add), reads=[B("s_y"), pyb], writes=[B("s_y")])
            P.op("dve", lambda e: e.tensor_tensor(out=yv[:C], in0=yv[:C], in1=t2[:C], op=ALU.add), reads=[B("s_y"), B("s_t2")], writes=[B("s_y")])
            yield
            pS, pSb = bank()
            for h in range(8):
                P.op("pe", lambda e, h=h: e.matmul(pS[:, h * 64:(h + 1) * 64], lhsT=Btm[:C, h // 4, :], rhs=xw[:C, h, :], start=True, stop=True),
                     reads=[B("s_Btm"), B("s_xw")], writes=[pSb])
            P.op("dve", lambda e: e.tensor_tensor(out=ST[:], in0=ST[:], in1=eend[:, :].unsqueeze(2).to_broadcast([128, 8, 64]), op=ALU.mult),
                 reads=[B("s_ST"), B("s_eend")], writes=[B("s_ST")])
            P.op("dve", lambda e: e.tensor_tensor(out=ST[:], in0=ST[:], in1=pS[:, :].rearrange("n (h p) -> n h p", h=8), op=ALU.add),
                 reads=[B("s_ST"), pSb], writes=[B("s_ST")])
            if c0 + C == L:
                pb, pbb = bank()
                STf = ST[:].rearrange("n h p -> n (h p)")
                for c in range(4):
                    P.op("pe", lambda e, c=c: e.transpose(out=pb[:, c * 128:(c + 1) * 128], in_=STf[:, c * 128:(c + 1) * 128], identity=ident[:, :]),
                         reads=[B("s_ST"), B("ident")], writes=[pbb])
                P.op("act", lambda e: e.copy(out=Sn[:].rearrange("q c n -> q (c n)"), in_=pb[:, :]), reads=[pbb], writes=[B("s_Sn")])
                P.dma("pool", self.io["new_ssm"][oi].rearrange("h p n -> (h p) n").rearrange("(c q) n -> q c n", q=128), Sn[:], reads=[B("s_Sn")],
                      writes=[B("new_ssm", oi)])
            done[0] += 1
            yield
            yf_ = yv[:C].rearrange("t h p -> t (h p)")
            P.op("dve", lambda e: e.tensor_tensor(out=yf_, in0=yf_, in1=zg[:C, :], op=ALU.mult), reads=[B("s_y"), B("s_zg")], writes=[B("s_y")])
            P.op("act", lambda e: e.activation(out=junk[:C, :], in_=yf_, func=AF.Square, accum_out=ss[:C, 0:1]), reads=[B("s_y")], writes=[B("s_junk"), B("s_ss")])
            self.rstd(ss, B("s_ss"), C, inv=1.0 / 512)
            P.op("dve", lambda e: e.scalar_tensor_tensor(out=yf_, in0=yf_, scalar=ss[:C, 2:3], in1=nw[:C, :], op0=ALU.mult, op1=ALU.mult),
                 reads=[B("s_y"), B("s_ss"), B("s_nw")], writes=[B("s_y")])
            P.dma("pool", self.io["mixD1"][a:a + C, :], yf_, reads=[B("s_y")], writes=[B("mixD1", a)])
            if c0 + C == L:
                for k in range(3):
                    P.dma("pool", self.io["new_conv"][oi, k].rearrange("(c p) -> p c", p=128), U[:, :, C + k], reads=[B("s_U")],
                          writes=[B("new_conv", oi, k)], **NC)

        seqs = [(0, T, None)] + [(T + s * LS, LS, s) for s in range(NS)]
        work = [(q0, L_, s, c0) for (q0, L_, s) in seqs for c0 in range(0, L_, CM)]
        active = []
        nxt_i = 0
        while active or nxt_i < len(work):
            while len(active) < KS and nxt_i < len(work):
                q0, L_, s, c0 = work[nxt_i]
                active.append(chunk(nxt_i, q0, L_, s, c0, nxt_i % KS))
                nxt_i += 1
            for g_ in list(active):
                try:
                    next(g_)
                except StopIteration:
                    active.remove(g_)
        P.barrier()


Model.phase_ssd = _ssd_phase
```
